# Optimizing a Trainium2 kernel written in Bass

```python
import math
import jax, jax.numpy as jnp
from jax import lax
import numpy as np

D_MODEL = 1024
BATCH = 8
SEQ = 4096
DEPTH = 1

SSD_HEADS = 16
SSD_HEAD_DIM = 64
SSD_GROUPS = 2
SSD_STATE = 128
SSD_CONV = 4
SSD_CHUNK = 128
SSD_INNER = SSD_HEADS * SSD_HEAD_DIM
SSD_XBC = SSD_INNER + 2 * SSD_GROUPS * SSD_STATE
NSA_HEADS = 8
NSA_KV_GROUPS = 2
NSA_HEAD_DIM = 64
NSA_CMP_LEN = 32
NSA_CMP_STRIDE = 16
NSA_CMP_HIDDEN = 256
NSA_SLC_BLOCK = 64
NSA_TOP_N = 16
NSA_WINDOW = 512
NSA_QBLOCK = 64
NSA_INNER = NSA_HEADS * NSA_HEAD_DIM
NSA_KV = 3 * 2 * NSA_KV_GROUPS * NSA_HEAD_DIM
NSA_GATES = 3 * NSA_HEADS
MEM_TOKENS = 256
MEM_HEADS = 4
MEM_HEAD_DIM = 128
MEM_INNER = MEM_HEADS * MEM_HEAD_DIM
N_BRANCH = 3
FFN_HIDDEN = 2816
FFN_CONV = 3
NORM_EPS = 1e-6
NEG = -1e30
BIG = 1e9

IN_SIZES = (SSD_INNER, SSD_XBC, SSD_HEADS, NSA_INNER, NSA_KV, NSA_GATES, MEM_INNER, N_BRANCH * D_MODEL)
IN_WIDTH = sum(IN_SIZES)
IN_OFFSETS = tuple(int(v) for v in np.cumsum(IN_SIZES)[:-1])

kernel_name = "hybrid_ssd_nsa_mem_convffn"


def rmsnorm(x, g):
    xf = x.astype(jnp.float32)
    y = xf * lax.rsqrt(jnp.mean(xf * xf, axis=-1, keepdims=True) + NORM_EPS)
    return (y * g.astype(jnp.float32)).astype(x.dtype)


def causal_dwconv(x, w, b):
    k, c = w.shape
    y = lax.conv_general_dilated(x, w[:, None, :].astype(x.dtype), window_strides=(1,),
                                 padding=[(k - 1, 0)], dimension_numbers=('NWC', 'WIO', 'NWC'),
                                 feature_group_count=c)
    return y + b.astype(x.dtype)


def alibi_slopes(n):
    return jnp.exp2(-8.0 * jnp.arange(1, n + 1, dtype=jnp.float32) / n)


def ssd_mixer(z, xbc_raw, dt_raw, conv_w, conv_b, dt_bias, a_log, d_skip, norm_g):
    bsz, s, _ = z.shape
    f32 = jnp.float32
    g, hg, p, n, L = SSD_GROUPS, SSD_HEADS // SSD_GROUPS, SSD_HEAD_DIM, SSD_STATE, SSD_CHUNK
    nc = s // L
    xbc = jax.nn.silu(causal_dwconv(xbc_raw, conv_w, conv_b))
    xs, bm, cm = jnp.split(xbc, [SSD_INNER, SSD_INNER + g * n], axis=-1)
    xc = xs.reshape(bsz, nc, L, g, hg, p).astype(f32)
    bc = bm.reshape(bsz, nc, L, g, n).astype(f32)
    cc = cm.reshape(bsz, nc, L, g, n).astype(f32)
    dt = jax.nn.softplus(dt_raw.astype(f32) + dt_bias.astype(f32)).reshape(bsz, nc, L, g, hg)
    a = -jnp.exp(a_log.astype(f32)).reshape(g, hg)
    a_cs = jnp.cumsum(jnp.transpose(dt * a, (0, 3, 4, 1, 2)), axis=-1)
    tril = jnp.tril(jnp.ones((L, L), dtype=bool))
    seg = a_cs[..., :, None] - a_cs[..., None, :]
    lmat = jnp.where(tril, jnp.exp(jnp.where(tril, seg, 0.0)), 0.0)
    xdt = xc * dt[..., None]
    cb = jnp.einsum('bclgn,bcsgn->bgcls', cc, bc)
    y_diag = jnp.einsum('bghcls,bcsghp->bclghp', cb[:, :, None] * lmat, xdt)
    decay_states = jnp.exp(a_cs[..., -1:] - a_cs)
    states = jnp.einsum('bclgn,bghcl,bclghp->bcghpn', bc, decay_states, xdt)
    chunk_decay = jnp.exp(a_cs[..., -1])

    def step(h, inp):
        s_c, d_c = inp
        return d_c[..., None, None] * h + s_c, h

    h0 = jnp.zeros((bsz, g, hg, p, n), f32)
    _, prev = lax.scan(step, h0, (jnp.moveaxis(states, 1, 0), jnp.moveaxis(chunk_decay, -1, 0)))
    prev = jnp.moveaxis(prev, 0, 1)
    y_off = jnp.einsum('bclgn,bcghpn,bghcl->bclghp', cc, prev, jnp.exp(a_cs))
    y = y_diag + y_off + xc * d_skip.astype(f32).reshape(g, hg)[..., None]
    y = y.reshape(bsz, s, SSD_INNER).astype(z.dtype)
    return rmsnorm(y * jax.nn.silu(z), norm_g)


def nsa_mixer(q, kv, gates, q_norm, k_norm, cmp_pe, cmp_w1, cmp_w2):
    bsz, s, _ = q.shape
    dtype = q.dtype
    f32 = jnp.float32
    G, hg, dk = NSA_KV_GROUPS, NSA_HEADS // NSA_KV_GROUPS, NSA_HEAD_DIM
    QB, SLC, W = NSA_QBLOCK, NSA_SLC_BLOCK, NSA_WINDOW
    scale = dk ** -0.5
    q = rmsnorm(q.reshape(bsz, s, G, hg, dk), q_norm)
    kv = kv.reshape(bsz, s, 3, 2, G, dk)

    ratio = NSA_CMP_LEN // NSA_CMP_STRIDE
    n_chunk = s // NSA_CMP_STRIDE
    n_cmp = n_chunk - ratio + 1
    chunks = jnp.moveaxis(kv[:, :, 0], 2, 0).reshape(2, bsz, n_chunk, NSA_CMP_STRIDE, G, dk)
    blocks = jnp.concatenate([chunks[:, :, r:r + n_cmp] for r in range(ratio)], axis=3)
    blocks = blocks + cmp_pe[:, None, None, :, None, :].astype(dtype)
    flat = jnp.swapaxes(blocks, 3, 4).reshape(2, bsz, n_cmp, G, NSA_CMP_LEN * dk)
    hid = jax.nn.silu(jnp.einsum('kbngf,kfh->kbngh', flat, cmp_w1))
    cmp = jnp.einsum('kbngh,khd->kbngd', hid, cmp_w2)
    k_cmp = rmsnorm(cmp[0], k_norm[0])
    v_cmp = cmp[1]
    cmp_start = jnp.arange(n_cmp) * NSA_CMP_STRIDE
    cmp_end = cmp_start + NSA_CMP_LEN - 1

    n_slc = s // SLC
    n_top = min(NSA_TOP_N, n_slc)
    k_slc = rmsnorm(kv[:, :, 1, 0], k_norm[1]).reshape(bsz, n_slc, SLC, G, dk).transpose(0, 3, 1, 2, 4)
    v_slc = kv[:, :, 1, 1].reshape(bsz, n_slc, SLC, G, dk).transpose(0, 3, 1, 2, 4)
    slc_start = jnp.arange(n_slc) * SLC
    overlap = jnp.clip(jnp.minimum(cmp_end[:, None], slc_start[None, :] + SLC - 1)
                       - jnp.maximum(cmp_start[:, None], slc_start[None, :]) + 1, 0)
    overlap = overlap.astype(f32) / NSA_CMP_LEN

    pad = jnp.zeros((bsz, W, G, dk), dtype)
    k_win = jnp.concatenate([pad, rmsnorm(kv[:, :, 2, 0], k_norm[2])], axis=1)
    v_win = jnp.concatenate([pad, kv[:, :, 2, 1]], axis=1)

    slopes = alibi_slopes(NSA_HEADS).reshape(G, hg)
    bidx = jnp.arange(bsz)[:, None, None, None]
    gidx = jnp.arange(G)[None, :, None, None]

    def block_fn(args):
        qb, t0 = args
        t = t0 + jnp.arange(QB)
        sc = jnp.einsum('bqghd,bngd->bghqn', qb, k_cmp).astype(f32) * scale
        dist = t[:, None] - cmp_end[None, :]
        sc = sc - slopes[:, :, None, None] * dist.astype(f32)
        valid = dist >= 0
        p_cmp = jnp.where(valid, jax.nn.softmax(jnp.where(valid, sc, NEG), axis=-1), 0.0)
        o_cmp = jnp.einsum('bghqn,bngd->bqghd', p_cmp.astype(dtype), v_cmp)
        imp = jnp.einsum('bghqn,nj->bgqj', p_cmp, overlap)
        jt = (t // SLC)[:, None]
        blk = jnp.arange(n_slc)[None, :]
        forced = (blk == 0) | (blk == jt) | (blk == jt - 1)
        imp = jnp.where(forced, BIG, imp)
        imp = jnp.where(blk > jt, NEG, imp)
        _, idx = lax.top_k(imp, n_top)
        k_sel = k_slc[bidx, gidx, idx]
        v_sel = v_slc[bidx, gidx, idx]
        ss = jnp.einsum('bqghd,bgqkld->bghqkl', qb, k_sel).astype(f32) * scale
        dsel = (t[:, None, None] - (idx[..., None] * SLC + jnp.arange(SLC)))[:, :, None]
        ss = ss - slopes[None, :, :, None, None, None] * dsel.astype(f32)
        ss = jnp.where(dsel >= 0, ss, NEG).reshape(bsz, G, hg, QB, n_top * SLC)
        p_sel = jax.nn.softmax(ss, axis=-1).reshape(bsz, G, hg, QB, n_top, SLC)
        o_slc = jnp.einsum('bghqkl,bgqkld->bqghd', p_sel.astype(dtype), v_sel)
        kw = lax.dynamic_slice_in_dim(k_win, t0, W + QB, axis=1)
        vw = lax.dynamic_slice_in_dim(v_win, t0, W + QB, axis=1)
        sw = jnp.einsum('bqghd,bsgd->bghqs', qb, kw).astype(f32) * scale
        kpos = t0 - W + jnp.arange(W + QB)
        dw = t[:, None] - kpos[None, :]
        vmask = (dw >= 0) & (dw < W) & (kpos[None, :] >= 0)
        sw = sw - slopes[:, :, None, None] * dw.astype(f32)
        p_win = jax.nn.softmax(jnp.where(vmask, sw, NEG), axis=-1)
        o_win = jnp.einsum('bghqs,bsgd->bqghd', p_win.astype(dtype), vw)
        return o_cmp, o_slc, o_win

    nq = s // QB
    q_blocks = jnp.moveaxis(q.reshape(bsz, nq, QB, G, hg, dk), 1, 0)
    o_cmp, o_slc, o_win = lax.map(block_fn, (q_blocks, jnp.arange(nq) * QB))

    def unblock(o):
        return jnp.moveaxis(o, 0, 1).reshape(bsz, s, G, hg, dk)

    gt = jax.nn.sigmoid(gates.astype(f32)).astype(dtype).reshape(bsz, s, 3, G, hg, 1)
    out = gt[:, :, 0] * unblock(o_cmp) + gt[:, :, 1] * unblock(o_slc) + gt[:, :, 2] * unblock(o_win)
    return out.reshape(bsz, s, NSA_INNER)


def memory_attention(q, mem, mem_norm, w_mem_kv, q_norm, k_norm):
    bsz, s, _ = q.shape
    m = mem.shape[1]
    q = rmsnorm(q.reshape(bsz, s, MEM_HEADS, MEM_HEAD_DIM), q_norm)
    kv = (rmsnorm(mem, mem_norm) @ w_mem_kv).reshape(bsz, m, 2, MEM_HEADS, MEM_HEAD_DIM)
    k = rmsnorm(kv[:, :, 0], k_norm)
    v = kv[:, :, 1]
    sc = jnp.einsum('bqhd,bkhd->bhqk', q, k).astype(jnp.float32) * MEM_HEAD_DIM ** -0.5
    p = jax.nn.softmax(sc, axis=-1)
    o = jnp.einsum('bhqk,bkhd->bqhd', p.astype(v.dtype), v)
    return o.reshape(bsz, s, MEM_INNER)


def hybrid_layer(x, mem, norm_mix, w_in, ssd_conv_w, ssd_conv_b, ssd_dt_bias, ssd_a_log, ssd_d,
                 ssd_norm, w_ssd_o, nsa_q_norm, nsa_k_norm, nsa_cmp_pe, nsa_cmp_w1, nsa_cmp_w2,
                 w_nsa_o, mem_norm, w_mem_kv, mem_q_norm, mem_k_norm, w_mem_o, w_out, norm_ffn,
                 w_ffn_up, ffn_conv_w, ffn_conv_b, w_ffn_down):
    bsz, s, d = x.shape
    xn = rmsnorm(x, norm_mix)
    proj = xn @ w_in
    z, xbc, dt_raw, q_nsa, kv_nsa, g_nsa, q_mem, g_merge = jnp.split(proj, IN_OFFSETS, axis=-1)
    y_ssd = ssd_mixer(z, xbc, dt_raw, ssd_conv_w, ssd_conv_b, ssd_dt_bias, ssd_a_log, ssd_d, ssd_norm) @ w_ssd_o
    y_nsa = nsa_mixer(q_nsa, kv_nsa, g_nsa, nsa_q_norm, nsa_k_norm, nsa_cmp_pe, nsa_cmp_w1, nsa_cmp_w2) @ w_nsa_o
    y_mem = memory_attention(q_mem, mem, mem_norm, w_mem_kv, mem_q_norm, mem_k_norm) @ w_mem_o
    g = jax.nn.sigmoid(g_merge.astype(jnp.float32)).astype(x.dtype).reshape(bsz, s, N_BRANCH, d)
    mixed = g[:, :, 0] * y_ssd + g[:, :, 1] * y_nsa + g[:, :, 2] * y_mem
    h = x + mixed @ w_out
    u = causal_dwconv(rmsnorm(h, norm_ffn) @ w_ffn_up, ffn_conv_w, ffn_conv_b)
    gate, val = jnp.split(u, 2, axis=-1)
    return h + (jax.nn.silu(gate) * val) @ w_ffn_down


def setup_inputs(seed: int = 0) -> dict:
    key = jax.random.key(seed)
    ks = jax.random.split(key, 32)
    f32 = jnp.float32
    L = DEPTH

    def nrm(k, shape, fan_in):
        return jax.random.normal(k, shape, f32) * fan_in ** -0.5

    def gain(k, shape):
        return 1.0 + 0.02 * jax.random.normal(k, shape, f32)

    dt = jnp.exp(jax.random.uniform(ks[6], (L, SSD_HEADS), f32, math.log(1e-3), math.log(1e-1)))
    return {
        "x": jax.random.normal(ks[0], (BATCH, SEQ, D_MODEL), f32),
        "mem": jax.random.normal(ks[1], (BATCH, MEM_TOKENS, D_MODEL), f32),
        "norm_mix": gain(ks[2], (L, D_MODEL)),
        "w_in": nrm(ks[3], (L, D_MODEL, IN_WIDTH), D_MODEL),
        "ssd_conv_w": nrm(ks[4], (L, SSD_CONV, SSD_XBC), SSD_CONV),
        "ssd_conv_b": 0.02 * jax.random.normal(ks[5], (L, SSD_XBC), f32),
        "ssd_dt_bias": dt + jnp.log(-jnp.expm1(-dt)),
        "ssd_a_log": jnp.log(jax.random.uniform(ks[7], (L, SSD_HEADS), f32, 1.0, 16.0)),
        "ssd_d": gain(ks[8], (L, SSD_HEADS)),
        "ssd_norm": gain(ks[9], (L, SSD_INNER)),
        "w_ssd_o": nrm(ks[10], (L, SSD_INNER, D_MODEL), SSD_INNER),
        "nsa_q_norm": gain(ks[11], (L, NSA_HEAD_DIM)),
        "nsa_k_norm": gain(ks[12], (L, 3, NSA_HEAD_DIM)),
        "nsa_cmp_pe": 0.02 * jax.random.normal(ks[13], (L, 2, NSA_CMP_LEN, NSA_HEAD_DIM), f32),
        "nsa_cmp_w1": nrm(ks[14], (L, 2, NSA_CMP_LEN * NSA_HEAD_DIM, NSA_CMP_HIDDEN), NSA_CMP_LEN * NSA_HEAD_DIM),
        "nsa_cmp_w2": nrm(ks[15], (L, 2, NSA_CMP_HIDDEN, NSA_HEAD_DIM), NSA_CMP_HIDDEN),
        "w_nsa_o": nrm(ks[16], (L, NSA_INNER, D_MODEL), NSA_INNER),
        "mem_norm": gain(ks[17], (L, D_MODEL)),
        "w_mem_kv": nrm(ks[18], (L, D_MODEL, 2 * MEM_INNER), D_MODEL),
        "mem_q_norm": gain(ks[19], (L, MEM_HEAD_DIM)),
        "mem_k_norm": gain(ks[20], (L, MEM_HEAD_DIM)),
        "w_mem_o": nrm(ks[21], (L, MEM_INNER, D_MODEL), MEM_INNER),
        "w_out": nrm(ks[22], (L, D_MODEL, D_MODEL), D_MODEL),
        "norm_ffn": gain(ks[23], (L, D_MODEL)),
        "w_ffn_up": nrm(ks[24], (L, D_MODEL, 2 * FFN_HIDDEN), D_MODEL),
        "ffn_conv_w": nrm(ks[25], (L, FFN_CONV, 2 * FFN_HIDDEN), FFN_CONV),
        "ffn_conv_b": 0.02 * jax.random.normal(ks[26], (L, 2 * FFN_HIDDEN), f32),
        "w_ffn_down": nrm(ks[27], (L, FFN_HIDDEN, D_MODEL), FFN_HIDDEN),
    }


def reference(x, mem, norm_mix, w_in, ssd_conv_w, ssd_conv_b, ssd_dt_bias, ssd_a_log, ssd_d,
              ssd_norm, w_ssd_o, nsa_q_norm, nsa_k_norm, nsa_cmp_pe, nsa_cmp_w1, nsa_cmp_w2,
              w_nsa_o, mem_norm, w_mem_kv, mem_q_norm, mem_k_norm, w_mem_o, w_out, norm_ffn,
              w_ffn_up, ffn_conv_w, ffn_conv_b, w_ffn_down):
    h = x
    for i in range(DEPTH):
        h = hybrid_layer(h, mem, norm_mix[i], w_in[i], ssd_conv_w[i], ssd_conv_b[i], ssd_dt_bias[i],
                         ssd_a_log[i], ssd_d[i], ssd_norm[i], w_ssd_o[i], nsa_q_norm[i], nsa_k_norm[i],
                         nsa_cmp_pe[i], nsa_cmp_w1[i], nsa_cmp_w2[i], w_nsa_o[i], mem_norm[i],
                         w_mem_kv[i], mem_q_norm[i], mem_k_norm[i], w_mem_o[i], w_out[i], norm_ffn[i],
                         w_ffn_up[i], ffn_conv_w[i], ffn_conv_b[i], w_ffn_down[i])
    return h
```

```python
import numpy as np
import ml_dtypes
import concourse.bass as bass
import concourse.mybir as mybir
from concourse.bass_utils import run_bass_kernel_spmd

F32 = mybir.dt.float32
BF16 = mybir.dt.bfloat16
ALU = mybir.AluOpType
AF = mybir.ActivationFunctionType
AX = mybir.AxisListType

S = 4096
D = 1024
NT = 32
INW = 7464
O_Z, O_XBC, O_DT, O_QN, O_KV, O_GN, O_QM, O_GM = 0, 1024, 2560, 2576, 3088, 3856, 3880, 4392
FFH = 2816
EPS = 1e-6
NEGB = -30000.0


class Buf:
    __slots__ = ("name", "w", "r")

    def __init__(self, name=""):
        self.name = name
        self.w = []
        self.r = {}


class Eng:
    def __init__(self, fw, name, handle, is_dma=False, nsem=1):
        self.fw = fw
        self.name = name
        self.h = handle
        self.is_dma = is_dma
        self.sems = [fw.nc.alloc_semaphore(f"s_{name}_{i}") for i in range(nsem)]
        self.count = 0
        self.waited = {}

    def wait(self, ev):
        if ev is None:
            return
        sem, val = ev
        k = id(sem)
        if self.waited.get(k, 0) >= val:
            return
        self.waited[k] = val
        self.h.wait_ge(sem, val)

    def signal(self, inst):
        if self.is_dma:
            n = len(self.sems)
            i = self.count
            sem = self.sems[i % n]
            inst.then_inc(sem, 16)
            self.count += 1
            return (sem, 16 * (i // n + 1))
        self.count += 1
        inst.then_inc(self.sems[0], 1)
        return (self.sems[0], self.count)

    def pre_dma(self):
        n = len(self.sems)
        i = self.count
        if i >= n:
            self.wait((self.sems[i % n], 16 * (i // n)))


class FW:
    def __init__(self, nc):
        self.nc = nc
        self.pe = Eng(self, "pe", nc.tensor)
        self.dve = Eng(self, "dve", nc.vector)
        self.act = Eng(self, "act", nc.scalar)
        self.pool = Eng(self, "pool", nc.gpsimd)
        self.sp = Eng(self, "sp", nc.sync, is_dma=True, nsem=12)
        self.pq = Eng(self, "pq", nc.gpsimd, is_dma=True, nsem=8)
        self.pq.waited = self.pool.waited
        self.engs = [self.pe, self.dve, self.act, self.pool, self.sp, self.pq]

    def _deps(self, eng, reads, writes, acc=False):
        for b in reads:
            for ev in b.w:
                eng.wait(ev)
        for b in writes:
            if not acc:
                for ev in b.w:
                    eng.wait(ev)
            for ev in b.r.values():
                eng.wait(ev)

    def _commit(self, ev, reads, writes, acc=False):
        for b in reads:
            b.r[id(ev[0])] = ev
        for b in writes:
            b.w = (b.w + [ev]) if acc else [ev]
            b.r = {}

    def op(self, eng, fn, reads=(), writes=()):
        self._deps(eng, reads, writes)
        inst = fn()
        ev = eng.signal(inst)
        self._commit(ev, reads, writes)
        return ev

    def dma(self, q, out, in_, reads=(), writes=(), acc=False, inflight=None, **kw):
        q.pre_dma()
        if inflight is not None and q.count - inflight >= 0:
            j = q.count - inflight
            n_ = len(q.sems)
            q.wait((q.sems[j % n_], 16 * (j // n_ + 1)))
        self._deps(q, reads, writes, acc)
        inst = q.h.dma_start(out=out, in_=in_, **kw)
        ev = q.signal(inst)
        self._commit(ev, reads, writes, acc)
        return ev

    def barrier(self):
        evs = []
        for e in self.engs:
            if e.is_dma:
                n = len(e.sems)
                for j in range(min(n, e.count)):
                    last = ((e.count - 1 - j) // n) * n + j
                    evs.append((e.sems[j], 16 * (last // n + 1)))
            elif e.count > 0:
                evs.append((e.sems[0], e.count))
        for e in (self.pe, self.dve, self.act, self.pool, self.sp):
            for ev in evs:
                e.wait(ev)


class Tl:
    def __init__(self, t, name):
        self.t = t
        self.b = Buf(name)

    def __getitem__(self, idx):
        return self.t[idx]


class Ring:
    def __init__(self, tiles):
        self.tiles = tiles
        self.i = 0

    def next(self):
        t = self.tiles[self.i % len(self.tiles)]
        self.i += 1
        return t


class K:
    def __init__(self, debug=False):
        self.debug = debug
        self.nc = bass.Bass("TRN2", target_bir_lowering=False)
        self.f = FW(self.nc)
        self.dram = {}
        self.dbuf = {}
        self._n = 0
        self._guards = []

    def din(self, name, shape, dt=F32):
        self.dram[name] = self.nc.dram_tensor(name, list(shape), dt, kind="ExternalInput").ap()
        self.dbuf[name] = Buf(name)
        return self.dram[name]

    def dscr(self, name, shape, dt, out=False):
        kind = "ExternalOutput" if (out or self.debug) else "Internal"
        self.dram[name] = self.nc.dram_tensor(name, list(shape), dt, kind=kind).ap()
        self.dbuf[name] = Buf(name)
        return self.dram[name]

    def sb(self, name, shape, dt):
        self._n += 1
        g = self.nc.sbuf_tensor(f"{name}_{self._n}", list(shape), dt)
        t = g.__enter__()
        self._guards.append(g)
        return Tl(t, name)

    def free_phase(self, keep=0):
        while len(self._guards) > keep:
            self._guards.pop().__exit__(None, None, None)

    def ring(self, name, shape, dt, n):
        return Ring([self.sb(f"{name}{i}", shape, dt) for i in range(n)])

    def ps(self, name, shape, dt=F32):
        self._n += 1
        return Tl(self.nc.alloc_psum_tensor(f"{name}_{self._n}", list(shape), dt), name)

    def V(self, fn, reads=(), writes=()):
        return self.f.op(self.f.dve, fn, [t.b if isinstance(t, Tl) else t for t in reads],
                         [t.b if isinstance(t, Tl) else t for t in writes])

    def A(self, fn, reads=(), writes=()):
        return self.f.op(self.f.act, fn, [t.b if isinstance(t, Tl) else t for t in reads],
                         [t.b if isinstance(t, Tl) else t for t in writes])

    def G(self, fn, reads=(), writes=()):
        return self.f.op(self.f.pool, fn, [t.b if isinstance(t, Tl) else t for t in reads],
                         [t.b if isinstance(t, Tl) else t for t in writes])

    def P(self, fn, reads=(), writes=()):
        return self.f.op(self.f.pe, fn, [t.b if isinstance(t, Tl) else t for t in reads],
                         [t.b if isinstance(t, Tl) else t for t in writes])

    def dma(self, out, in_, reads=(), writes=(), q=None, **kw):
        q = q or self.f.sp
        return self.f.dma(q, out, in_, [t.b if isinstance(t, Tl) else t for t in reads],
                          [t.b if isinstance(t, Tl) else t for t in writes], **kw)

    def load(self, tl, src_ap, src_name, idx=slice(None), cast=False, **kw):
        q = self.f.pq if cast else self.f.sp
        return self.dma(tl.t[idx], src_ap, reads=[], writes=[tl], q=q, **kw)

    def store(self, dst_ap, dst_name, tl, idx=slice(None), **kw):
        q = self.f.pq if getattr(self, "store_on_pq", False) else None
        return self.dma(dst_ap, tl.t[idx], reads=[tl], writes=[], q=q, **kw)

    def mm(self, out_ps, pairs, reads, out_ap=None):
        nc = self.nc
        oap = out_ap if out_ap is not None else out_ps.t[:]

        def fn():
            n = len(pairs)
            for i, (l, r) in enumerate(pairs):
                ins = nc.tensor.matmul(oap, l, r, start=(i == 0), stop=(i == n - 1))
            return ins
        return self.P(fn, reads=reads, writes=[out_ps])

    def rsqrt(self, out_tl, in_tl, scale, shape_idx=slice(None)):
        nc = self.nc
        psl = shape_idx[0] if isinstance(shape_idx, tuple) else slice(None)
        self.A(lambda: nc.scalar.activation(out=out_tl.t[shape_idx], in_=in_tl.t[shape_idx], func=AF.Ln,
                                            scale=scale, bias=self.c_eps.t[psl, 0:1]),
               reads=[in_tl, self.c_eps], writes=[out_tl])
        self.A(lambda: nc.scalar.activation(out=out_tl.t[shape_idx], in_=out_tl.t[shape_idx], func=AF.Exp,
                                            scale=-0.5),
               reads=[out_tl], writes=[out_tl])


COLS = {}
ROWS = {}


def _mk_tables(p):
    cols, rows = [], []

    def addc(name, arr):
        arr = np.asarray(arr, np.float32).reshape(128, -1)
        COLS[name] = (sum(a.shape[1] for a in cols), arr.shape[1])
        cols.append(arr)

    def addr(name, vec):
        vec = np.asarray(vec, np.float32).reshape(1, -1)
        ROWS[name] = (sum(a.shape[1] for a in rows), vec.shape[1])
        rows.append(np.broadcast_to(vec, (128, vec.shape[1])))

    def fm(v):
        return np.asarray(v, np.float32).reshape(-1, 128).T

    addc("g_mix", fm(p["norm_mix"][0]))
    addc("g_qn", np.tile(p["nsa_q_norm"][0], 2)[:, None])
    addc("g_ks", np.tile(p["nsa_k_norm"][0, 1], 2)[:, None])
    addc("g_kw", np.tile(p["nsa_k_norm"][0, 2], 2)[:, None])
    addc("g_kc", np.tile(p["nsa_k_norm"][0, 0], 2)[:, None])
    addc("g_mq", p["mem_q_norm"][0][:, None])
    addc("g_mk", p["mem_k_norm"][0][:, None])
    addc("g_memn", fm(p["mem_norm"][0]))
    addc("g_ssdn", fm(p["ssd_norm"][0]))
    addc("g_ffn", fm(p["norm_ffn"][0]))
    addc("cw_ssd", np.stack([fm(p["ssd_conv_w"][0, k]) for k in range(4)], 1).reshape(128, -1))
    addc("cb_ssd", fm(p["ssd_conv_b"][0]))
    addc("cw_ffn", np.stack([fm(p["ffn_conv_w"][0, k]) for k in range(3)], 1).reshape(128, -1))
    addc("cb_ffn", fm(p["ffn_conv_b"][0]))
    addr("dtb", p["ssd_dt_bias"][0])
    addr("alog", p["ssd_a_log"][0])
    addr("ssd_d", p["ssd_d"][0])
    return np.concatenate(cols, 1), np.ascontiguousarray(np.concatenate(rows, 1))


def _consts():
    c = {}
    c["ident"] = np.eye(128, dtype=np.float32)
    i = np.arange(128)
    c["c_tri"] = (i[:, None] <= i[None, :]).astype(np.float32)
    negm = np.where(i[None, :] < i[:, None], NEGB, 0.0).astype(np.float32)
    c["c_negm4"] = np.tile(negm, (1, 4))
    d16 = np.zeros((16, 16, 128), np.float32)
    d16[np.arange(16), np.arange(16), :] = 1.0
    c["c_delta16"] = d16.reshape(16, 2048)
    sel = np.zeros((128, 128), np.float32); sel[127, :] = 1.0
    c["c_sel127"] = sel
    tok = np.arange(S)
    c["c_eblk"] = (tok[None, :] // 64 == np.arange(64)[:, None]).astype(np.float32)
    kside = np.zeros((2, 64, S), np.float32)
    kside[0, :60] = c["c_eblk"][:60]
    for v in range(2):
        kside[v, 60] = tok % 128
        kside[v, 61] = tok // 128
        kside[v, 62] = 1.0
        kside[v, 63] = 1.0
    c["c_kside"] = kside.reshape(128, S)
    raq = np.zeros((32, 2, 4, 4, 128), np.float32)
    for it in range(32):
        for g in range(2):
            for hl in range(4):
                sl = 2.0 ** -(4 * g + hl + 1)
                raq[it, g, 0, hl] = 8 * sl
                raq[it, g, 1, hl] = 1024 * sl
                raq[it, g, 2, hl] = -8 * sl * i
                raq[it, g, 3, hl] = -1024 * sl * it
    c["c_raq"] = raq.reshape(32 * 2 * 4, 512)
    la = np.zeros((3, S), np.float32)
    la[0] = tok % 128; la[1] = 1.0; la[2] = tok // 128
    c["c_laall"] = la
    ra = np.zeros((3, 2, 4, 128), np.float32)
    for g in range(2):
        for hl in range(4):
            sl = 2.0 ** -(4 * g + hl + 1)
            ra[0, g, hl] = 8 * sl
            ra[1, g, hl] = -8 * sl * i
            ra[2, g, hl] = -1024 * sl
    c["c_ra"] = ra.reshape(3, 1024)
    rac = np.zeros((3, 2, 4, 128), np.float32)
    for g in range(2):
        for hl in range(4):
            sl = 2.0 ** -(4 * g + hl + 1)
            rac[0, g, hl] = 128 * sl
            rac[1, g, hl] = -8 * sl * (i - 31)
            rac[2, g, hl] = -1024 * sl
    c["c_rac"] = rac.reshape(3, 1024)
    kk = np.arange(16)
    shm = np.zeros((16, 32, 128), np.float32)
    for dc in range(32):
        shm[:, dc, :] = ((i[None, :] - 8 * dc) == (kk[:, None] - 8))
    c["c_shm"] = shm.reshape(16, S)
    c["c_cmask"] = np.tile(np.where(16 * (kk[:, None] - 8) > (i[None, :] - 31), 8 * NEGB, 0.0).astype(np.float32), (1, 4))
    c["c_caus"] = np.tile(np.where(i[:, None] > i[None, :], 8 * NEGB, 0.0).astype(np.float32), (1, 4))
    c["c_winm"] = np.tile(np.where(i[:, None] <= i[None, :], 8 * NEGB, 0.0).astype(np.float32), (1, 4))
    c["c_d0"] = (i[None, :] - 16.0 * i[:, None]).astype(np.float32)
    rel = i[None, :] - 64 - (i[:, None] // 64)
    c["c_mw1"] = np.where(rel >= -1, 0.0, 1.0).astype(np.float32)
    c["c_mw2"] = np.where((rel == 0) | (rel == -1), 1e9, np.where(rel > 0, -1e30, 0.0)).astype(np.float32)
    n = np.arange(255)
    j = np.arange(64)
    ov = np.clip(np.minimum(16 * n[:, None] + 31, 64 * j[None, :] + 63) - np.maximum(16 * n[:, None], 64 * j[None, :]) + 1, 0, None) / 32.0
    ovl = np.zeros((128, 2, 64), np.float32)
    ovl[:, 0] = ov[:128]; ovl[:127, 1] = ov[128:]
    c["c_ovl"] = ovl.reshape(128, 128)
    return c


def bc(ap, shape):
    return ap.to_broadcast(list(shape))


def phase_setup(k):
    nc = k.nc
    k.din("cols", [128, k.ncols])
    k.din("rows", [128, k.nrows])
    k.din("ident", [128, 128])
    k.cols = k.sb("cols", [128, k.ncols], F32)
    k.rows = k.sb("rows", [128, k.nrows], F32)
    k.ident = k.sb("ident", [128, 128], BF16)
    k.identf = k.sb("identf", [128, 128], F32)
    k.c_eps = k.sb("c_eps", [128, 1], F32)
    k.load(k.cols, k.dram["cols"], "cols")
    k.load(k.rows, k.dram["rows"], "rows")
    k.load(k.ident, k.dram["ident"], "ident", cast=True)
    k.load(k.identf, k.dram["ident"], "ident")
    k.V(lambda: nc.vector.memset(k.c_eps.t[:], EPS), writes=[k.c_eps])
    k.psf = Ring([k.ps(f"psf{i}", [128, 512], F32) for i in range(6)])
    k.psb = Ring([k.ps(f"psb{i}", [128, 1024], BF16) for i in range(2)])


def col(k, name, j=0, n=1):
    s0, _ = COLS[name]
    return k.cols.t[:, s0 + j:s0 + j + n]


def row(k, name):
    s0, n = ROWS[name]
    return k.rows.t[:, s0:s0 + n]


def transpose_to(k, dst_tl, dst_ap_fn, src_tl, nblk, gain=None, reads=(), ring=None):
    nc = k.nc
    pt = (ring or k.psb).next()

    def tr():
        for j in range(nblk):
            ins = nc.tensor.transpose(pt.t[:, j * 128:(j + 1) * 128], src_tl.t[:, j * 128:(j + 1) * 128], k.ident.t[:])
        return ins
    k.P(tr, reads=[src_tl, k.ident] + list(reads), writes=[pt])
    return pt


CONV_W = (("w_ssd_o", [1024, 1024]), ("c_negm4", [128, 512]),
          ("cmp_w1", [4096, 256]), ("cmp_w2", [512, 64]), ("c_peT", [64, 64]),
          ("c_kside", [128, S]), ("c_eblk", [64, S]), ("c_raq", [256, 512]), ("c_laall", [3, S]), ("c_shm", [16, S]),
          ("c_rac", [3, 1024]), ("c_cmask", [16, 512]), ("c_caus", [128, 512]), ("c_winm", [128, 512]), ("c_ovl", [128, 128]),
          ("w_nsa_o", [512, 1024]), ("w_mem_kv", [1024, 1024]), ("w_mem_o", [512, 1024]), ("w_out", [1024, 1024]),
          ("w_ffn_up", [1024, 2 * FFH]), ("w_ffn_down", [FFH, 1024]))


LATE_CONV = ("w_mem_kv", "w_mem_o", "w_out", "w_ffn_up", "w_ffn_down")


def convert_weights(k, after=()):
    k.late_conv = []
    for nm, shp in CONV_W:
        src = k.din(nm, shp)
        dst = k.dscr(nm + "_bf", shp, BF16)
        rows = shp[0]
        if nm in LATE_CONV:
            step = 64 if nm == "w_ffn_up" else 128
            for r0 in range(0, rows, step):
                r1 = min(rows, r0 + step)
                k.late_conv.append((dst[r0:r1, :], src[r0:r1, :], nm))
            continue
        step = 256
        for r0 in range(0, rows, step):
            r1 = min(rows, r0 + step)
            k.dma(dst[r0:r1, :], src[r0:r1, :], reads=list(after), writes=[k.dbuf[nm + "_bf"]], q=k.f.pq, acc=True)


def issue_late_conv(k, n=1):
    for _ in range(n):
        if getattr(k, "late_conv", None):
            d_, s_, nm = k.late_conv.pop(0)
            k.dma(d_, s_, writes=[k.dbuf[nm + "_bf"]], q=k.f.pq, acc=True, inflight=2)


def wload(k, tl, name, rearr, idx=slice(None), **kw):
    src = k.dram[name + "_bf"]
    return k.dma(tl.t[idx], src.rearrange(rearr, **kw), reads=[k.dbuf[name + "_bf"]], writes=[tl])


def phase_a(k):
    nc = k.nc
    x = k.din("x", [S, D])
    w_in = k.din("w_in", [D, INW])
    z_s = k.dscr("z_s", [S, 1024], BF16)
    xbcT_s = k.dscr("xbcT_s", [12, 128, S], BF16)
    dts_s = k.dscr("dts_s", [S, 64], F32)
    qT_s = k.dscr("qT_s", [4, 128, S], BF16)
    kTs_s = k.dscr("kTs_s", [128, S], BF16)
    kTw_s = k.dscr("kTw_s", [128, S], BF16)
    v_s = k.dscr("v_s", [S, 256], BF16)
    gn_s = k.dscr("gn_s", [S, 24], F32)
    qmT_s = k.dscr("qmT_s", [4, 128, S], BF16)
    kvcT_s = k.dscr("kvcT_s", [2, 128, S], BF16)
    gmT_s = k.dscr("gmT_s", [24, 128, S], BF16)

    WIN = k.sb("WIN", [128, 8, INW], BF16)
    wgroups = [(0, 1024), (2560, 3088), (3344, 4392), (1024, 2560), (3088, 3344), (4392, 5928), (5928, INW)]
    WINb = {}
    for (c_lo, c_hi) in wgroups:
        b_ = Buf(f"WIN{c_lo}")
        for kk in range(8):
            k.dma(WIN.t[:, kk, c_lo:c_hi], w_in[kk * 128:(kk + 1) * 128, c_lo:c_hi], writes=[b_], q=k.f.pq, acc=True, inflight=3)
        for c in range(c_lo, c_hi, 8):
            WINb[c] = b_

    def wbuf(c0):
        return WINb[(c0 // 8) * 8]
    convert_weights(k, after=[WINb[c_lo] for (c_lo, _c) in wgroups])
    psb_main = Ring([k.psb.tiles[0]])
    psb_prep = Ring([k.psb.tiles[1]])
    k.arow = k.sb("arow", [128, 16], F32)
    k.A(lambda: nc.scalar.activation(out=k.arow.t[:], in_=row(k, "alog"), func=AF.Exp), reads=[k.rows], writes=[k.arow])
    k.V(lambda: nc.vector.tensor_scalar(out=k.arow.t[:], in0=k.arow.t[:], scalar1=-1.0, scalar2=None, op0=ALU.mult),
        reads=[k.arow], writes=[k.arow])

    xt_r = k.ring("xt", [128, D], F32, 3)
    junk = k.sb("junk", [128, D], BF16)
    ss_r = k.ring("ss", [128, 8], F32, 4)
    xs_r = k.ring("xs", [128, D], BF16, 2)
    ssp_r = k.ring("ssp", [128, 8], F32, 2)
    xnT_r = k.ring("xnT", [128, 8, 512], BF16, 2)
    zt_r = k.ring("zt", [128, 1024], BF16, 2)
    fm_r = k.ring("fm", [128, 6, 512], BF16, 3)
    dt_r = k.ring("dtt", [128, 64], F32, 2)
    for t_ in dt_r.tiles:
        k.V(lambda: nc.vector.memset(t_.t[:, 32:48], 0.0), writes=[t_])
    sq_r = k.ring("sq", [128, 512], F32, 2)
    qn_r = k.ring("qn", [128, 512], BF16, 2)
    qmn_r = k.ring("qmn", [128, 512], BF16, 2)
    qT_r = k.ring("qTt", [128, 4, 128], BF16, 2)
    kn_r = k.ring("kn", [128, 256], BF16, 2)
    kT_r = k.ring("kTt", [128, 2, 128], BF16, 2)
    vv_r = k.ring("vv", [128, 256], BF16, 2)
    gn_r = k.ring("gnt", [128, 24], F32, 2)

    def tok_major(xnT, tt, tok0, c0, n):
        pt = k.psf.next()
        k.mm(pt, [(xnT.t[:, kk, tt * 128:(tt + 1) * 128], WIN.t[:, kk, c0:c0 + n]) for kk in range(8)],
             reads=[xnT, wbuf(c0)], out_ap=pt.t[:, 0:n])
        run_deferred()
        fill()
        return pt

    def headnorm(pt, n, hd, c_lo=0):
        nh = n // hd
        sq = sq_r.next()
        k.A(lambda: nc.scalar.activation(out=sq.t[:, 0:n], in_=pt.t[:, c_lo:c_lo + n], func=AF.Square), reads=[pt], writes=[sq])
        ss = ss_r.next()
        k.V(lambda: nc.vector.tensor_reduce(out=ss.t[:, 0:nh], in_=sq.t[:, 0:n].rearrange("p (h d) -> p h d", d=hd),
                                            axis=AX.X, op=ALU.add), reads=[sq], writes=[ss])
        k.rsqrt(ss, ss, 1.0 / hd, (slice(None), slice(0, nh)))
        return ss

    def prep(tb, xnT):
        xts = []

        def ld(tt):
            xt_ = xt_r.next()
            k.load(xt_, x[tb * 512 + tt * 128:tb * 512 + (tt + 1) * 128, :], "x")
            xts.append(xt_)
        ld(0)
        ld(1)
        yield
        for tt in range(4):
            tok0 = tb * 512 + tt * 128
            xt = xts[tt]
            if tt + 2 < 4:
                ld(tt + 2)
            ss = ssp_r.next()
            k.A(lambda: nc.scalar.activation(out=junk.t[:], in_=xt.t[:], func=AF.Square, accum_out=ss.t[:, 0:1]),
                reads=[xt], writes=[junk, ss])
            yield
            k.rsqrt(ss, ss, 1.0 / D, (slice(None), slice(0, 1)))
            yield
            xs = xs_r.next()
            k.V(lambda: nc.vector.tensor_scalar(out=xs.t[:], in0=xt.t[:], scalar1=ss.t[:, 0:1], scalar2=None, op0=ALU.mult),
                reads=[xt, ss], writes=[xs])
            yield
            pt = transpose_to(k, None, None, xs, 8, ring=psb_prep)
            k.V(lambda: nc.vector.tensor_tensor(
                out=xnT.t[:, :, tt * 128:(tt + 1) * 128], in0=pt.t[:].rearrange("p (k t) -> p k t", t=128),
                in1=bc(col(k, "g_mix", 0, 8).unsqueeze(2), [128, 8, 128]), op=ALU.mult),
                reads=[pt, k.cols], writes=[xnT])
            yield

    filler = [None]
    deferred = []
    gcount = [0]

    def run_deferred(force=False):
        gcount[0] += 1
        while deferred and (force or deferred[0][0] <= gcount[0]):
            deferred.pop(0)[1]()

    def fill():
        if filler[0] is not None:
            next(filler[0], None)

    xnT_next = xnT_r.next()
    for _ in prep(0, xnT_next):
        pass
    for tb in range(8):
        xnT = xnT_next
        if tb + 1 < 8:
            xnT_next = xnT_r.next()
            filler[0] = prep(tb + 1, xnT_next)
        else:
            filler[0] = None
        for tt in range(4):
            tok0 = tb * 512 + tt * 128
            zt = zt_r.next()
            for hh in range(2):
                pt = tok_major(xnT, tt, tok0, O_Z + hh * 512, 512)
                k.A(lambda: nc.scalar.copy(out=zt.t[:, hh * 512:(hh + 1) * 512], in_=pt.t[:]), reads=[pt], writes=[zt])
            k.store(z_s[tok0:tok0 + 128, :], "z_s", zt)
            pt = tok_major(xnT, tt, tok0, O_DT, 16)
            dtt = dt_r.next()
            k.V(lambda: nc.vector.tensor_tensor(out=dtt.t[:, 0:16], in0=pt.t[:, 0:16], in1=row(k, "dtb"), op=ALU.add),
                reads=[pt, k.rows], writes=[dtt])
            k.A(lambda: nc.scalar.activation(out=dtt.t[:, 0:16], in_=dtt.t[:, 0:16], func=AF.Exp), reads=[dtt], writes=[dtt])
            k.A(lambda: nc.scalar.activation(out=dtt.t[:, 0:16], in_=dtt.t[:, 0:16], func=AF.Ln, bias=1.0), reads=[dtt], writes=[dtt])
            k.V(lambda: nc.vector.tensor_tensor(out=dtt.t[:, 16:32], in0=dtt.t[:, 0:16], in1=k.arow.t[:], op=ALU.mult),
                reads=[dtt, k.arow], writes=[dtt])
            k.V(lambda: nc.vector.tensor_scalar(out=dtt.t[:, 48:64], in0=dtt.t[:, 16:32], scalar1=-1.0, scalar2=None, op0=ALU.mult),
                reads=[dtt], writes=[dtt])
            k.store(dts_s[tok0:tok0 + 128, :], "dts_s", dtt)
            pt = tok_major(xnT, tt, tok0, O_QN, 512)
            rs = headnorm(pt, 512, 64)
            qn = qn_r.next()
            k.V(lambda: nc.vector.tensor_tensor(out=qn.t[:].rearrange("p (h d) -> p h d", d=64),
                                                in0=pt.t[:].rearrange("p (h d) -> p h d", d=64),
                                                in1=bc(rs.t[:, 0:8].unsqueeze(2), [128, 8, 64]), op=ALU.mult),
                reads=[pt, rs], writes=[qn])
            def dq(qn=qn, tok0=tok0):
                ptb = transpose_to(k, None, None, qn, 4, ring=psb_main)
                qTt = qT_r.next()
                k.V(lambda: nc.vector.tensor_scalar(out=qTt.t[:].rearrange("p j t -> p (j t)"), in0=ptb.t[:, 0:512],
                                                    scalar1=col(k, "g_qn"), scalar2=None, op0=ALU.mult),
                    reads=[ptb, k.cols], writes=[qTt])
                k.store(qT_s[:, :, tok0:tok0 + 128].rearrange("j p t -> p j t"), "qT_s", qTt)
            deferred.append((gcount[0] + 6, dq))
            pt = tok_major(xnT, tt, tok0, O_KV + 256, 512)
            rs = headnorm(pt, 512, 64)
            kn = kn_r.next()
            for b, c_lo, r_lo in ((0, 0, 0), (1, 256, 4)):
                k.V(lambda: nc.vector.tensor_tensor(out=kn.t[:, b * 128:(b + 1) * 128].rearrange("p (h d) -> p h d", d=64),
                                                    in0=pt.t[:, c_lo:c_lo + 128].rearrange("p (h d) -> p h d", d=64),
                                                    in1=bc(rs.t[:, r_lo:r_lo + 2].unsqueeze(2), [128, 2, 64]), op=ALU.mult),
                    reads=[pt, rs], writes=[kn])
            vv = vv_r.next()
            k.A(lambda: nc.scalar.copy(out=vv.t[:].rearrange("p (b c) -> p b c", b=2),
                                       in_=pt.t[:].rearrange("p (b c) -> p b c", b=2)[:, :, 128:256]), reads=[pt], writes=[vv])
            k.store(v_s[tok0:tok0 + 128, :], "v_s", vv)
            def dk(kn=kn, tok0=tok0):
                ptb = transpose_to(k, None, None, kn, 2, ring=psb_main)
                kTt = kT_r.next()
                k.V(lambda: nc.vector.tensor_scalar(out=kTt.t[:, 0, :], in0=ptb.t[:, 0:128], scalar1=col(k, "g_ks"), scalar2=None,
                                                    op0=ALU.mult), reads=[ptb, k.cols], writes=[kTt])
                k.V(lambda: nc.vector.tensor_scalar(out=kTt.t[:, 1, :], in0=ptb.t[:, 128:256], scalar1=col(k, "g_kw"), scalar2=None,
                                                    op0=ALU.mult), reads=[ptb, k.cols], writes=[kTt])
                k.store(kTs_s[:, tok0:tok0 + 128], "kTs_s", kTt, (slice(None), 0, slice(None)))
                k.store(kTw_s[:, tok0:tok0 + 128], "kTw_s", kTt, (slice(None), 1, slice(None)))
            deferred.append((gcount[0] + 6, dk))
            pt = tok_major(xnT, tt, tok0, O_GN, 24)
            gnt = gn_r.next()
            k.A(lambda: nc.scalar.activation(out=gnt.t[:], in_=pt.t[:, 0:24], func=AF.Exp, scale=-1.0), reads=[pt], writes=[gnt])
            k.V(lambda: nc.vector.tensor_scalar(out=gnt.t[:], in0=gnt.t[:], scalar1=1.0, scalar2=None, op0=ALU.add),
                reads=[gnt], writes=[gnt])
            k.V(lambda: nc.vector.reciprocal(out=gnt.t[:], in_=gnt.t[:]), reads=[gnt], writes=[gnt])
            k.store(gn_s[tok0:tok0 + 128, :], "gn_s", gnt)
            pt = tok_major(xnT, tt, tok0, O_QM, 512)
            rs = headnorm(pt, 512, 128)
            qn = qmn_r.next()
            k.V(lambda: nc.vector.tensor_tensor(out=qn.t[:].rearrange("p (h d) -> p h d", d=128),
                                                in0=pt.t[:].rearrange("p (h d) -> p h d", d=128),
                                                in1=bc(rs.t[:, 0:4].unsqueeze(2), [128, 4, 128]), op=ALU.mult),
                reads=[pt, rs], writes=[qn])
            def dm(qn=qn, tok0=tok0):
                ptb = transpose_to(k, None, None, qn, 4, ring=psb_main)
                qTt = qT_r.next()
                k.V(lambda: nc.vector.tensor_scalar(out=qTt.t[:].rearrange("p j t -> p (j t)"), in0=ptb.t[:, 0:512],
                                                    scalar1=col(k, "g_mq"), scalar2=None, op0=ALU.mult),
                    reads=[ptb, k.cols], writes=[qTt])
                k.store(qmT_s[:, :, tok0:tok0 + 128].rearrange("j p t -> p j t"), "qmT_s", qTt)
            deferred.append((gcount[0] + 6, dm))
        t0 = tb * 512

        def feat_major(c0):
            pt = k.psf.next()
            k.mm(pt, [(WIN.t[:, kk, c0:c0 + 128], xnT.t[:, kk, :]) for kk in range(8)], reads=[xnT, wbuf(c0)])
            run_deferred()
            fill()
            return pt
        for grp in range(2):
            fm = fm_r.next()
            for j in range(6):
                m = grp * 6 + j
                pt = feat_major(O_XBC + m * 128)
                if j % 2 == 0:
                    k.V(lambda: nc.vector.tensor_copy(out=fm.t[:, j, :], in_=pt.t[:]), reads=[pt], writes=[fm])
                else:
                    k.A(lambda: nc.scalar.copy(out=fm.t[:, j, :], in_=pt.t[:]), reads=[pt], writes=[fm])
            k.store(xbcT_s[grp * 6:grp * 6 + 6, :, t0:t0 + 512].rearrange("m p t -> p m t"), "xbcT_s", fm)
        fm = fm_r.next()
        for j in range(2):
            pt = feat_major(O_KV + j * 128)
            k.V(lambda: nc.vector.tensor_copy(out=fm.t[:, j, :], in_=pt.t[:]), reads=[pt], writes=[fm])
        k.store(kvcT_s[:, :, t0:t0 + 512].rearrange("m p t -> p m t"), "kvcT_s", fm, (slice(None), slice(0, 2), slice(None)))
        for grp in range(4):
            fm = fm_r.next()
            for j in range(6):
                m = grp * 6 + j
                pt = feat_major(O_GM + m * 128)
                k.A(lambda: nc.scalar.activation(out=fm.t[:, j, :], in_=pt.t[:], func=AF.Sigmoid), reads=[pt], writes=[fm])
            k.store(gmT_s[grp * 6:grp * 6 + 6, :, t0:t0 + 512].rearrange("m p t -> p m t"), "gmT_s", fm)
        if tb == 7:
            run_deferred(force=True)
        if filler[0] is not None:
            for _ in filler[0]:
                pass


def phase_b(k):
    nc = k.nc
    for nm, shp in (("c_tri", [128, 128]), ("c_delta16", [16, 2048]), ("c_sel127", [128, 128])):
        k.din(nm, shp)
    mix1T_s = k.dscr("mix1T_s", [8, 128, S], BF16)
    xbcT_s, dts_s, z_s, gmT_s = k.dram["xbcT_s"], k.dram["dts_s"], k.dram["z_s"], k.dram["gmT_s"]

    TRI = k.sb("TRI", [128, 128], F32); k.load(TRI, k.dram["c_tri"], "")
    DEL = k.sb("DEL", [16, 16, 128], F32); k.load(DEL, k.dram["c_delta16"].rearrange("p (h l) -> p h l", l=128), "")
    SEL = k.sb("SEL", [128, 128], F32); k.load(SEL, k.dram["c_sel127"], "")
    WSO = k.sb("WSO", [128, 8, 1024], BF16)
    wload(k, WSO, "w_ssd_o", "(k p) n -> p k n", p=128)
    DGC = k.sb("DGC", [128, 48, 128], BF16)
    k.V(lambda: nc.vector.tensor_tensor(out=DGC.t[:], in0=bc(k.identf.t[:].unsqueeze(1), [128, 48, 128]),
                                        in1=bc(col(k, "cw_ssd", 0, 48).unsqueeze(2), [128, 48, 128]), op=ALU.mult),
        reads=[k.identf, k.cols], writes=[DGC])
    DIAGD = k.sb("DIAGD", [128, 16, 128], BF16)
    k.V(lambda: nc.vector.tensor_tensor(out=DIAGD.t[:], in0=bc(k.identf.t[:].unsqueeze(1), [128, 16, 128]),
                                        in1=bc(row(k, "ssd_d").unsqueeze(2), [128, 16, 128]), op=ALU.mult),
        reads=[k.identf, k.rows], writes=[DIAGD])
    state = k.sb("state", [128, 1024], F32)
    prevbf = k.sb("prevbf", [128, 1024], BF16)
    k.V(lambda: nc.vector.memset(state.t[:], 0.0), writes=[state])
    k.V(lambda: nc.vector.memset(prevbf.t[:], 0.0), writes=[prevbf])

    NEGM4b = k.sb("NEGM4b", [128, 512], BF16)
    k.dma(NEGM4b.t[:], k.dram["c_negm4_bf"], reads=[k.dbuf["c_negm4_bf"]], writes=[NEGM4b])
    LT_r = k.ring("LT", [48, 128], F32, 2)
    RT_r = k.ring("RT", [48, 16, 128], F32, 2)
    for t_ in LT_r.tiles:
        k.V(lambda: nc.vector.memset(t_.t[:], 0.0), writes=[t_])
        k.V(lambda: nc.vector.memset(t_.t[0:16, :], 1.0), writes=[t_])
    for t_ in RT_r.tiles:
        k.V(lambda: nc.vector.memset(t_.t[:], 0.0), writes=[t_])
        k.dma(t_.t[32:48, :, :], k.dram["c_delta16"].rearrange("p (h l) -> p h l", l=128), writes=[t_])

    XR_r = k.ring("XR", [128, 12, 515], BF16, 2)
    for t in XR_r.tiles:
        k.V(lambda: nc.vector.memset(t.t[:, :, 0:3], 0.0), writes=[t])
    XC_r = k.ring("XC", [128, 12, 512], BF16, 2)
    GM_r = k.ring("GM", [128, 8, 512], BF16, 2)
    ynT_r = k.ring("ynT", [128, 8, 512], BF16, 2)
    mo_r = k.ring("mo", [128, 512], BF16, 2)
    dts_r = k.ring("dts", [128, 64], F32, 2)
    z_r = k.ring("zc", [128, 1024], BF16, 2)
    sm_r = k.ring("sm", [128, 64], F32, 2)
    xtok_r = k.ring("xtok", [128, 3, 1024], BF16, 1)
    bt_r = k.ring("btok", [128, 256], BF16, 1)
    cbT_r = k.ring("cbT", [128, 2, 128], BF16, 1)
    lm_r = k.ring("lm", [128, 4, 128], BF16, 2)
    MT_r = k.ring("MT", [128, 16, 128], BF16, 1)
    yf_r = k.ring("yf", [128, 1024], F32, 2)
    zs_r = k.ring("zs", [128, 1024], F32, 2)
    yn_r = k.ring("yn", [128, 1024], BF16, 2)
    junk = k.sb("junkb", [128, 1024], BF16)

    def load_chunk(c):
        dts = dts_r.next(); k.load(dts, dts_s[c * 128:(c + 1) * 128, :], "")
        zc = z_r.next(); k.load(zc, z_s[c * 128:(c + 1) * 128, :], "")
        return dts, zc

    def tail(yn, ynT, cs, do_proj, GM, t0):
        pt = transpose_to(k, None, None, yn, 8)
        k.V(lambda: nc.vector.tensor_tensor(out=ynT.t[:, :, cs], in0=pt.t[:].rearrange("p (k t) -> p k t", t=128),
                                            in1=bc(col(k, "g_ssdn", 0, 8).unsqueeze(2), [128, 8, 128]), op=ALU.mult),
            reads=[pt, k.cols], writes=[ynT])
        if do_proj:
            for e in range(8):
                pt = k.psf.next()
                k.mm(pt, [(WSO.t[:, kk, e * 128:(e + 1) * 128], ynT.t[:, kk, :]) for kk in range(8)], reads=[WSO, ynT])
                mo = mo_r.next()
                k.V(lambda: nc.vector.tensor_tensor(out=mo.t[:], in0=pt.t[:], in1=GM.t[:, e, :], op=ALU.mult),
                    reads=[pt, GM], writes=[mo])
                k.store(mix1T_s[e, :, t0:t0 + 512], "", mo)

    nxt = load_chunk(0)
    pending_gate = None
    for tb in range(8):
        t0 = tb * 512
        XR = XR_r.next()
        if tb == 0:
            k.load(XR, xbcT_s[:, :, 0:512].rearrange("m p t -> p m t"), "", (slice(None), slice(None), slice(3, 515)))
        else:
            k.load(XR, xbcT_s[:, :, t0 - 3:t0 + 512].rearrange("m p t -> p m t"), "")
        XC = XC_r.next()
        for m in range(12):
            pt = k.psf.next()
            k.mm(pt, [(DGC.t[:, kk * 12 + m, :], XR.t[:, m, kk:kk + 512]) for kk in range(4)], reads=[DGC, XR])
            k.A(lambda: nc.scalar.activation(out=XC.t[:, m, :], in_=pt.t[:], func=AF.Silu, bias=col(k, "cb_ssd", m)),
                reads=[pt, k.cols], writes=[XC])
        GM = GM_r.next()
        k.load(GM, gmT_s[0:8, :, t0:t0 + 512].rearrange("m p t -> p m t"), "")
        ynT = ynT_r.next()
        for cc in range(4):
            c = tb * 4 + cc
            cs = slice(cc * 128, (cc + 1) * 128)
            dts, zc = nxt
            if c + 1 < NT:
                nxt = load_chunk(c + 1)
            p1 = k.psf.next()
            k.mm(p1, [(TRI.t[:], dts.t[:, 16:32])], reads=[TRI, dts], out_ap=p1.t[:, 0:16])
            p2 = k.psf.next()
            k.mm(p2, [(dts.t[:, 16:64], TRI.t[:])], reads=[TRI, dts], out_ap=p2.t[0:48, 0:128])
            sm = sm_r.next()
            k.V(lambda: nc.vector.tensor_copy(out=sm.t[:, 0:16], in_=p1.t[:, 0:16]), reads=[p1], writes=[sm])
            LT = LT_r.next()
            RT = RT_r.next()
            k.V(lambda: nc.vector.tensor_copy(out=LT.t[32:48, :], in_=p2.t[32:48, 0:128]), reads=[p2], writes=[LT])
            k.V(lambda: nc.vector.tensor_tensor(out=RT.t[0:16, :, :], in0=bc(p2.t[0:16, 0:128].unsqueeze(1), [16, 16, 128]),
                                                in1=DEL.t[:], op=ALU.mult), reads=[p2, DEL], writes=[RT])
            p3 = k.psf.next()
            k.mm(p3, [(SEL.t[:], sm.t[:, 0:16])], reads=[SEL, sm], out_ap=p3.t[:, 0:16])
            k.A(lambda: nc.scalar.activation(out=sm.t[:, 16:32], in_=sm.t[:, 0:16], func=AF.Exp), reads=[sm], writes=[sm])
            k.V(lambda: nc.vector.tensor_tensor(out=sm.t[:, 32:48], in0=p3.t[:, 0:16], in1=sm.t[:, 0:16], op=ALU.subtract),
                reads=[p3, sm], writes=[sm])
            k.A(lambda: nc.scalar.activation(out=sm.t[:, 32:48], in_=sm.t[:, 32:48], func=AF.Exp), reads=[sm], writes=[sm])
            k.A(lambda: nc.scalar.activation(out=sm.t[:, 48:64], in_=p3.t[:, 0:16], func=AF.Exp), reads=[p3, sm], writes=[sm])
            k.V(lambda: nc.vector.tensor_tensor(out=sm.t[:, 32:48], in0=sm.t[:, 32:48], in1=dts.t[:, 0:16], op=ALU.mult),
                reads=[sm, dts], writes=[sm])
            prev_gate = pending_gate
            if prev_gate is not None:
                prev_gate[0]()
            pb = k.psb.next()

            def trx():
                for m in range(8):
                    ins = nc.tensor.transpose(pb.t[:, m * 128:(m + 1) * 128], XC.t[:, m, cs], k.ident.t[:])
                return ins
            k.P(trx, reads=[XC, k.ident], writes=[pb])
            xtok = xtok_r.next()
            k.A(lambda: nc.scalar.copy(out=xtok.t[:, 0, :], in_=pb.t[:]), reads=[pb], writes=[xtok])
            k.V(lambda: nc.vector.tensor_tensor(out=xtok.t[:, 1, :].rearrange("p (h d) -> p h d", d=64),
                                                in0=pb.t[:].rearrange("p (h d) -> p h d", d=64),
                                                in1=bc(dts.t[:, 0:16].unsqueeze(2), [128, 16, 64]), op=ALU.mult),
                reads=[pb, dts], writes=[xtok])
            k.V(lambda: nc.vector.tensor_tensor(out=xtok.t[:, 2, :].rearrange("p (h d) -> p h d", d=64),
                                                in0=pb.t[:].rearrange("p (h d) -> p h d", d=64),
                                                in1=bc(sm.t[:, 32:48].unsqueeze(2), [128, 16, 64]), op=ALU.mult),
                reads=[pb, sm], writes=[xtok])
            pb2 = k.psb.next()

            def trb():
                for g in range(2):
                    ins = nc.tensor.transpose(pb2.t[:, g * 128:(g + 1) * 128], XC.t[:, 8 + g, cs], k.ident.t[:])
                return ins
            k.P(trb, reads=[XC, k.ident], writes=[pb2])
            btok = bt_r.next()
            k.A(lambda: nc.scalar.copy(out=btok.t[:], in_=pb2.t[:, 0:256]), reads=[pb2], writes=[btok])
            p4 = k.psf.next()

            def cbf():
                for g in range(2):
                    ins = nc.tensor.matmul(p4.t[:, g * 128:(g + 1) * 128], XC.t[:, 8 + g, cs], XC.t[:, 10 + g, cs],
                                           start=True, stop=True)
                return ins
            k.P(cbf, reads=[XC], writes=[p4])
            cbT = cbT_r.next()
            k.V(lambda: nc.vector.tensor_copy(out=cbT.t[:].rearrange("p g l -> p (g l)"), in_=p4.t[:, 0:256]),
                reads=[p4], writes=[cbT])
            yn_prev = prev_gate[1]() if prev_gate is not None else None
            MT = MT_r.next()
            for hq in range(4):
                g = hq // 2
                pg = k.psf.next()
                k.mm(pg, [(LT.t[:, :], RT.t[:, 4 * hq:4 * hq + 4, :].rearrange("p h l -> p (h l)")),
                          (k.ident.t[:], NEGM4b.t[:])], reads=[LT, RT, k.ident, NEGM4b])
                lm = lm_r.next()
                k.A(lambda: nc.scalar.activation(out=lm.t[:].rearrange("p h l -> p (h l)"), in_=pg.t[:], func=AF.Exp),
                    reads=[pg], writes=[lm])
                k.V(lambda: nc.vector.tensor_tensor(out=MT.t[:, 4 * hq:4 * hq + 4, :], in0=lm.t[:],
                                                    in1=bc(cbT.t[:, g, :].unsqueeze(1), [128, 4, 128]), op=ALU.mult),
                    reads=[lm, cbT], writes=[MT])
            if prev_gate is not None:
                prev_gate[2](yn_prev)
            yd = [k.psf.next(), k.psf.next()]
            for hb in range(2):
                def ydf():
                    for hh in range(8):
                        h = hb * 8 + hh
                        nc.tensor.matmul(yd[hb].t[:, hh * 64:(hh + 1) * 64], MT.t[:, h, :], xtok.t[:, 1, h * 64:(h + 1) * 64],
                                         start=True, stop=False)
                        ins = nc.tensor.matmul(yd[hb].t[:, hh * 64:(hh + 1) * 64], DIAGD.t[:, h, :],
                                               xtok.t[:, 0, h * 64:(h + 1) * 64], start=False, stop=True)
                    return ins
                k.P(ydf, reads=[MT, xtok, DIAGD], writes=[yd[hb]])
            yo = [k.psf.next(), k.psf.next()]
            for g in range(2):
                k.mm(yo[g], [(XC.t[:, 10 + g, cs], prevbf.t[:, g * 512:(g + 1) * 512])], reads=[XC, prevbf])
            yf = yf_r.next()
            for g in range(2):
                gs = slice(g * 512, (g + 1) * 512)
                k.V(lambda: nc.vector.tensor_tensor(out=yf.t[:, gs].rearrange("p (h d) -> p h d", d=64),
                                                    in0=yo[g].t[:].rearrange("p (h d) -> p h d", d=64),
                                                    in1=bc(sm.t[:, 16 + 8 * g:24 + 8 * g].unsqueeze(2), [128, 8, 64]), op=ALU.mult),
                    reads=[yo[g], sm], writes=[yf])
                k.V(lambda: nc.vector.tensor_tensor(out=yf.t[:, gs], in0=yd[g].t[:], in1=yf.t[:, gs], op=ALU.add),
                    reads=[yd[g], yf], writes=[yf])
            sn = [k.psf.next(), k.psf.next()]
            for g in range(2):
                k.mm(sn[g], [(btok.t[:, g * 128:(g + 1) * 128], xtok.t[:, 2, g * 512:(g + 1) * 512])], reads=[btok, xtok])
            k.G(lambda: nc.gpsimd.tensor_tensor(out=state.t[:].rearrange("p (h d) -> p h d", d=64),
                                                in0=state.t[:].rearrange("p (h d) -> p h d", d=64),
                                                in1=bc(sm.t[:, 48:64].unsqueeze(2), [128, 16, 64]), op=ALU.mult),
                reads=[state, sm], writes=[state])
            for g in range(2):
                gs = slice(g * 512, (g + 1) * 512)
                k.V(lambda: nc.vector.tensor_tensor(out=state.t[:, gs], in0=sn[g].t[:], in1=state.t[:, gs], op=ALU.add),
                    reads=[sn[g], state], writes=[state])
            k.A(lambda: nc.scalar.copy(out=prevbf.t[:], in_=state.t[:]), reads=[state], writes=[prevbf])
            zs = zs_r.next()
            k.A(lambda: nc.scalar.activation(out=zs.t[:], in_=zc.t[:], func=AF.Silu), reads=[zc], writes=[zs])
            k.G(lambda: nc.gpsimd.tensor_tensor(out=zs.t[:], in0=zs.t[:], in1=yf.t[:], op=ALU.mult), reads=[zs, yf], writes=[zs])
            def gate1(zs=zs, sm=sm):
                k.A(lambda: nc.scalar.activation(out=junk.t[:], in_=zs.t[:], func=AF.Square, accum_out=sm.t[:, 0:1]),
                    reads=[zs, sm], writes=[junk, sm])
                k.rsqrt(sm, sm, 1.0 / 1024, (slice(None), slice(0, 1)))

            def gate2(zs=zs, sm=sm):
                yn = yn_r.next()
                k.V(lambda: nc.vector.tensor_scalar(out=yn.t[:], in0=zs.t[:], scalar1=sm.t[:, 0:1], scalar2=None, op0=ALU.mult),
                    reads=[zs, sm], writes=[yn])
                return yn

            def gate3(yn, ynT=ynT, cs=cs, dp=(cc == 3), GM=GM, t0=t0):
                tail(yn, ynT, cs, dp, GM, t0)
            pending_gate = [gate1, gate2, gate3]
    pending_gate[0]()
    pending_gate[2](pending_gate[1]())


def phase_m(k):
    nc = k.nc
    mem = k.din("mem", [256, D])
    mix3T_s = k.dscr("mix3T_s", [8, 128, S], BF16)
    qmT_s, gmT_s = k.dram["qmT_s"], k.dram["gmT_s"]
    WKV = k.sb("WKV", [128, 8, 1024], BF16)
    wload(k, WKV, "w_mem_kv", "(k p) n -> p k n", p=128)
    WMO = k.sb("WMO", [128, 4, 1024], BF16)
    wload(k, WMO, "w_mem_o", "(k p) n -> p k n", p=128)
    memnT = k.sb("memnT", [128, 8, 256], BF16)
    kmT = k.sb("kmT", [128, 4, 256], BF16)
    VM = k.sb("VM", [128, 2, 512], BF16)
    onesc = k.sb("onesc", [128, 1], BF16)
    k.V(lambda: nc.vector.memset(onesc.t[:], 1.0), writes=[onesc])
    mt_r = k.ring("memt", [128, D], F32, 2)
    ms_r = k.ring("mems", [128, D], BF16, 2)
    ss_r = k.ring("mss", [128, 8], F32, 2)
    sq_r = k.ring("msq", [128, 512], F32, 1)
    kn_r = k.ring("mkn", [128, 512], BF16, 1)
    junk = k.sb("junkm", [128, D], BF16)
    for mt in range(2):
        xt = mt_r.next(); k.load(xt, mem[mt * 128:(mt + 1) * 128, :], "")
        ss = ss_r.next()
        k.A(lambda: nc.scalar.activation(out=junk.t[:], in_=xt.t[:], func=AF.Square, accum_out=ss.t[:, 0:1]),
            reads=[xt], writes=[junk, ss])
        k.rsqrt(ss, ss, 1.0 / D, (slice(None), slice(0, 1)))
        xs = ms_r.next()
        k.V(lambda: nc.vector.tensor_scalar(out=xs.t[:], in0=xt.t[:], scalar1=ss.t[:, 0:1], scalar2=None, op0=ALU.mult),
            reads=[xt, ss], writes=[xs])
        pt = transpose_to(k, None, None, xs, 8)
        k.V(lambda: nc.vector.tensor_tensor(out=memnT.t[:, :, mt * 128:(mt + 1) * 128],
                                            in0=pt.t[:].rearrange("p (k t) -> p k t", t=128),
                                            in1=bc(col(k, "g_memn", 0, 8).unsqueeze(2), [128, 8, 128]), op=ALU.mult),
            reads=[pt, k.cols], writes=[memnT])
    for mt in range(2):
        pt = k.psf.next()
        k.mm(pt, [(memnT.t[:, kk, mt * 128:(mt + 1) * 128], WKV.t[:, kk, 0:512]) for kk in range(8)], reads=[memnT, WKV])
        sq = sq_r.next()
        k.A(lambda: nc.scalar.activation(out=sq.t[:], in_=pt.t[:], func=AF.Square), reads=[pt], writes=[sq])
        ss = ss_r.next()
        k.V(lambda: nc.vector.tensor_reduce(out=ss.t[:, 0:4], in_=sq.t[:].rearrange("p (h d) -> p h d", d=128), axis=AX.X,
                                            op=ALU.add), reads=[sq], writes=[ss])
        k.rsqrt(ss, ss, 1.0 / 128, (slice(None), slice(0, 4)))
        kn = kn_r.next()
        k.V(lambda: nc.vector.tensor_tensor(out=kn.t[:].rearrange("p (h d) -> p h d", d=128),
                                            in0=pt.t[:].rearrange("p (h d) -> p h d", d=128),
                                            in1=bc(ss.t[:, 0:4].unsqueeze(2), [128, 4, 128]), op=ALU.mult),
            reads=[pt, ss], writes=[kn])
        ptb = transpose_to(k, None, None, kn, 4)
        k.V(lambda: nc.vector.tensor_scalar(out=kmT.t[:, :, mt * 128:(mt + 1) * 128],
                                            in0=ptb.t[:, 0:512].rearrange("p (h t) -> p h t", t=128),
                                            scalar1=col(k, "g_mk"), scalar2=None, op0=ALU.mult),
            reads=[ptb, k.cols], writes=[kmT])
        pt = k.psf.next()
        k.mm(pt, [(memnT.t[:, kk, mt * 128:(mt + 1) * 128], WKV.t[:, kk, 512:1024]) for kk in range(8)], reads=[memnT, WKV])
        k.A(lambda: nc.scalar.copy(out=VM.t[:, mt, :], in_=pt.t[:]), reads=[pt], writes=[VM])

    if getattr(k, "WUP", None) is not None:
        issue_wup(k)
    qm_r = k.ring("qmb", [128, 4, 512], BF16, 2)
    GM_r = k.ring("GMm", [128, 8, 512], BF16, 3)
    PT_r = k.ring("PTm", [128, 2, 512], BF16, 4)
    om_r = k.ring("omem", [128, 4, 512], BF16, 2)
    omT_r = k.ring("omT", [128, 4, 512], BF16, 2)
    rd_r = k.ring("rdm", [128, 16], F32, 4)
    mo_r = k.ring("mom", [128, 512], BF16, 2)
    sc = 128.0 ** -0.5

    def loads(tb):
        qm = qm_r.next(); k.load(qm, qmT_s[:, :, tb * 512:(tb + 1) * 512].rearrange("h p t -> p h t"), "")
        GM = GM_r.next(); k.load(GM, gmT_s[16:24, :, tb * 512:(tb + 1) * 512].rearrange("m p t -> p m t"), "")
        return qm, GM

    def tail(om, GM, t0):
        omT = omT_r.next()
        for tt in range(4):
            pb = k.psb.next()

            def tro():
                for j in range(4):
                    ins = nc.tensor.transpose(pb.t[:, j * 128:(j + 1) * 128], om.t[:, tt, j * 128:(j + 1) * 128], k.ident.t[:])
                return ins
            k.P(tro, reads=[om, k.ident], writes=[pb])
            k.A(lambda: nc.scalar.copy(out=omT.t[:, :, tt * 128:(tt + 1) * 128],
                                       in_=pb.t[:, 0:512].rearrange("p (k t) -> p k t", t=128)), reads=[pb], writes=[omT])
        for e in range(8):
            pt = k.psf.next()
            k.mm(pt, [(WMO.t[:, kk, e * 128:(e + 1) * 128], omT.t[:, kk, :]) for kk in range(4)], reads=[WMO, omT])
            mo = mo_r.next()
            k.V(lambda: nc.vector.tensor_tensor(out=mo.t[:], in0=pt.t[:], in1=GM.t[:, e, :], op=ALU.mult),
                reads=[pt, GM], writes=[mo])
            k.store(mix3T_s[e, :, t0:t0 + 512], "", mo)

    nxt = loads(0)
    pending = None
    for tb in range(8):
        t0 = tb * 512
        qm, GM = nxt
        if tb + 1 < 8:
            nxt = loads(tb + 1)
        om = om_r.next()
        PTs = []
        for h in range(4):
            PT = PT_r.next()
            for mt in range(2):
                pt = k.psf.next()
                k.mm(pt, [(kmT.t[:, h, mt * 128:(mt + 1) * 128], qm.t[:, h, :])], reads=[kmT, qm])
                k.A(lambda: nc.scalar.activation(out=PT.t[:, mt, :], in_=pt.t[:], func=AF.Exp, scale=sc), reads=[pt], writes=[PT])
            PTs.append(PT)
        if pending is not None:
            pending()
        for h in (1, 3):
            for tt in range(4):
                po = k.psf.next()
                rd = rd_r.next()

                def pv():
                    for hh in (h - 1, h):
                        P_ = PTs[hh]
                        for mt in range(2):
                            nc.tensor.matmul(po.t[:, (hh % 2) * 128:(hh % 2 + 1) * 128], P_.t[:, mt, tt * 128:(tt + 1) * 128],
                                             VM.t[:, mt, hh * 128:(hh + 1) * 128], start=(mt == 0), stop=(mt == 1))
                        for mt in range(2):
                            ins = nc.tensor.matmul(po.t[:, 256 + (hh % 2):257 + (hh % 2)], P_.t[:, mt, tt * 128:(tt + 1) * 128],
                                                   onesc.t[:], start=(mt == 0), stop=(mt == 1))
                    return ins
                k.P(pv, reads=[PTs[h - 1], PTs[h], VM, onesc], writes=[po])
                k.V(lambda: nc.vector.reciprocal(out=rd.t[:, 0:2], in_=po.t[:, 256:258]), reads=[po], writes=[rd])
                for j_, hh_ in enumerate((h - 1, h)):
                    k.A(lambda: nc.scalar.activation(out=om.t[:, tt, hh_ * 128:(hh_ + 1) * 128],
                                                     in_=po.t[:, j_ * 128:(j_ + 1) * 128], func=AF.Copy,
                                                     scale=rd.t[:, j_:j_ + 1]), reads=[po, rd], writes=[om])
        pending = (lambda om=om, GM=GM, t0=t0: tail(om, GM, t0))
    pending()


def phase_n1(k):
    nc = k.nc
    kcT_s = k.dscr("kcT_s", [2, 64, 256], BF16)
    vc_s = k.dscr("vc_s", [2, 2, 128, 64], BF16)
    kvcT_s = k.dram["kvcT_s"]
    W1s = k.sb("W1s", [64, 2, 32, 256], BF16)
    for kv in range(2):
        k.dma(W1s.t[:, kv, :, :], k.dram["cmp_w1_bf"][kv * 2048:(kv + 1) * 2048, :].rearrange("(pos d) hid -> d pos hid", d=64),
              reads=[k.dbuf["cmp_w1_bf"]], writes=[W1s], acc=True)
    W2s = k.sb("W2s", [128, 2, 2, 64], BF16)
    for kv in range(2):
        k.dma(W2s.t[:, kv, :, :], k.dram["cmp_w2_bf"][kv * 256:(kv + 1) * 256, :].rearrange("(mt p) d -> p mt d", p=128),
              reads=[k.dbuf["cmp_w2_bf"]], writes=[W2s], acc=True)
    PE_ = k.sb("peTs", [64, 64], BF16)
    k.dma(PE_.t[:], k.dram["c_peT_bf"], reads=[k.dbuf["c_peT_bf"]], writes=[PE_])
    XCm = k.sb("XCm", [64, 2, 2, S], BF16)
    for kv in range(2):
        for g in range(2):
            k.dma(XCm.t[:, kv, g, :], kvcT_s[kv, g * 64:(g + 1) * 64, :], writes=[XCm], acc=True)
    cb1 = k.sb("cb1", [128, 4], F32)
    HID = k.sb("HID", [128, 2, 256], BF16)
    kc = k.sb("kc", [128, 128], BF16)
    k.V(lambda: nc.vector.memset(kc.t[:], 0.0), writes=[kc])
    ss = k.sb("ssn1", [128, 8], F32)
    junk = k.sb("junkn1", [128, 64], F32)
    kcT = k.sb("kcTt", [64, 128], BF16)
    vct = k.sb("vct", [128, 64], BF16)
    for kv in range(2):
        for mt in range(2):
            pt = k.psf.next()
            k.mm(pt, [(W1s.t[:, kv, pos, mt * 128:(mt + 1) * 128], PE_.t[:, kv * 32 + pos:kv * 32 + pos + 1]) for pos in range(32)],
                 reads=[W1s, PE_], out_ap=pt.t[:, 0:1])
            k.V(lambda: nc.vector.tensor_copy(out=cb1.t[:, kv * 2 + mt:kv * 2 + mt + 1], in_=pt.t[:, 0:1]), reads=[pt], writes=[cb1])
    for kv in range(2):
        for g in range(2):
            for mt in range(2):
                pt = k.psf.next()
                k.mm(pt, [(W1s.t[:, kv, pos, mt * 128:(mt + 1) * 128], XCm.t[:, kv, g, pos:pos + 16 * 254 + 1:16]) for pos in range(32)],
                     reads=[W1s, XCm], out_ap=pt.t[:, 0:255])
                k.A(lambda: nc.scalar.activation(out=HID.t[:, mt, 0:255], in_=pt.t[:, 0:255], func=AF.Silu,
                                                 bias=cb1.t[:, kv * 2 + mt:kv * 2 + mt + 1]), reads=[pt, cb1], writes=[HID])
            for u in range(2):
                nn = 128 if u == 0 else 127
                pt = k.psf.next()
                k.mm(pt, [(HID.t[:, mt, u * 128:u * 128 + nn], W2s.t[:, kv, mt, :]) for mt in range(2)], reads=[HID, W2s],
                     out_ap=pt.t[0:nn, 0:64])
                if kv == 0:
                    k.A(lambda: nc.scalar.activation(out=junk.t[0:nn, :], in_=pt.t[0:nn, 0:64], func=AF.Square,
                                                     accum_out=ss.t[0:nn, 0:1]), reads=[pt], writes=[junk, ss])
                    k.rsqrt(ss, ss, 1.0 / 64, (slice(0, nn), slice(0, 1)))
                    k.V(lambda: nc.vector.memset(kc.t[:], 0.0), writes=[kc])
                    k.V(lambda: nc.vector.tensor_scalar(out=kc.t[0:nn, 0:64], in0=pt.t[0:nn, 0:64], scalar1=ss.t[0:nn, 0:1],
                                                        scalar2=None, op0=ALU.mult), reads=[pt, ss], writes=[kc])
                    pb = k.psb.next()
                    k.P(lambda: nc.tensor.transpose(pb.t[:, 0:128], kc.t[:], k.ident.t[:]), reads=[kc, k.ident], writes=[pb])
                    k.V(lambda: nc.vector.tensor_scalar(out=kcT.t[:], in0=pb.t[0:64, 0:128], scalar1=col(k, "g_kc")[0:64, :],
                                                        scalar2=None, op0=ALU.mult), reads=[pb, k.cols], writes=[kcT])
                    k.store(kcT_s[g, :, u * 128:(u + 1) * 128], "", kcT)
                else:
                    k.V(lambda: nc.vector.memset(vct.t[:], 0.0), writes=[vct])
                    k.V(lambda: nc.vector.tensor_copy(out=vct.t[0:nn, :], in_=pt.t[0:nn, 0:64]), reads=[pt], writes=[vct])
                    k.store(vc_s[g, u], "", vct)


def phase_n2(k):
    nc = k.nc
    for nm, shp in (("c_d0", [128, 128]), ("c_mw1", [128, 128]), ("c_mw2", [128, 128])):
        k.din(nm, shp)
    mix2T_s = k.dscr("mix2T_s", [8, 128, S], BF16)
    onsa_s = k.dscr("onsa_s", [S, 512], F32) if k.debug else None
    qn_dbg = k.dscr("qn_dbg", [32, 2, 64, 512], BF16) if k.debug else None
    dr = k.dram
    qT8 = dr["qT_s"].rearrange("j (r d) t -> (j r) d t", d=64)
    KE, KW = [], []
    for g in range(2):
        ke = k.sb(f"KE{g}", [128, S], BF16)
        k.dma(ke.t[0:64, :], dr["kTs_s"][g * 64:(g + 1) * 64, :], writes=[ke], acc=True)
        k.dma(ke.t[64:128, :], dr["c_kside_bf"][0:64, :], reads=[k.dbuf["c_kside_bf"]], writes=[ke], acc=True)
        KE.append(ke)
        kw = k.sb(f"KW{g}", [128, S], BF16)
        k.dma(kw.t[0:64, :], dr["kTw_s"][g * 64:(g + 1) * 64, :], writes=[kw], acc=True)
        k.dma(kw.t[64:128, :], dr["c_kside_bf"][64:128, :], reads=[k.dbuf["c_kside_bf"]], writes=[kw], acc=True)
        KW.append(kw)
    EB = k.sb("EB", [128, 256], BF16)
    k.dma(EB.t[64:128, :], dr["c_eblk_bf"][:, 3840:4096], reads=[k.dbuf["c_eblk_bf"]], writes=[EB])
    RAQ = dr["c_raq_bf"]
    VV = k.sb("VV", [128, 32, 256], BF16)
    k.dma(VV.t[:], dr["v_s"].rearrange("(kt p) c -> p kt c", p=128), writes=[VV])
    VS = k.sb("VS", [128, 32, 2, 65], BF16)
    VW = k.sb("VW", [128, 32, 2, 65], BF16)
    for br, vt in ((0, VS), (1, VW)):
        k.V(lambda: nc.vector.memset(vt.t[:, :, :, 64:65], 1.0), writes=[vt])
        k.V(lambda: nc.vector.tensor_copy(out=vt.t[:, :, :, 0:64],
                                          in_=VV.t[:, :, br * 128:(br + 1) * 128].rearrange("p k (g d) -> p k g d", d=64)),
            reads=[VV], writes=[vt])
    KC = k.sb("KC", [64, 2, 256], BF16)
    k.dma(KC.t[:], dr["kcT_s"].rearrange("g d n -> d g n"), writes=[KC])
    VC = k.sb("VC", [128, 2, 2, 65], BF16)
    k.V(lambda: nc.vector.memset(VC.t[:, :, :, 64:65], 1.0), writes=[VC])
    for g in range(2):
        k.dma(VC.t[:, :, g, 0:64], dr["vc_s"][g].rearrange("u p d -> p u d"), writes=[VC], acc=True)
    OV = k.sb("OV", [128, 2, 64], BF16)
    k.dma(OV.t[:], dr["c_ovl_bf"].rearrange("p (u j) -> p u j", j=64), reads=[k.dbuf["c_ovl_bf"]], writes=[OV])
    LA = k.sb("LA", [3, S], BF16); k.dma(LA.t[:], dr["c_laall_bf"], reads=[k.dbuf["c_laall_bf"]], writes=[LA])
    RAC = k.sb("RAC", [3, 1024], BF16); k.dma(RAC.t[:], dr["c_rac_bf"], reads=[k.dbuf["c_rac_bf"]], writes=[RAC])
    SHM = k.sb("SHM", [16, S], BF16); k.dma(SHM.t[:], dr["c_shm_bf"], reads=[k.dbuf["c_shm_bf"]], writes=[SHM])
    CMASK = k.sb("CMASK", [16, 512], BF16); k.dma(CMASK.t[:], dr["c_cmask_bf"], reads=[k.dbuf["c_cmask_bf"]], writes=[CMASK])
    CAUS = k.sb("CAUS", [128, 512], BF16); k.dma(CAUS.t[:], dr["c_caus_bf"], reads=[k.dbuf["c_caus_bf"]], writes=[CAUS])
    WINM = k.sb("WINM", [128, 512], BF16); k.dma(WINM.t[:], dr["c_winm_bf"], reads=[k.dbuf["c_winm_bf"]], writes=[WINM])
    D0 = k.sb("D0", [128, 128], F32); k.load(D0, dr["c_d0"], "")
    MW1 = k.sb("MW1", [128, 128], F32); k.load(MW1, dr["c_mw1"], "")
    MW2 = k.sb("MW2", [128, 128], F32); k.load(MW2, dr["c_mw2"], "")
    WNO = k.sb("WNO", [128, 4, 1024], BF16)
    wload(k, WNO, "w_nsa_o", "(k p) n -> p k n", p=128)
    ZR = k.sb("ZR", [1, 512], BF16)
    k.V(lambda: nc.vector.memset(ZR.t[:], 0.0), writes=[ZR])
    SELT_r = k.ring("SELT", [128, 128], BF16, 3)
    for t_ in SELT_r.tiles:
        k.V(lambda: nc.vector.memset(t_.t[:], 0.0), writes=[t_])

    ACC = [k.psf.tiles[4], k.psf.tiles[5]]
    GM_r = k.ring("GMn", [128, 8, 512], BF16, 2)
    gn_r = k.ring("gnn", [128, 24], F32, 4)
    oacc_r = k.ring("oacc", [128, 512], F32, 4)
    otmp_r = k.ring("otmp", [128, 256], F32, 2)
    QN_r = k.ring("QN", [128, 512], BF16, 4)
    PC_r = k.ring("PC", [128, 512], BF16, 4)
    PS_r = k.ring("PS", [128, 512], BF16, 4)
    st_r = k.ring("stn", [128, 32], F32, 3)
    stB_r = k.ring("stb", [128, 32], F32, 2)
    imp_r = k.ring("imp", [128, 192], F32, 3)
    onb_r = k.ring("onb", [128, 512], BF16, 1)
    mo_r = k.ring("mon", [128, 512], BF16, 2)

    def evac_branch(po, st, gnt, br, g, oacc, first):
        pov = po.t[:, 0:260].rearrange("p (h e) -> p h e", e=65)
        k.V(lambda: nc.vector.tensor_scalar(out=st.t[:, 0:4], in0=pov[:, :, 64], scalar1=1e-30, scalar2=None, op0=ALU.add),
            reads=[po], writes=[st])
        yield
        k.V(lambda: nc.vector.reciprocal(out=st.t[:, 0:4], in_=st.t[:, 0:4]), reads=[st], writes=[st])
        yield
        k.V(lambda: nc.vector.tensor_tensor(out=st.t[:, 4:8], in0=st.t[:, 0:4], in1=gnt.t[:, br * 8 + g * 4:br * 8 + g * 4 + 4],
                                            op=ALU.mult), reads=[st, gnt], writes=[st])
        yield
        dst = oacc.t[:, g * 256:(g + 1) * 256].rearrange("p (h d) -> p h d", d=64)
        if first:
            k.V(lambda: nc.vector.tensor_tensor(out=dst, in0=pov[:, :, 0:64], in1=bc(st.t[:, 4:8].unsqueeze(2), [128, 4, 64]),
                                                op=ALU.mult), reads=[po, st], writes=[oacc])
        else:
            ot = otmp_r.next()
            k.V(lambda: nc.vector.tensor_tensor(out=ot.t[:].rearrange("p (h d) -> p h d", d=64), in0=pov[:, :, 0:64],
                                                in1=bc(st.t[:, 4:8].unsqueeze(2), [128, 4, 64]), op=ALU.mult),
                reads=[po, st], writes=[ot])
            yield
            k.V(lambda: nc.vector.tensor_tensor(out=oacc.t[:, g * 256:(g + 1) * 256], in0=oacc.t[:, g * 256:(g + 1) * 256],
                                                in1=ot.t[:], op=ALU.add), reads=[ot, oacc], writes=[oacc])
        yield

    QN2_r = k.ring("QN2", [128, 512], BF16, 3)
    QN2s = {}
    psA = [k.psf.tiles[2], k.psf.tiles[3]]
    psB3 = Tl(k.psb.tiles[1].t[:].bitcast(F32), "psB3")
    psB3.b = k.psb.tiles[1].b
    psB = Ring(k.psf.tiles[0:2] + [psB3])
    psbT = Ring([k.psb.tiles[0]])

    def stageA(i, g, QN, oacc, gnt):
        tok0 = i * 128
        k.dma(QN.t[0:64, :].rearrange("p (h t) -> p h t", t=128),
              qT8[4 * g:4 * g + 4, :, tok0:tok0 + 128].rearrange("h d t -> d h t"), writes=[QN], acc=True)
        k.dma(QN.t[124:128, :], RAQ[(i * 2 + g) * 4:(i * 2 + g) * 4 + 4, :], reads=[k.dbuf["c_raq_bf"]], writes=[QN], acc=True)
        yield
        nu = 2 if i >= 16 else 1
        PCs, nns = [], []
        for u in range(nu):
            dc = i - 16 * u
            nn = min(128 if u == 0 else 127, 8 * dc + 7)
            nns.append(nn)
            pt = psA[u]
            k.mm(pt, [(KC.t[:, g, u * 128:u * 128 + nn], QN.t[0:64, :]),
                      (LA.t[:, dc * 128:dc * 128 + nn], RAC.t[:, g * 512:(g + 1) * 512]),
                      (SHM.t[:, dc * 128:dc * 128 + nn], CMASK.t[:, :])],
                 reads=[KC, QN, LA, RAC, SHM, CMASK], out_ap=pt.t[0:nn, :])
            yield
            PC = PC_r.next()
            k.A(lambda: nc.scalar.activation(out=PC.t[0:nn, :], in_=pt.t[0:nn, :], func=AF.Exp, scale=0.125),
                reads=[pt], writes=[PC])
            yield
            PCs.append(PC)
        for _ in range(4):
            yield
        po, pi = psA[0], psA[1]

        def pvc():
            for hl in range(4):
                for u in range(nu):
                    nc.tensor.matmul(po.t[:, hl * 65:(hl + 1) * 65], PCs[u].t[0:nns[u], hl * 128:(hl + 1) * 128],
                                     VC.t[0:nns[u], u, g, :], start=(u == 0), stop=(u == nu - 1))
                for u in range(nu):
                    ins = nc.tensor.matmul(pi.t[:, hl * 64:(hl + 1) * 64], PCs[u].t[0:nns[u], hl * 128:(hl + 1) * 128],
                                           OV.t[0:nns[u], u, :], start=(u == 0), stop=(u == nu - 1))
            return ins
        k.P(pvc, reads=PCs + [VC, OV], writes=[po, pi])
        yield
        st = st_r.next()
        for _ in evac_branch(po, st, gnt, 0, g, oacc, True):
            yield
        imp = imp_r.next()
        k.V(lambda: nc.vector.tensor_scalar(out=imp.t[:, 0:64], in0=pi.t[:, 0:64], scalar1=st.t[:, 0:1], scalar2=None,
                                            op0=ALU.mult), reads=[pi, st], writes=[imp])
        yield
        for hl in range(1, 4):
            k.V(lambda: nc.vector.scalar_tensor_tensor(out=imp.t[:, 0:64], in0=pi.t[:, hl * 64:(hl + 1) * 64],
                                                       scalar=st.t[:, hl:hl + 1], in1=imp.t[:, 0:64],
                                                       op0=ALU.mult, op1=ALU.add), reads=[pi, st, imp], writes=[imp])
            yield
        c0 = 64 - 2 * i
        k.V(lambda: nc.vector.tensor_tensor(out=imp.t[:, 64:128], in0=imp.t[:, 0:64], in1=MW1.t[:, c0:c0 + 64], op=ALU.mult),
            reads=[imp, MW1], writes=[imp])
        yield
        k.V(lambda: nc.vector.tensor_tensor(out=imp.t[:, 64:128], in0=imp.t[:, 64:128], in1=MW2.t[:, c0:c0 + 64], op=ALU.add),
            reads=[imp, MW2], writes=[imp])
        yield
        k.V(lambda: nc.vector.memset(imp.t[:, 64:65], 1e9), reads=[imp], writes=[imp])
        yield
        k.V(lambda: nc.vector.max(out=st.t[:, 16:24], in_=imp.t[:, 64:128]), reads=[imp], writes=[st])
        yield
        k.V(lambda: nc.vector.match_replace(out=imp.t[:, 128:192], in_to_replace=st.t[:, 16:24], in_values=imp.t[:, 64:128],
                                            imm_value=-3.0e38), reads=[imp, st], writes=[imp])
        yield
        k.V(lambda: nc.vector.max(out=st.t[:, 24:32], in_=imp.t[:, 128:192]), reads=[imp], writes=[st])
        yield
        k.V(lambda: nc.vector.tensor_reduce(out=st.t[:, 8:9], in_=st.t[:, 24:32], axis=AX.X, op=ALU.min), reads=[st], writes=[st])
        yield
        SELT = SELT_r.next()
        k.V(lambda: nc.vector.tensor_scalar(out=SELT.t[:, 64:128], in0=imp.t[:, 64:128], scalar1=st.t[:, 8:9],
                                            scalar2=8.0 * NEGB, op0=ALU.is_lt, op1=ALU.mult), reads=[imp, st], writes=[SELT])
        yield
        for _ in range(9):
            yield
        pb = psbT.next()
        k.P(lambda: nc.tensor.transpose(pb.t[:, 0:128], SELT.t[:], k.ident.t[:]), reads=[SELT, k.ident], writes=[pb])
        k.V(lambda: nc.vector.tensor_copy(out=QN.t[64:124, :].rearrange("p (h t) -> p h t", t=128),
                                          in_=bc(pb.t[64:124, 0:128].unsqueeze(1), [60, 4, 128])), reads=[pb], writes=[QN])
        if i >= 30:
            QN2 = QN2_r.next()
            k.V(lambda: nc.vector.tensor_copy(out=QN2.t[64:128, :].rearrange("p (h t) -> p h t", t=128),
                                              in_=bc(pb.t[64:128, 0:128].unsqueeze(1), [64, 4, 128])), reads=[pb], writes=[QN2])
            QN2s[(i, g)] = QN2
        yield

    def stageB(i, g, QN, oacc, gnt, filler, late=None):
        k0 = max(0, i - 4)
        steps = [(0, kt) for kt in range(i + 1)] + [(1, kt) for kt in range(k0, i + 1)]
        pend_pv = []
        nstep = 0
        late = late if late is not None else []
        for (br, kt) in steps:
            pt = psB.next()
            if br == 0:
                pairs = [(KE[g].t[:, kt * 128:(kt + 1) * 128], QN.t[:, :])]
                rd_ = [KE[g], QN]
                if kt >= 30:
                    QN2 = QN2s[(i, g)]
                    pairs.append((EB.t[64:128, (kt - 30) * 128:(kt - 29) * 128], QN2.t[64:128, :])); rd_ += [EB, QN2]
            else:
                pairs = [(KW[g].t[:, kt * 128:(kt + 1) * 128], QN.t[:, :])]
                rd_ = [KW[g], QN]
            if kt == i:
                pairs.append((k.ident.t[:], CAUS.t[:])); rd_ += [k.ident, CAUS]
            if br == 1 and kt == i - 4:
                pairs.append((k.ident.t[:], WINM.t[:])); rd_ += [k.ident, WINM]
            k.mm(pt, pairs, reads=rd_)
            PS = PS_r.next()
            k.A(lambda: nc.scalar.activation(out=PS.t[:], in_=pt.t[:], func=AF.Exp, scale=0.125), reads=[pt], writes=[PS])
            if len(pend_pv) >= 2:
                pend_pv.pop(0)()

            def pv(br=br, kt=kt, PS=PS):
                first = (kt == 0) if br == 0 else (kt == k0)
                Vt = VS if br == 0 else VW

                def f():
                    if first:
                        nc.tensor.matmul(ACC[br].t[:, 0:260], ZR.t[0:1, 0:128], ZR.t[0:1, 0:260], start=True, stop=False)
                    for hl in range(4):
                        ins = nc.tensor.matmul(ACC[br].t[:, hl * 65:(hl + 1) * 65], PS.t[:, hl * 128:(hl + 1) * 128],
                                               Vt.t[:, kt, g, :], start=False, stop=(kt == i))
                    return ins
                k.P(f, reads=[PS, Vt, ZR], writes=[ACC[br]])
            pend_pv.append(pv)
            nstep += 1
            if late and nstep == 4:
                late.pop(0)()
            if filler is not None:
                for _ in range(3):
                    filler()
        while pend_pv:
            pend_pv.pop(0)()
        while late:
            late.pop(0)()
        for br in range(2):
            stx = stB_r.next()
            for _ in evac_branch(ACC[br], stx, gnt, 1 + br, g, oacc, False):
                pass

    order = []
    lo_, hi_ = 0, NT - 1
    while lo_ <= hi_:
        order.append(hi_)
        if lo_ != hi_:
            order.append(lo_)
        hi_ -= 1
        lo_ += 1
    pairs_ig = [(i, g) for i in order for g in range(2)]
    ctx = {}

    def get_ctx(i):
        if i not in ctx:
            gnt = gn_r.next(); k.load(gnt, dr["gn_s"][i * 128:(i + 1) * 128, :], "")
            ctx[i] = (oacc_r.next(), gnt)
        return ctx[i]

    QNs = {}
    gens = {}

    def make_A(p):
        if p in gens or p >= len(pairs_ig):
            return
        i, g = pairs_ig[p]
        oacc, gnt = get_ctx(i)
        QNs[p] = QN_r.next()
        gens[p] = stageA(i, g, QNs[p], oacc, gnt)

    def chain(ps):
        def step():
            for p_ in ps:
                g_ = gens.get(p_)
                if g_ is None:
                    continue
                try:
                    next(g_)
                    return
                except StopIteration:
                    continue
        return step

    onTs = [k.sb(f"onT{tb}", [128, 4, 512], BF16) for tb in range(8)]
    done_cnt = [0] * 8
    make_A(0)
    for _ in gens[0]:
        pass
    late_fns = []
    for p, (i, g) in enumerate(pairs_ig):
        tb, tt = i // 4, i % 4
        t0 = tb * 512
        oacc, gnt = get_ctx(i)
        make_A(p + 1)
        make_A(p + 2)
        issue_late_conv(k, 1)
        stageB(i, g, QNs[p], oacc, gnt, chain([p + 1, p + 2]), late_fns)
        if p + 1 in gens:
            for _ in gens[p + 1]:
                pass
        if g == 1:
            def finalize(i=i, tb=tb, tt=tt, t0=t0, oacc=oacc):
                onT = onTs[tb]
                onb = onb_r.next()
                k.A(lambda: nc.scalar.copy(out=onb.t[:], in_=oacc.t[:]), reads=[oacc], writes=[onb])
                pb = transpose_to(k, None, None, onb, 4, ring=psbT)
                k.A(lambda: nc.scalar.copy(out=onT.t[:, :, tt * 128:(tt + 1) * 128],
                                           in_=pb.t[:, 0:512].rearrange("p (k t) -> p k t", t=128)), reads=[pb], writes=[onT])
                done_cnt[tb] += 1
                if done_cnt[tb] == 4:
                    GM = GM_r.next(); k.load(GM, dr["gmT_s"][8:16, :, t0:t0 + 512].rearrange("m p t -> p m t"), "")
                    for e in range(8):
                        pt = psB.next()
                        k.mm(pt, [(WNO.t[:, kk, e * 128:(e + 1) * 128], onT.t[:, kk, :]) for kk in range(4)], reads=[WNO, onT])
                        mo = mo_r.next()
                        k.V(lambda: nc.vector.tensor_tensor(out=mo.t[:], in0=pt.t[:], in1=GM.t[:, e, :], op=ALU.mult),
                            reads=[pt, GM], writes=[mo])
                        k.store(mix2T_s[e, :, t0:t0 + 512], "", mo)
            late_fns.append(finalize)
            del ctx[i]
    while late_fns:
        late_fns.pop(0)()
    issue_late_conv(k, 1000)


def phase_d1(k):
    nc = k.nc
    x = k.dram["x"]
    h_s = k.dscr("h_s", [S, D], F32)
    hnT_s = k.dscr("hnT_s", [8, 128, S], BF16)
    WOUT = k.sb("WOUT", [128, 8, D], BF16)
    wload(k, WOUT, "w_out", "(k p) n -> p k n", p=128)
    MA_r = k.ring("MxA", [128, 8, 512], BF16, 3)
    MB_r = k.ring("MxB", [128, 8, 512], BF16, 4)
    xt_r = k.ring("xtd", [128, D], F32, 2)
    ht_r = k.ring("htd", [128, D], F32, 2)
    hs_r = k.ring("hsd", [128, D], BF16, 3)
    hnT_r = k.ring("hnTd", [128, 8, 512], BF16, 2)
    ss_r = k.ring("ssd", [128, 8], F32, 2)
    junk = k.sb("junkd", [128, D], BF16)

    def load_mix(tb):
        Ms = []
        for j, nm in enumerate(("mix1T_s", "mix2T_s", "mix3T_s")):
            M = (MA_r if j == 0 else MB_r).next()
            k.load(M, k.dram[nm][:, :, tb * 512:(tb + 1) * 512].rearrange("m p t -> p m t"), "")
            Ms.append(M)
        return Ms
    def sum_mix(Ms):
        k.V(lambda: nc.vector.tensor_tensor(out=Ms[0].t[:], in0=Ms[0].t[:], in1=Ms[1].t[:], op=ALU.add), reads=[Ms[0], Ms[1]], writes=[Ms[0]])
        k.V(lambda: nc.vector.tensor_tensor(out=Ms[0].t[:], in0=Ms[0].t[:], in1=Ms[2].t[:], op=ALU.add), reads=[Ms[0], Ms[2]], writes=[Ms[0]])
        return Ms

    Ms_next = sum_mix(load_mix(0))
    Ms_ld = load_mix(1)
    pend = []
    gtt = [0]
    for tb in range(8):
        t0 = tb * 512
        Ms = Ms_next
        mixed = Ms[0]
        hnT = hnT_r.next()
        for tt in range(4):
            tok0 = t0 + tt * 128
            xt = xt_r.next(); k.load(xt, x[tok0:tok0 + 128, :], "")
            ht = ht_r.next()
            for nh in range(2):
                pt = k.psf.next()
                k.mm(pt, [(mixed.t[:, kk, tt * 128:(tt + 1) * 128], WOUT.t[:, kk, nh * 512:(nh + 1) * 512]) for kk in range(8)],
                     reads=[mixed, WOUT])
                k.V(lambda: nc.vector.tensor_tensor(out=ht.t[:, nh * 512:(nh + 1) * 512], in0=pt.t[:], in1=xt.t[:, nh * 512:(nh + 1) * 512],
                                                    op=ALU.add), reads=[pt, xt], writes=[ht])
            gtt[0] += 1
            while pend and pend[0][0] <= gtt[0]:
                pend.pop(0)[1]()
            if tt == 1 and tb + 1 < 8:
                Ms_next = sum_mix(Ms_ld)
            if tt == 2 and tb + 2 < 8:
                Ms_ld = load_mix(tb + 2)
            k.store(h_s[tok0:tok0 + 128, :], "", ht)
            ss = ss_r.next()
            k.A(lambda: nc.scalar.activation(out=junk.t[:], in_=ht.t[:], func=AF.Square, accum_out=ss.t[:, 0:1]),
                reads=[ht], writes=[junk, ss])
            k.rsqrt(ss, ss, 1.0 / D, (slice(None), slice(0, 1)))
            hs = hs_r.next()
            k.V(lambda: nc.vector.tensor_scalar(out=hs.t[:], in0=ht.t[:], scalar1=ss.t[:, 0:1], scalar2=None, op0=ALU.mult),
                reads=[ht, ss], writes=[hs])
            def tailf(hs=hs, hnT=hnT, tt=tt):
                pb = transpose_to(k, None, None, hs, 8)
                k.V(lambda: nc.vector.tensor_tensor(out=hnT.t[:, :, tt * 128:(tt + 1) * 128], in0=pb.t[:].rearrange("p (k t) -> p k t", t=128),
                                                    in1=bc(col(k, "g_ffn", 0, 8).unsqueeze(2), [128, 8, 128]), op=ALU.mult),
                    reads=[pb, k.cols], writes=[hnT])
            pend.append((gtt[0] + 2, tailf))
        pend.append((gtt[0] + 2, lambda hnT=hnT, t0=t0: k.store(hnT_s[:, :, t0:t0 + 512].rearrange("m p t -> p m t"), "", hnT)))
    while pend:
        pend.pop(0)[1]()


def prefetch_wup(k):
    k.WUP = k.sb("WUP", [128, 8, 2 * FFH], BF16)


def issue_wup(k):
    for kk in range(8):
        k.dma(k.WUP.t[:, kk, :], k.dram["w_ffn_up_bf"][kk * 128:(kk + 1) * 128, :], reads=[k.dbuf["w_ffn_up_bf"]],
              writes=[k.WUP], acc=True)


def phase_d2a(k):
    nc = k.nc
    aT_s = k.dscr("aT_s", [22, 128, S], BF16)
    hnT_s = k.dram["hnT_s"]
    WUP = k.WUP
    DGF = k.sb("DGF", [128, 88, 128], BF16)
    k.V(lambda: nc.vector.tensor_tensor(out=DGF.t[:], in0=bc(k.identf.t[:].unsqueeze(1), [128, 88, 128]),
                                        in1=bc(col(k, "cw_ffn", 0, 88).unsqueeze(2), [128, 88, 128]), op=ALU.mult),
        reads=[k.identf, k.cols], writes=[DGF])
    UH = k.sb("UH", [128, 44, 2], BF16)
    UHb = [Buf(f"UH{m}") for m in range(44)]
    k.V(lambda: nc.vector.memset(UH.t[:], 0.0), writes=UHb)
    hnT_r = k.ring("hnTf", [128, 8, 512], BF16, 2)
    u_r = k.ring("usb", [128, 512], BF16, 4)
    ga_r = k.ring("ga", [128, 512], BF16, 3)
    cv_r = k.ring("cv", [128, 512], F32, 3)
    AT_r = k.ring("AT", [128, 22, 512], BF16, 2)
    hnT_next = hnT_r.next(); k.load(hnT_next, hnT_s[:, :, 0:512].rearrange("m p t -> p m t"), "")
    for tb in range(8):
        t0 = tb * 512
        hnT = hnT_next
        if tb + 1 < 8:
            hnT_next = hnT_r.next(); k.load(hnT_next, hnT_s[:, :, t0 + 512:t0 + 1024].rearrange("m p t -> p m t"), "")
        AT = AT_r.next()
        order = [m for c in range(22) for m in (c, c + 22)]
        gas = {}
        pending = None
        for m in order:
            pt = k.psf.next()
            k.mm(pt, [(WUP.t[:, kk, m * 128:(m + 1) * 128], hnT.t[:, kk, :]) for kk in range(8)], reads=[WUP, hnT])
            u = u_r.next()
            if m < 22:
                k.A(lambda: nc.scalar.copy(out=u.t[:], in_=pt.t[:]), reads=[pt], writes=[u])
            else:
                k.V(lambda: nc.vector.tensor_copy(out=u.t[:], in_=pt.t[:]), reads=[pt], writes=[u])
            if pending is not None:
                pending()

            def conv(m=m, u=u):
                p2 = k.psf.next()

                def f():
                    d0, d1 = (DGF.t[:, kk * 44 + m, :] for kk in range(2))
                    nc.tensor.matmul(p2.t[:, 1:512], d1, u.t[:, 0:511], start=True, stop=False)
                    nc.tensor.matmul(p2.t[:, 0:1], d1, UH.t[:, m, 1:2], start=False, stop=False)
                    nc.tensor.matmul(p2.t[:, 2:512], d0, u.t[:, 0:510], start=False, stop=False)
                    return nc.tensor.matmul(p2.t[:, 0:2], d0, UH.t[:, m, 0:2], start=False, stop=True)
                k.P(f, reads=[DGF, u, UHb[m]], writes=[p2])
                k.G(lambda: nc.gpsimd.tensor_copy(out=UH.t[:, m, :], in_=u.t[:, 510:512]), reads=[u], writes=[UHb[m]])
                cv = cv_r.next()
                k.V(lambda: nc.vector.scalar_tensor_tensor(out=cv.t[:], in0=u.t[:], scalar=col(k, "cw_ffn", 2 * 44 + m),
                                                           in1=p2.t[:], op0=ALU.mult, op1=ALU.add),
                    reads=[u, k.cols, p2], writes=[cv])
                if m < 22:
                    ga = ga_r.next()
                    k.A(lambda: nc.scalar.activation(out=ga.t[:], in_=cv.t[:], func=AF.Silu, bias=col(k, "cb_ffn", m)),
                        reads=[cv, k.cols], writes=[ga])
                    gas[m] = ga
                else:
                    ga = gas.pop(m - 22)
                    k.V(lambda: nc.vector.scalar_tensor_tensor(out=AT.t[:, m - 22, :], in0=cv.t[:], scalar=col(k, "cb_ffn", m),
                                                               in1=ga.t[:], op0=ALU.add, op1=ALU.mult),
                        reads=[cv, k.cols, ga], writes=[AT])
            pending = conv
        pending()
        k.store(aT_s[:, :, t0:t0 + 512].rearrange("m p t -> p m t"), "", AT)


def phase_d2b(k):
    nc = k.nc
    out = k.dscr("out", [S, D], F32, out=True)
    aT_s, h_s = k.dram["aT_s"], k.dram["h_s"]
    WDN = k.sb("WDN", [128, 22, D], BF16)
    WDNb = [Buf("WDN0"), Buf("WDN1")]
    for nh in range(2):
        k.dma(WDN.t[:, :, nh * 512:(nh + 1) * 512],
              k.dram["w_ffn_down_bf"].rearrange("(c p) n -> p c n", p=128)[:, :, nh * 512:(nh + 1) * 512],
              reads=[k.dbuf["w_ffn_down_bf"]], writes=[WDNb[nh]])
    AT_r = k.ring("ATb", [128, 22, 512], BF16, 2)
    ht_r = k.ring("htb", [128, D], F32, 3)
    AT_next = AT_r.next(); k.load(AT_next, aT_s[:, :, 0:512].rearrange("m p t -> p m t"), "")
    for tb in range(8):
        t0 = tb * 512
        AT = AT_next
        if tb + 1 < 8:
            AT_next = AT_r.next(); k.load(AT_next, aT_s[:, :, t0 + 512:t0 + 1024].rearrange("m p t -> p m t"), "")
        for tt in range(4):
            tok0 = t0 + tt * 128
            ht = ht_r.next(); k.load(ht, h_s[tok0:tok0 + 128, :], "")
            for nh in range(2):
                pt = k.psf.next()
                k.mm(pt, [(AT.t[:, c, tt * 128:(tt + 1) * 128], WDN.t[:, c, nh * 512:(nh + 1) * 512]) for c in range(22)],
                     reads=[AT, WDNb[nh]])
                k.V(lambda: nc.vector.tensor_tensor(out=ht.t[:, nh * 512:(nh + 1) * 512], in0=pt.t[:], in1=ht.t[:, nh * 512:(nh + 1) * 512],
                                                    op=ALU.add), reads=[pt, ht], writes=[ht])
            k.store(out[tok0:tok0 + 128, :], "", ht)


def build_program(inputs, debug=False, phases="abcde", batch=0):
    p = {n: np.asarray(v) for n, v in inputs.items()}
    cols, rows = _mk_tables(p)
    consts = _consts()
    k = K(debug=debug)
    k.ncols, k.nrows = cols.shape[1], rows.shape[1]
    phase_setup(k)
    base = len(k._guards)
    if "a" in phases:
        phase_a(k)
        k.f.barrier()
        k.free_phase(base)
    k.store_on_pq = True
    if "b" in phases:
        phase_b(k)
        k.f.barrier()
        k.free_phase(base)
    if "n" in phases:
        phase_n1(k)
        k.f.barrier()
        k.free_phase(base)
        phase_n2(k)
        k.f.barrier()
        k.free_phase(base)
    base2 = base
    if "d" in phases:
        prefetch_wup(k)
        base2 = base + 1
    if "m" in phases:
        phase_m(k)
        k.f.barrier()
        k.free_phase(base2)
    if "d" in phases:
        if "m" not in phases:
            issue_wup(k)
        phase_d1(k)
        k.f.barrier()
        k.free_phase(base2)
        phase_d2a(k)
        k.f.barrier()
        k.free_phase(base)
        phase_d2b(k)
        k.f.barrier()
        k.free_phase(base)
        k.store_on_pq = False
    k.f.barrier()
    shared = {"cols": cols, "rows": rows, "w_in": np.ascontiguousarray(p["w_in"][0]),
              "w_ssd_o": np.ascontiguousarray(p["w_ssd_o"][0]),
              "w_nsa_o": np.ascontiguousarray(p["w_nsa_o"][0]),
              "w_out": np.ascontiguousarray(p["w_out"][0]), "w_ffn_up": np.ascontiguousarray(p["w_ffn_up"][0]),
              "w_ffn_down": np.ascontiguousarray(p["w_ffn_down"][0]),
              "w_mem_kv": np.ascontiguousarray(p["w_mem_kv"][0]), "w_mem_o": np.ascontiguousarray(p["w_mem_o"][0])}
    shared["cmp_w1"] = np.ascontiguousarray(p["nsa_cmp_w1"][0]).reshape(4096, 256)
    shared["cmp_w2"] = np.ascontiguousarray(p["nsa_cmp_w2"][0]).reshape(512, 64)
    shared["c_peT"] = np.ascontiguousarray(p["nsa_cmp_pe"][0].transpose(2, 0, 1).reshape(64, 64))
    shared.update(consts)
    k.shared = shared
    in_map = dict(shared)
    in_map["x"] = np.ascontiguousarray(p["x"][batch])
    in_map["mem"] = np.ascontiguousarray(p["mem"][batch])
    in_map = {n: v for n, v in in_map.items() if n in k.dram}
    return k, in_map


def kernel(**inputs):
    k, _ = build_program(inputs, debug=False, phases="abnmd", batch=0)
    x = np.asarray(inputs["x"], np.float32)
    mem = np.asarray(inputs["mem"], np.float32)
    in_maps = []
    for b in range(8):
        m = {n: v for n, v in k.shared.items() if n in k.dram}
        m["x"] = np.ascontiguousarray(x[b])
        m["mem"] = np.ascontiguousarray(mem[b])
        in_maps.append(m)
    res = run_bass_kernel_spmd(k.nc, in_maps, core_ids=list(range(8)))
    return np.stack([np.asarray(r["out"], np.float32) for r in res.results], axis=0)
```

```python
import numpy as np
import ml_dtypes
import concourse.bass as bass
import concourse.mybir as mybir
from concourse.bass_utils import run_bass_kernel_spmd

F32 = mybir.dt.float32
BF16 = mybir.dt.bfloat16
ALU = mybir.AluOpType
AF = mybir.ActivationFunctionType
AX = mybir.AxisListType

S = 4096
D = 1024
NT = 32
INW = 7464
O_Z, O_XBC, O_DT, O_QN, O_KV, O_GN, O_QM, O_GM = 0, 1024, 2560, 2576, 3088, 3856, 3880, 4392
FFH = 2816
EPS = 1e-6
NEGB = -30000.0


class Buf:
    __slots__ = ("name", "w", "r")

    def __init__(self, name=""):
        self.name = name
        self.w = []
        self.r = {}


class Eng:
    def __init__(self, fw, name, handle, is_dma=False, nsem=1):
        self.fw = fw
        self.name = name
        self.h = handle
        self.is_dma = is_dma
        self.sems = [fw.nc.alloc_semaphore(f"s_{name}_{i}") for i in range(nsem)]
        self.count = 0
        self.waited = {}

    def wait(self, ev):
        if ev is None:
            return
        sem, val = ev
        k = id(sem)
        if self.waited.get(k, 0) >= val:
            return
        self.waited[k] = val
        self.h.wait_ge(sem, val)

    def signal(self, inst):
        if self.is_dma:
            n = len(self.sems)
            i = self.count
            sem = self.sems[i % n]
            inst.then_inc(sem, 16)
            self.count += 1
            return (sem, 16 * (i // n + 1))
        self.count += 1
        inst.then_inc(self.sems[0], 1)
        return (self.sems[0], self.count)

    def pre_dma(self):
        n = len(self.sems)
        i = self.count
        if i >= n:
            self.wait((self.sems[i % n], 16 * (i // n)))


class FW:
    def __init__(self, nc):
        self.nc = nc
        self.pe = Eng(self, "pe", nc.tensor)
        self.dve = Eng(self, "dve", nc.vector)
        self.act = Eng(self, "act", nc.scalar)
        self.pool = Eng(self, "pool", nc.gpsimd)
        self.sp = Eng(self, "sp", nc.sync, is_dma=True, nsem=12)
        self.pq = Eng(self, "pq", nc.gpsimd, is_dma=True, nsem=8)
        self.pq.waited = self.pool.waited
        self.engs = [self.pe, self.dve, self.act, self.pool, self.sp, self.pq]

    def _deps(self, eng, reads, writes, acc=False):
        for b in reads:
            for ev in b.w:
                eng.wait(ev)
        for b in writes:
            if not acc:
                for ev in b.w:
                    eng.wait(ev)
            for ev in b.r.values():
                eng.wait(ev)

    def _commit(self, ev, reads, writes, acc=False):
        for b in reads:
            b.r[id(ev[0])] = ev
        for b in writes:
            b.w = (b.w + [ev]) if acc else [ev]
            b.r = {}

    def op(self, eng, fn, reads=(), writes=()):
        self._deps(eng, reads, writes)
        inst = fn()
        ev = eng.signal(inst)
        self._commit(ev, reads, writes)
        return ev

    def dma(self, q, out, in_, reads=(), writes=(), acc=False, inflight=None, **kw):
        q.pre_dma()
        if inflight is not None and q.count - inflight >= 0:
            j = q.count - inflight
            n_ = len(q.sems)
            q.wait((q.sems[j % n_], 16 * (j // n_ + 1)))
        self._deps(q, reads, writes, acc)
        inst = q.h.dma_start(out=out, in_=in_, **kw)
        ev = q.signal(inst)
        self._commit(ev, reads, writes, acc)
        return ev

    def barrier(self):
        evs = []
        for e in self.engs:
            if e.is_dma:
                n = len(e.sems)
                for j in range(min(n, e.count)):
                    last = ((e.count - 1 - j) // n) * n + j
                    evs.append((e.sems[j], 16 * (last // n + 1)))
            elif e.count > 0:
                evs.append((e.sems[0], e.count))
        for e in (self.pe, self.dve, self.act, self.pool, self.sp):
            for ev in evs:
                e.wait(ev)


class Tl:
    def __init__(self, t, name):
        self.t = t
        self.b = Buf(name)

    def __getitem__(self, idx):
        return self.t[idx]


class Ring:
    def __init__(self, tiles):
        self.tiles = tiles
        self.i = 0

    def next(self):
        t = self.tiles[self.i % len(self.tiles)]
        self.i += 1
        return t


class K:
    def __init__(self, debug=False):
        self.debug = debug
        self.nc = bass.Bass("TRN2", target_bir_lowering=False)
        self.f = FW(self.nc)
        self.dram = {}
        self.dbuf = {}
        self._n = 0
        self._guards = []

    def din(self, name, shape, dt=F32):
        self.dram[name] = self.nc.dram_tensor(name, list(shape), dt, kind="ExternalInput").ap()
        self.dbuf[name] = Buf(name)
        return self.dram[name]

    def dscr(self, name, shape, dt, out=False):
        kind = "ExternalOutput" if (out or self.debug) else "Internal"
        self.dram[name] = self.nc.dram_tensor(name, list(shape), dt, kind=kind).ap()
        self.dbuf[name] = Buf(name)
        return self.dram[name]

    def sb(self, name, shape, dt):
        self._n += 1
        g = self.nc.sbuf_tensor(f"{name}_{self._n}", list(shape), dt)
        t = g.__enter__()
        self._guards.append(g)
        return Tl(t, name)

    def free_phase(self, keep=0):
        while len(self._guards) > keep:
            self._guards.pop().__exit__(None, None, None)

    def ring(self, name, shape, dt, n):
        return Ring([self.sb(f"{name}{i}", shape, dt) for i in range(n)])

    def ps(self, name, shape, dt=F32):
        self._n += 1
        return Tl(self.nc.alloc_psum_tensor(f"{name}_{self._n}", list(shape), dt), name)

    def V(self, fn, reads=(), writes=()):
        return self.f.op(self.f.dve, fn, [t.b if isinstance(t, Tl) else t for t in reads],
                         [t.b if isinstance(t, Tl) else t for t in writes])

    def A(self, fn, reads=(), writes=()):
        return self.f.op(self.f.act, fn, [t.b if isinstance(t, Tl) else t for t in reads],
                         [t.b if isinstance(t, Tl) else t for t in writes])

    def G(self, fn, reads=(), writes=()):
        return self.f.op(self.f.pool, fn, [t.b if isinstance(t, Tl) else t for t in reads],
                         [t.b if isinstance(t, Tl) else t for t in writes])

    def P(self, fn, reads=(), writes=()):
        return self.f.op(self.f.pe, fn, [t.b if isinstance(t, Tl) else t for t in reads],
                         [t.b if isinstance(t, Tl) else t for t in writes])

    def dma(self, out, in_, reads=(), writes=(), q=None, **kw):
        q = q or self.f.sp
        return self.f.dma(q, out, in_, [t.b if isinstance(t, Tl) else t for t in reads],
                          [t.b if isinstance(t, Tl) else t for t in writes], **kw)

    def load(self, tl, src_ap, src_name, idx=slice(None), cast=False, **kw):
        q = self.f.pq if cast else self.f.sp
        return self.dma(tl.t[idx], src_ap, reads=[], writes=[tl], q=q, **kw)

    def store(self, dst_ap, dst_name, tl, idx=slice(None), **kw):
        q = self.f.pq if getattr(self, "store_on_pq", False) else None
        return self.dma(dst_ap, tl.t[idx], reads=[tl], writes=[], q=q, **kw)

    def mm(self, out_ps, pairs, reads, out_ap=None):
        nc = self.nc
        oap = out_ap if out_ap is not None else out_ps.t[:]

        def fn():
            n = len(pairs)
            for i, (l, r) in enumerate(pairs):
                ins = nc.tensor.matmul(oap, l, r, start=(i == 0), stop=(i == n - 1))
            return ins
        return self.P(fn, reads=reads, writes=[out_ps])

    def rsqrt(self, out_tl, in_tl, scale, shape_idx=slice(None)):
        nc = self.nc
        psl = shape_idx[0] if isinstance(shape_idx, tuple) else slice(None)
        self.A(lambda: nc.scalar.activation(out=out_tl.t[shape_idx], in_=in_tl.t[shape_idx], func=AF.Ln,
                                            scale=scale, bias=self.c_eps.t[psl, 0:1]),
               reads=[in_tl, self.c_eps], writes=[out_tl])
        self.A(lambda: nc.scalar.activation(out=out_tl.t[shape_idx], in_=out_tl.t[shape_idx], func=AF.Exp,
                                            scale=-0.5),
               reads=[out_tl], writes=[out_tl])


COLS = {}
ROWS = {}


def _mk_tables(p):
    cols, rows = [], []

    def addc(name, arr):
        arr = np.asarray(arr, np.float32).reshape(128, -1)
        COLS[name] = (sum(a.shape[1] for a in cols), arr.shape[1])
        cols.append(arr)

    def addr(name, vec):
        vec = np.asarray(vec, np.float32).reshape(1, -1)
        ROWS[name] = (sum(a.shape[1] for a in rows), vec.shape[1])
        rows.append(np.broadcast_to(vec, (128, vec.shape[1])))

    def fm(v):
        return np.asarray(v, np.float32).reshape(-1, 128).T

    addc("g_mix", fm(p["norm_mix"][0]))
    addc("g_qn", np.tile(p["nsa_q_norm"][0], 2)[:, None])
    addc("g_ks", np.tile(p["nsa_k_norm"][0, 1], 2)[:, None])
    addc("g_kw", np.tile(p["nsa_k_norm"][0, 2], 2)[:, None])
    addc("g_kc", np.tile(p["nsa_k_norm"][0, 0], 2)[:, None])
    addc("g_mq", p["mem_q_norm"][0][:, None])
    addc("g_mk", p["mem_k_norm"][0][:, None])
    addc("g_memn", fm(p["mem_norm"][0]))
    addc("g_ssdn", fm(p["ssd_norm"][0]))
    addc("g_ffn", fm(p["norm_ffn"][0]))
    addc("cw_ssd", np.stack([fm(p["ssd_conv_w"][0, k]) for k in range(4)], 1).reshape(128, -1))
    addc("cb_ssd", fm(p["ssd_conv_b"][0]))
    addc("cw_ffn", np.stack([fm(p["ffn_conv_w"][0, k]) for k in range(3)], 1).reshape(128, -1))
    addc("cb_ffn", fm(p["ffn_conv_b"][0]))
    addr("dtb", p["ssd_dt_bias"][0])
    addr("alog", p["ssd_a_log"][0])
    addr("ssd_d", p["ssd_d"][0])
    return np.concatenate(cols, 1), np.ascontiguousarray(np.concatenate(rows, 1))


def _consts():
    c = {}
    c["ident"] = np.eye(128, dtype=np.float32)
    i = np.arange(128)
    c["c_tri"] = (i[:, None] <= i[None, :]).astype(np.float32)
    negm = np.where(i[None, :] < i[:, None], NEGB, 0.0).astype(np.float32)
    c["c_negm4"] = np.tile(negm, (1, 4))
    d16 = np.zeros((16, 16, 128), np.float32)
    d16[np.arange(16), np.arange(16), :] = 1.0
    c["c_delta16"] = d16.reshape(16, 2048)
    sel = np.zeros((128, 128), np.float32); sel[127, :] = 1.0
    c["c_sel127"] = sel
    tok = np.arange(S)
    c["c_eblk"] = (tok[None, :] // 64 == np.arange(64)[:, None]).astype(np.float32)
    kside = np.zeros((2, 64, S), np.float32)
    kside[0, :60] = c["c_eblk"][:60]
    for v in range(2):
        kside[v, 60] = tok % 128
        kside[v, 61] = tok // 128
        kside[v, 62] = 1.0
        kside[v, 63] = 1.0
    c["c_kside"] = kside.reshape(128, S)
    raq = np.zeros((32, 2, 4, 4, 128), np.float32)
    for it in range(32):
        for g in range(2):
            for hl in range(4):
                sl = 2.0 ** -(4 * g + hl + 1)
                raq[it, g, 0, hl] = 8 * sl
                raq[it, g, 1, hl] = 1024 * sl
                raq[it, g, 2, hl] = -8 * sl * i
                raq[it, g, 3, hl] = -1024 * sl * it
    c["c_raq"] = raq.reshape(32 * 2 * 4, 512)
    la = np.zeros((3, S), np.float32)
    la[0] = tok % 128; la[1] = 1.0; la[2] = tok // 128
    c["c_laall"] = la
    ra = np.zeros((3, 2, 4, 128), np.float32)
    for g in range(2):
        for hl in range(4):
            sl = 2.0 ** -(4 * g + hl + 1)
            ra[0, g, hl] = 8 * sl
            ra[1, g, hl] = -8 * sl * i
            ra[2, g, hl] = -1024 * sl
    c["c_ra"] = ra.reshape(3, 1024)
    rac = np.zeros((3, 2, 4, 128), np.float32)
    for g in range(2):
        for hl in range(4):
            sl = 2.0 ** -(4 * g + hl + 1)
            rac[0, g, hl] = 128 * sl
            rac[1, g, hl] = -8 * sl * (i - 31)
            rac[2, g, hl] = -1024 * sl
    c["c_rac"] = rac.reshape(3, 1024)
    kk = np.arange(16)
    shm = np.zeros((16, 32, 128), np.float32)
    for dc in range(32):
        shm[:, dc, :] = ((i[None, :] - 8 * dc) == (kk[:, None] - 8))
    c["c_shm"] = shm.reshape(16, S)
    c["c_cmask"] = np.tile(np.where(16 * (kk[:, None] - 8) > (i[None, :] - 31), 8 * NEGB, 0.0).astype(np.float32), (1, 4))
    c["c_caus"] = np.tile(np.where(i[:, None] > i[None, :], 8 * NEGB, 0.0).astype(np.float32), (1, 4))
    c["c_winm"] = np.tile(np.where(i[:, None] <= i[None, :], 8 * NEGB, 0.0).astype(np.float32), (1, 4))
    c["c_d0"] = (i[None, :] - 16.0 * i[:, None]).astype(np.float32)
    rel = i[None, :] - 64 - (i[:, None] // 64)
    c["c_mw1"] = np.where(rel >= -1, 0.0, 1.0).astype(np.float32)
    c["c_mw2"] = np.where((rel == 0) | (rel == -1), 1e9, np.where(rel > 0, -1e30, 0.0)).astype(np.float32)
    n = np.arange(255)
    j = np.arange(64)
    ov = np.clip(np.minimum(16 * n[:, None] + 31, 64 * j[None, :] + 63) - np.maximum(16 * n[:, None], 64 * j[None, :]) + 1, 0, None) / 32.0
    ovl = np.zeros((128, 2, 64), np.float32)
    ovl[:, 0] = ov[:128]; ovl[:127, 1] = ov[128:]
    c["c_ovl"] = ovl.reshape(128, 128)
    return c


def bc(ap, shape):
    return ap.to_broadcast(list(shape))


def phase_setup(k):
    nc = k.nc
    k.din("cols", [128, k.ncols])
    k.din("rows", [128, k.nrows])
    k.din("ident", [128, 128])
    k.cols = k.sb("cols", [128, k.ncols], F32)
    k.rows = k.sb("rows", [128, k.nrows], F32)
    k.ident = k.sb("ident", [128, 128], BF16)
    k.identf = k.sb("identf", [128, 128], F32)
    k.c_eps = k.sb("c_eps", [128, 1], F32)
    k.load(k.cols, k.dram["cols"], "cols")
    k.load(k.rows, k.dram["rows"], "rows")
    k.load(k.ident, k.dram["ident"], "ident", cast=True)
    k.load(k.identf, k.dram["ident"], "ident")
    k.V(lambda: nc.vector.memset(k.c_eps.t[:], EPS), writes=[k.c_eps])
    k.psf = Ring([k.ps(f"psf{i}", [128, 512], F32) for i in range(6)])
    k.psb = Ring([k.ps(f"psb{i}", [128, 1024], BF16) for i in range(2)])


def col(k, name, j=0, n=1):
    s0, _ = COLS[name]
    return k.cols.t[:, s0 + j:s0 + j + n]


def row(k, name):
    s0, n = ROWS[name]
    return k.rows.t[:, s0:s0 + n]


def transpose_to(k, dst_tl, dst_ap_fn, src_tl, nblk, gain=None, reads=(), ring=None):
    nc = k.nc
    pt = (ring or k.psb).next()

    def tr():
        for j in range(nblk):
            ins = nc.tensor.transpose(pt.t[:, j * 128:(j + 1) * 128], src_tl.t[:, j * 128:(j + 1) * 128], k.ident.t[:])
        return ins
    k.P(tr, reads=[src_tl, k.ident] + list(reads), writes=[pt])
    return pt


CONV_W = (("w_ssd_o", [1024, 1024]), ("c_negm4", [128, 512]),
          ("cmp_w1", [4096, 256]), ("cmp_w2", [512, 64]), ("c_peT", [64, 64]),
          ("c_kside", [128, S]), ("c_eblk", [64, S]), ("c_raq", [256, 512]), ("c_laall", [3, S]), ("c_shm", [16, S]),
          ("c_rac", [3, 1024]), ("c_cmask", [16, 512]), ("c_caus", [128, 512]), ("c_winm", [128, 512]), ("c_ovl", [128, 128]),
          ("w_nsa_o", [512, 1024]), ("w_mem_kv", [1024, 1024]), ("w_mem_o", [512, 1024]), ("w_out", [1024, 1024]),
          ("w_ffn_up", [1024, 2 * FFH]), ("w_ffn_down", [FFH, 1024]))


LATE_CONV = ("w_mem_kv", "w_mem_o", "w_out", "w_ffn_up", "w_ffn_down")


def convert_weights(k, after=()):
    k.late_conv = []
    for nm, shp in CONV_W:
        src = k.din(nm, shp)
        dst = k.dscr(nm + "_bf", shp, BF16)
        rows = shp[0]
        if nm in LATE_CONV:
            step = 64 if nm == "w_ffn_up" else 128
            for r0 in range(0, rows, step):
                r1 = min(rows, r0 + step)
                k.late_conv.append((dst[r0:r1, :], src[r0:r1, :], nm))
            continue
        step = 256
        for r0 in range(0, rows, step):
            r1 = min(rows, r0 + step)
            k.dma(dst[r0:r1, :], src[r0:r1, :], reads=list(after), writes=[k.dbuf[nm + "_bf"]], q=k.f.pq, acc=True)


def issue_late_conv(k, n=1):
    for _ in range(n):
        if getattr(k, "late_conv", None):
            d_, s_, nm = k.late_conv.pop(0)
            k.dma(d_, s_, writes=[k.dbuf[nm + "_bf"]], q=k.f.pq, acc=True, inflight=2)


def wload(k, tl, name, rearr, idx=slice(None), **kw):
    src = k.dram[name + "_bf"]
    return k.dma(tl.t[idx], src.rearrange(rearr, **kw), reads=[k.dbuf[name + "_bf"]], writes=[tl])


def phase_a(k):
    nc = k.nc
    x = k.din("x", [S, D])
    w_in = k.din("w_in", [D, INW])
    z_s = k.dscr("z_s", [S, 1024], BF16)
    xbcT_s = k.dscr("xbcT_s", [12, 128, S], BF16)
    dts_s = k.dscr("dts_s", [S, 64], F32)
    qT_s = k.dscr("qT_s", [4, 128, S], BF16)
    kTs_s = k.dscr("kTs_s", [128, S], BF16)
    kTw_s = k.dscr("kTw_s", [128, S], BF16)
    v_s = k.dscr("v_s", [S, 256], BF16)
    gn_s = k.dscr("gn_s", [S, 24], F32)
    qmT_s = k.dscr("qmT_s", [4, 128, S], BF16)
    kvcT_s = k.dscr("kvcT_s", [2, 128, S], BF16)
    gmT_s = k.dscr("gmT_s", [24, 128, S], BF16)

    WIN = k.sb("WIN", [128, 8, INW], BF16)
    wgroups = [(0, 1024), (2560, 3088), (3344, 4392), (1024, 2560), (3088, 3344), (4392, 5928), (5928, INW)]
    WINb = {}
    for (c_lo, c_hi) in wgroups:
        b_ = Buf(f"WIN{c_lo}")
        for kk in range(8):
            k.dma(WIN.t[:, kk, c_lo:c_hi], w_in[kk * 128:(kk + 1) * 128, c_lo:c_hi], writes=[b_], q=k.f.pq, acc=True, inflight=3)
        for c in range(c_lo, c_hi, 8):
            WINb[c] = b_

    def wbuf(c0):
        return WINb[(c0 // 8) * 8]
    convert_weights(k, after=[WINb[c_lo] for (c_lo, _c) in wgroups])
    psb_main = Ring([k.psb.tiles[0]])
    psb_prep = Ring([k.psb.tiles[1]])
    k.arow = k.sb("arow", [128, 16], F32)
    k.A(lambda: nc.scalar.activation(out=k.arow.t[:], in_=row(k, "alog"), func=AF.Exp), reads=[k.rows], writes=[k.arow])
    k.V(lambda: nc.vector.tensor_scalar(out=k.arow.t[:], in0=k.arow.t[:], scalar1=-1.0, scalar2=None, op0=ALU.mult),
        reads=[k.arow], writes=[k.arow])

    xt_r = k.ring("xt", [128, D], F32, 3)
    junk = k.sb("junk", [128, D], BF16)
    ss_r = k.ring("ss", [128, 8], F32, 4)
    xs_r = k.ring("xs", [128, D], BF16, 2)
    ssp_r = k.ring("ssp", [128, 8], F32, 2)
    xnT_r = k.ring("xnT", [128, 8, 512], BF16, 2)
    zt_r = k.ring("zt", [128, 1024], BF16, 2)
    fm_r = k.ring("fm", [128, 6, 512], BF16, 3)
    dt_r = k.ring("dtt", [128, 64], F32, 2)
    for t_ in dt_r.tiles:
        k.V(lambda: nc.vector.memset(t_.t[:, 32:48], 0.0), writes=[t_])
    sq_r = k.ring("sq", [128, 512], F32, 2)
    qn_r = k.ring("qn", [128, 512], BF16, 2)
    qmn_r = k.ring("qmn", [128, 512], BF16, 2)
    qT_r = k.ring("qTt", [128, 4, 128], BF16, 2)
    kn_r = k.ring("kn", [128, 256], BF16, 2)
    kT_r = k.ring("kTt", [128, 2, 128], BF16, 2)
    vv_r = k.ring("vv", [128, 256], BF16, 2)
    gn_r = k.ring("gnt", [128, 24], F32, 2)

    def tok_major(xnT, tt, tok0, c0, n):
        pt = k.psf.next()
        k.mm(pt, [(xnT.t[:, kk, tt * 128:(tt + 1) * 128], WIN.t[:, kk, c0:c0 + n]) for kk in range(8)],
             reads=[xnT, wbuf(c0)], out_ap=pt.t[:, 0:n])
        run_deferred()
        fill()
        return pt

    def headnorm(pt, n, hd, c_lo=0):
        nh = n // hd
        sq = sq_r.next()
        k.A(lambda: nc.scalar.activation(out=sq.t[:, 0:n], in_=pt.t[:, c_lo:c_lo + n], func=AF.Square), reads=[pt], writes=[sq])
        ss = ss_r.next()
        k.V(lambda: nc.vector.tensor_reduce(out=ss.t[:, 0:nh], in_=sq.t[:, 0:n].rearrange("p (h d) -> p h d", d=hd),
                                            axis=AX.X, op=ALU.add), reads=[sq], writes=[ss])
        k.rsqrt(ss, ss, 1.0 / hd, (slice(None), slice(0, nh)))
        return ss

    def prep(tb, xnT):
        xts = []

        def ld(tt):
            xt_ = xt_r.next()
            k.load(xt_, x[tb * 512 + tt * 128:tb * 512 + (tt + 1) * 128, :], "x")
            xts.append(xt_)
        ld(0)
        ld(1)
        yield
        for tt in range(4):
            tok0 = tb * 512 + tt * 128
            xt = xts[tt]
            if tt + 2 < 4:
                ld(tt + 2)
            ss = ssp_r.next()
            k.A(lambda: nc.scalar.activation(out=junk.t[:], in_=xt.t[:], func=AF.Square, accum_out=ss.t[:, 0:1]),
                reads=[xt], writes=[junk, ss])
            yield
            k.rsqrt(ss, ss, 1.0 / D, (slice(None), slice(0, 1)))
            yield
            xs = xs_r.next()
            k.V(lambda: nc.vector.tensor_scalar(out=xs.t[:], in0=xt.t[:], scalar1=ss.t[:, 0:1], scalar2=None, op0=ALU.mult),
                reads=[xt, ss], writes=[xs])
            yield
            pt = transpose_to(k, None, None, xs, 8, ring=psb_prep)
            k.V(lambda: nc.vector.tensor_tensor(
                out=xnT.t[:, :, tt * 128:(tt + 1) * 128], in0=pt.t[:].rearrange("p (k t) -> p k t", t=128),
                in1=bc(col(k, "g_mix", 0, 8).unsqueeze(2), [128, 8, 128]), op=ALU.mult),
                reads=[pt, k.cols], writes=[xnT])
            yield

    filler = [None]
    deferred = []
    gcount = [0]

    def run_deferred(force=False):
        gcount[0] += 1
        while deferred and (force or deferred[0][0] <= gcount[0]):
            deferred.pop(0)[1]()

    def fill():
        if filler[0] is not None:
            next(filler[0], None)

    xnT_next = xnT_r.next()
    for _ in prep(0, xnT_next):
        pass
    for tb in range(8):
        xnT = xnT_next
        if tb + 1 < 8:
            xnT_next = xnT_r.next()
            filler[0] = prep(tb + 1, xnT_next)
        else:
            filler[0] = None
        for tt in range(4):
            tok0 = tb * 512 + tt * 128
            zt = zt_r.next()
            for hh in range(2):
                pt = tok_major(xnT, tt, tok0, O_Z + hh * 512, 512)
                k.A(lambda: nc.scalar.copy(out=zt.t[:, hh * 512:(hh + 1) * 512], in_=pt.t[:]), reads=[pt], writes=[zt])
            k.store(z_s[tok0:tok0 + 128, :], "z_s", zt)
            pt = tok_major(xnT, tt, tok0, O_DT, 16)
            dtt = dt_r.next()
            k.V(lambda: nc.vector.tensor_tensor(out=dtt.t[:, 0:16], in0=pt.t[:, 0:16], in1=row(k, "dtb"), op=ALU.add),
                reads=[pt, k.rows], writes=[dtt])
            k.A(lambda: nc.scalar.activation(out=dtt.t[:, 0:16], in_=dtt.t[:, 0:16], func=AF.Exp), reads=[dtt], writes=[dtt])
            k.A(lambda: nc.scalar.activation(out=dtt.t[:, 0:16], in_=dtt.t[:, 0:16], func=AF.Ln, bias=1.0), reads=[dtt], writes=[dtt])
            k.V(lambda: nc.vector.tensor_tensor(out=dtt.t[:, 16:32], in0=dtt.t[:, 0:16], in1=k.arow.t[:], op=ALU.mult),
                reads=[dtt, k.arow], writes=[dtt])
            k.V(lambda: nc.vector.tensor_scalar(out=dtt.t[:, 48:64], in0=dtt.t[:, 16:32], scalar1=-1.0, scalar2=None, op0=ALU.mult),
                reads=[dtt], writes=[dtt])
            k.store(dts_s[tok0:tok0 + 128, :], "dts_s", dtt)
            pt = tok_major(xnT, tt, tok0, O_QN, 512)
            rs = headnorm(pt, 512, 64)
            qn = qn_r.next()
            k.V(lambda: nc.vector.tensor_tensor(out=qn.t[:].rearrange("p (h d) -> p h d", d=64),
                                                in0=pt.t[:].rearrange("p (h d) -> p h d", d=64),
                                                in1=bc(rs.t[:, 0:8].unsqueeze(2), [128, 8, 64]), op=ALU.mult),
                reads=[pt, rs], writes=[qn])
            def dq(qn=qn, tok0=tok0):
                ptb = transpose_to(k, None, None, qn, 4, ring=psb_main)
                qTt = qT_r.next()
                k.V(lambda: nc.vector.tensor_scalar(out=qTt.t[:].rearrange("p j t -> p (j t)"), in0=ptb.t[:, 0:512],
                                                    scalar1=col(k, "g_qn"), scalar2=None, op0=ALU.mult),
                    reads=[ptb, k.cols], writes=[qTt])
                k.store(qT_s[:, :, tok0:tok0 + 128].rearrange("j p t -> p j t"), "qT_s", qTt)
            deferred.append((gcount[0] + 6, dq))
            pt = tok_major(xnT, tt, tok0, O_KV + 256, 512)
            rs = headnorm(pt, 512, 64)
            kn = kn_r.next()
            for b, c_lo, r_lo in ((0, 0, 0), (1, 256, 4)):
                k.V(lambda: nc.vector.tensor_tensor(out=kn.t[:, b * 128:(b + 1) * 128].rearrange("p (h d) -> p h d", d=64),
                                                    in0=pt.t[:, c_lo:c_lo + 128].rearrange("p (h d) -> p h d", d=64),
                                                    in1=bc(rs.t[:, r_lo:r_lo + 2].unsqueeze(2), [128, 2, 64]), op=ALU.mult),
                    reads=[pt, rs], writes=[kn])
            vv = vv_r.next()
            k.A(lambda: nc.scalar.copy(out=vv.t[:].rearrange("p (b c) -> p b c", b=2),
                                       in_=pt.t[:].rearrange("p (b c) -> p b c", b=2)[:, :, 128:256]), reads=[pt], writes=[vv])
            k.store(v_s[tok0:tok0 + 128, :], "v_s", vv)
            def dk(kn=kn, tok0=tok0):
                ptb = transpose_to(k, None, None, kn, 2, ring=psb_main)
                kTt = kT_r.next()
                k.V(lambda: nc.vector.tensor_scalar(out=kTt.t[:, 0, :], in0=ptb.t[:, 0:128], scalar1=col(k, "g_ks"), scalar2=None,
                                                    op0=ALU.mult), reads=[ptb, k.cols], writes=[kTt])
                k.V(lambda: nc.vector.tensor_scalar(out=kTt.t[:, 1, :], in0=ptb.t[:, 128:256], scalar1=col(k, "g_kw"), scalar2=None,
                                                    op0=ALU.mult), reads=[ptb, k.cols], writes=[kTt])
                k.store(kTs_s[:, tok0:tok0 + 128], "kTs_s", kTt, (slice(None), 0, slice(None)))
                k.store(kTw_s[:, tok0:tok0 + 128], "kTw_s", kTt, (slice(None), 1, slice(None)))
            deferred.append((gcount[0] + 6, dk))
            pt = tok_major(xnT, tt, tok0, O_GN, 24)
            gnt = gn_r.next()
            k.A(lambda: nc.scalar.activation(out=gnt.t[:], in_=pt.t[:, 0:24], func=AF.Exp, scale=-1.0), reads=[pt], writes=[gnt])
            k.V(lambda: nc.vector.tensor_scalar(out=gnt.t[:], in0=gnt.t[:], scalar1=1.0, scalar2=None, op0=ALU.add),
                reads=[gnt], writes=[gnt])
            k.V(lambda: nc.vector.reciprocal(out=gnt.t[:], in_=gnt.t[:]), reads=[gnt], writes=[gnt])
            k.store(gn_s[tok0:tok0 + 128, :], "gn_s", gnt)
            pt = tok_major(xnT, tt, tok0, O_QM, 512)
            rs = headnorm(pt, 512, 128)
            qn = qmn_r.next()
            k.V(lambda: nc.vector.tensor_tensor(out=qn.t[:].rearrange("p (h d) -> p h d", d=128),
                                                in0=pt.t[:].rearrange("p (h d) -> p h d", d=128),
                                                in1=bc(rs.t[:, 0:4].unsqueeze(2), [128, 4, 128]), op=ALU.mult),
                reads=[pt, rs], writes=[qn])
            def dm(qn=qn, tok0=tok0):
                ptb = transpose_to(k, None, None, qn, 4, ring=psb_main)
                qTt = qT_r.next()
                k.V(lambda: nc.vector.tensor_scalar(out=qTt.t[:].rearrange("p j t -> p (j t)"), in0=ptb.t[:, 0:512],
                                                    scalar1=col(k, "g_mq"), scalar2=None, op0=ALU.mult),
                    reads=[ptb, k.cols], writes=[qTt])
                k.store(qmT_s[:, :, tok0:tok0 + 128].rearrange("j p t -> p j t"), "qmT_s", qTt)
            deferred.append((gcount[0] + 6, dm))
        t0 = tb * 512

        def feat_major(c0):
            pt = k.psf.next()
            k.mm(pt, [(WIN.t[:, kk, c0:c0 + 128], xnT.t[:, kk, :]) for kk in range(8)], reads=[xnT, wbuf(c0)])
            run_deferred()
            fill()
            return pt
        for grp in range(2):
            fm = fm_r.next()
            for j in range(6):
                m = grp * 6 + j
                pt = feat_major(O_XBC + m * 128)
                if j % 2 == 0:
                    k.V(lambda: nc.vector.tensor_copy(out=fm.t[:, j, :], in_=pt.t[:]), reads=[pt], writes=[fm])
                else:
                    k.A(lambda: nc.scalar.copy(out=fm.t[:, j, :], in_=pt.t[:]), reads=[pt], writes=[fm])
            k.store(xbcT_s[grp * 6:grp * 6 + 6, :, t0:t0 + 512].rearrange("m p t -> p m t"), "xbcT_s", fm)
        fm = fm_r.next()
        for j in range(2):
            pt = feat_major(O_KV + j * 128)
            k.V(lambda: nc.vector.tensor_copy(out=fm.t[:, j, :], in_=pt.t[:]), reads=[pt], writes=[fm])
        k.store(kvcT_s[:, :, t0:t0 + 512].rearrange("m p t -> p m t"), "kvcT_s", fm, (slice(None), slice(0, 2), slice(None)))
        for grp in range(4):
            fm = fm_r.next()
            for j in range(6):
                m = grp * 6 + j
                pt = feat_major(O_GM + m * 128)
                k.A(lambda: nc.scalar.activation(out=fm.t[:, j, :], in_=pt.t[:], func=AF.Sigmoid), reads=[pt], writes=[fm])
            k.store(gmT_s[grp * 6:grp * 6 + 6, :, t0:t0 + 512].rearrange("m p t -> p m t"), "gmT_s", fm)
        if tb == 7:
            run_deferred(force=True)
        if filler[0] is not None:
            for _ in filler[0]:
                pass


def phase_b(k):
    nc = k.nc
    for nm, shp in (("c_tri", [128, 128]), ("c_delta16", [16, 2048]), ("c_sel127", [128, 128])):
        k.din(nm, shp)
    mix1T_s = k.dscr("mix1T_s", [8, 128, S], BF16)
    xbcT_s, dts_s, z_s, gmT_s = k.dram["xbcT_s"], k.dram["dts_s"], k.dram["z_s"], k.dram["gmT_s"]

    TRI = k.sb("TRI", [128, 128], F32); k.load(TRI, k.dram["c_tri"], "")
    DEL = k.sb("DEL", [16, 16, 128], F32); k.load(DEL, k.dram["c_delta16"].rearrange("p (h l) -> p h l", l=128), "")
    SEL = k.sb("SEL", [128, 128], F32); k.load(SEL, k.dram["c_sel127"], "")
    WSO = k.sb("WSO", [128, 8, 1024], BF16)
    wload(k, WSO, "w_ssd_o", "(k p) n -> p k n", p=128)
    DGC = k.sb("DGC", [128, 48, 128], BF16)
    k.V(lambda: nc.vector.tensor_tensor(out=DGC.t[:], in0=bc(k.identf.t[:].unsqueeze(1), [128, 48, 128]),
                                        in1=bc(col(k, "cw_ssd", 0, 48).unsqueeze(2), [128, 48, 128]), op=ALU.mult),
        reads=[k.identf, k.cols], writes=[DGC])
    DIAGD = k.sb("DIAGD", [128, 16, 128], BF16)
    k.V(lambda: nc.vector.tensor_tensor(out=DIAGD.t[:], in0=bc(k.identf.t[:].unsqueeze(1), [128, 16, 128]),
                                        in1=bc(row(k, "ssd_d").unsqueeze(2), [128, 16, 128]), op=ALU.mult),
        reads=[k.identf, k.rows], writes=[DIAGD])
    state = k.sb("state", [128, 1024], F32)
    prevbf = k.sb("prevbf", [128, 1024], BF16)
    k.V(lambda: nc.vector.memset(state.t[:], 0.0), writes=[state])
    k.V(lambda: nc.vector.memset(prevbf.t[:], 0.0), writes=[prevbf])

    NEGM4b = k.sb("NEGM4b", [128, 512], BF16)
    k.dma(NEGM4b.t[:], k.dram["c_negm4_bf"], reads=[k.dbuf["c_negm4_bf"]], writes=[NEGM4b])
    LT_r = k.ring("LT", [48, 128], F32, 2)
    RT_r = k.ring("RT", [48, 16, 128], F32, 2)
    for t_ in LT_r.tiles:
        k.V(lambda: nc.vector.memset(t_.t[:], 0.0), writes=[t_])
        k.V(lambda: nc.vector.memset(t_.t[0:16, :], 1.0), writes=[t_])
    for t_ in RT_r.tiles:
        k.V(lambda: nc.vector.memset(t_.t[:], 0.0), writes=[t_])
        k.dma(t_.t[32:48, :, :], k.dram["c_delta16"].rearrange("p (h l) -> p h l", l=128), writes=[t_])

    XR_r = k.ring("XR", [128, 12, 515], BF16, 2)
    for t in XR_r.tiles:
        k.V(lambda: nc.vector.memset(t.t[:, :, 0:3], 0.0), writes=[t])
    XC_r = k.ring("XC", [128, 12, 512], BF16, 2)
    GM_r = k.ring("GM", [128, 8, 512], BF16, 2)
    ynT_r = k.ring("ynT", [128, 8, 512], BF16, 2)
    mo_r = k.ring("mo", [128, 512], BF16, 2)
    dts_r = k.ring("dts", [128, 64], F32, 2)
    z_r = k.ring("zc", [128, 1024], BF16, 2)
    sm_r = k.ring("sm", [128, 64], F32, 2)
    xtok_r = k.ring("xtok", [128, 3, 1024], BF16, 1)
    bt_r = k.ring("btok", [128, 256], BF16, 1)
    cbT_r = k.ring("cbT", [128, 2, 128], BF16, 1)
    lm_r = k.ring("lm", [128, 4, 128], BF16, 2)
    MT_r = k.ring("MT", [128, 16, 128], BF16, 1)
    yf_r = k.ring("yf", [128, 1024], F32, 2)
    zs_r = k.ring("zs", [128, 1024], F32, 2)
    yn_r = k.ring("yn", [128, 1024], BF16, 2)
    junk = k.sb("junkb", [128, 1024], BF16)

    def load_chunk(c):
        dts = dts_r.next(); k.load(dts, dts_s[c * 128:(c + 1) * 128, :], "")
        zc = z_r.next(); k.load(zc, z_s[c * 128:(c + 1) * 128, :], "")
        return dts, zc

    def tail(yn, ynT, cs, do_proj, GM, t0):
        pt = transpose_to(k, None, None, yn, 8)
        k.V(lambda: nc.vector.tensor_tensor(out=ynT.t[:, :, cs], in0=pt.t[:].rearrange("p (k t) -> p k t", t=128),
                                            in1=bc(col(k, "g_ssdn", 0, 8).unsqueeze(2), [128, 8, 128]), op=ALU.mult),
            reads=[pt, k.cols], writes=[ynT])
        if do_proj:
            for e in range(8):
                pt = k.psf.next()
                k.mm(pt, [(WSO.t[:, kk, e * 128:(e + 1) * 128], ynT.t[:, kk, :]) for kk in range(8)], reads=[WSO, ynT])
                mo = mo_r.next()
                k.V(lambda: nc.vector.tensor_tensor(out=mo.t[:], in0=pt.t[:], in1=GM.t[:, e, :], op=ALU.mult),
                    reads=[pt, GM], writes=[mo])
                k.store(mix1T_s[e, :, t0:t0 + 512], "", mo)

    nxt = load_chunk(0)
    pending_gate = None
    for tb in range(8):
        t0 = tb * 512
        XR = XR_r.next()
        if tb == 0:
            k.load(XR, xbcT_s[:, :, 0:512].rearrange("m p t -> p m t"), "", (slice(None), slice(None), slice(3, 515)))
        else:
            k.load(XR, xbcT_s[:, :, t0 - 3:t0 + 512].rearrange("m p t -> p m t"), "")
        XC = XC_r.next()
        for m in range(12):
            pt = k.psf.next()
            k.mm(pt, [(DGC.t[:, kk * 12 + m, :], XR.t[:, m, kk:kk + 512]) for kk in range(4)], reads=[DGC, XR])
            k.A(lambda: nc.scalar.activation(out=XC.t[:, m, :], in_=pt.t[:], func=AF.Silu, bias=col(k, "cb_ssd", m)),
                reads=[pt, k.cols], writes=[XC])
        GM = GM_r.next()
        k.load(GM, gmT_s[0:8, :, t0:t0 + 512].rearrange("m p t -> p m t"), "")
        ynT = ynT_r.next()
        for cc in range(4):
            c = tb * 4 + cc
            cs = slice(cc * 128, (cc + 1) * 128)
            dts, zc = nxt
            if c + 1 < NT:
                nxt = load_chunk(c + 1)
            p1 = k.psf.next()
            k.mm(p1, [(TRI.t[:], dts.t[:, 16:32])], reads=[TRI, dts], out_ap=p1.t[:, 0:16])
            p2 = k.psf.next()
            k.mm(p2, [(dts.t[:, 16:64], TRI.t[:])], reads=[TRI, dts], out_ap=p2.t[0:48, 0:128])
            sm = sm_r.next()
            k.V(lambda: nc.vector.tensor_copy(out=sm.t[:, 0:16], in_=p1.t[:, 0:16]), reads=[p1], writes=[sm])
            LT = LT_r.next()
            RT = RT_r.next()
            k.V(lambda: nc.vector.tensor_copy(out=LT.t[32:48, :], in_=p2.t[32:48, 0:128]), reads=[p2], writes=[LT])
            k.V(lambda: nc.vector.tensor_tensor(out=RT.t[0:16, :, :], in0=bc(p2.t[0:16, 0:128].unsqueeze(1), [16, 16, 128]),
                                                in1=DEL.t[:], op=ALU.mult), reads=[p2, DEL], writes=[RT])
            p3 = k.psf.next()
            k.mm(p3, [(SEL.t[:], sm.t[:, 0:16])], reads=[SEL, sm], out_ap=p3.t[:, 0:16])
            k.A(lambda: nc.scalar.activation(out=sm.t[:, 16:32], in_=sm.t[:, 0:16], func=AF.Exp), reads=[sm], writes=[sm])
            k.V(lambda: nc.vector.tensor_tensor(out=sm.t[:, 32:48], in0=p3.t[:, 0:16], in1=sm.t[:, 0:16], op=ALU.subtract),
                reads=[p3, sm], writes=[sm])
            k.A(lambda: nc.scalar.activation(out=sm.t[:, 32:48], in_=sm.t[:, 32:48], func=AF.Exp), reads=[sm], writes=[sm])
            k.A(lambda: nc.scalar.activation(out=sm.t[:, 48:64], in_=p3.t[:, 0:16], func=AF.Exp), reads=[p3, sm], writes=[sm])
            k.V(lambda: nc.vector.tensor_tensor(out=sm.t[:, 32:48], in0=sm.t[:, 32:48], in1=dts.t[:, 0:16], op=ALU.mult),
                reads=[sm, dts], writes=[sm])
            prev_gate = pending_gate
            if prev_gate is not None:
                prev_gate[0]()
            pb = k.psb.next()

            def trx():
                for m in range(8):
                    ins = nc.tensor.transpose(pb.t[:, m * 128:(m + 1) * 128], XC.t[:, m, cs], k.ident.t[:])
                return ins
            k.P(trx, reads=[XC, k.ident], writes=[pb])
            xtok = xtok_r.next()
            k.A(lambda: nc.scalar.copy(out=xtok.t[:, 0, :], in_=pb.t[:]), reads=[pb], writes=[xtok])
            k.V(lambda: nc.vector.tensor_tensor(out=xtok.t[:, 1, :].rearrange("p (h d) -> p h d", d=64),
                                                in0=pb.t[:].rearrange("p (h d) -> p h d", d=64),
                                                in1=bc(dts.t[:, 0:16].unsqueeze(2), [128, 16, 64]), op=ALU.mult),
                reads=[pb, dts], writes=[xtok])
            k.V(lambda: nc.vector.tensor_tensor(out=xtok.t[:, 2, :].rearrange("p (h d) -> p h d", d=64),
                                                in0=pb.t[:].rearrange("p (h d) -> p h d", d=64),
                                                in1=bc(sm.t[:, 32:48].unsqueeze(2), [128, 16, 64]), op=ALU.mult),
                reads=[pb, sm], writes=[xtok])
            pb2 = k.psb.next()

            def trb():
                for g in range(2):
                    ins = nc.tensor.transpose(pb2.t[:, g * 128:(g + 1) * 128], XC.t[:, 8 + g, cs], k.ident.t[:])
                return ins
            k.P(trb, reads=[XC, k.ident], writes=[pb2])
            btok = bt_r.next()
            k.A(lambda: nc.scalar.copy(out=btok.t[:], in_=pb2.t[:, 0:256]), reads=[pb2], writes=[btok])
            p4 = k.psf.next()

            def cbf():
                for g in range(2):
                    ins = nc.tensor.matmul(p4.t[:, g * 128:(g + 1) * 128], XC.t[:, 8 + g, cs], XC.t[:, 10 + g, cs],
                                           start=True, stop=True)
                return ins
            k.P(cbf, reads=[XC], writes=[p4])
            cbT = cbT_r.next()
            k.V(lambda: nc.vector.tensor_copy(out=cbT.t[:].rearrange("p g l -> p (g l)"), in_=p4.t[:, 0:256]),
                reads=[p4], writes=[cbT])
            yn_prev = prev_gate[1]() if prev_gate is not None else None
            MT = MT_r.next()
            for hq in range(4):
                g = hq // 2
                pg = k.psf.next()
                k.mm(pg, [(LT.t[:, :], RT.t[:, 4 * hq:4 * hq + 4, :].rearrange("p h l -> p (h l)")),
                          (k.ident.t[:], NEGM4b.t[:])], reads=[LT, RT, k.ident, NEGM4b])
                lm = lm_r.next()
                k.A(lambda: nc.scalar.activation(out=lm.t[:].rearrange("p h l -> p (h l)"), in_=pg.t[:], func=AF.Exp),
                    reads=[pg], writes=[lm])
                k.V(lambda: nc.vector.tensor_tensor(out=MT.t[:, 4 * hq:4 * hq + 4, :], in0=lm.t[:],
                                                    in1=bc(cbT.t[:, g, :].unsqueeze(1), [128, 4, 128]), op=ALU.mult),
                    reads=[lm, cbT], writes=[MT])
            if prev_gate is not None:
                prev_gate[2](yn_prev)
            yd = [k.psf.next(), k.psf.next()]
            for hb in range(2):
                def ydf():
                    for hh in range(8):
                        h = hb * 8 + hh
                        nc.tensor.matmul(yd[hb].t[:, hh * 64:(hh + 1) * 64], MT.t[:, h, :], xtok.t[:, 1, h * 64:(h + 1) * 64],
                                         start=True, stop=False)
                        ins = nc.tensor.matmul(yd[hb].t[:, hh * 64:(hh + 1) * 64], DIAGD.t[:, h, :],
                                               xtok.t[:, 0, h * 64:(h + 1) * 64], start=False, stop=True)
                    return ins
                k.P(ydf, reads=[MT, xtok, DIAGD], writes=[yd[hb]])
            yo = [k.psf.next(), k.psf.next()]
            for g in range(2):
                k.mm(yo[g], [(XC.t[:, 10 + g, cs], prevbf.t[:, g * 512:(g + 1) * 512])], reads=[XC, prevbf])
            yf = yf_r.next()
            for g in range(2):
                gs = slice(g * 512, (g + 1) * 512)
                k.V(lambda: nc.vector.tensor_tensor(out=yf.t[:, gs].rearrange("p (h d) -> p h d", d=64),
                                                    in0=yo[g].t[:].rearrange("p (h d) -> p h d", d=64),
                                                    in1=bc(sm.t[:, 16 + 8 * g:24 + 8 * g].unsqueeze(2), [128, 8, 64]), op=ALU.mult),
                    reads=[yo[g], sm], writes=[yf])
                k.V(lambda: nc.vector.tensor_tensor(out=yf.t[:, gs], in0=yd[g].t[:], in1=yf.t[:, gs], op=ALU.add),
                    reads=[yd[g], yf], writes=[yf])
            sn = [k.psf.next(), k.psf.next()]
            for g in range(2):
                k.mm(sn[g], [(btok.t[:, g * 128:(g + 1) * 128], xtok.t[:, 2, g * 512:(g + 1) * 512])], reads=[btok, xtok])
            k.G(lambda: nc.gpsimd.tensor_tensor(out=state.t[:].rearrange("p (h d) -> p h d", d=64),
                                                in0=state.t[:].rearrange("p (h d) -> p h d", d=64),
                                                in1=bc(sm.t[:, 48:64].unsqueeze(2), [128, 16, 64]), op=ALU.mult),
                reads=[state, sm], writes=[state])
            for g in range(2):
                gs = slice(g * 512, (g + 1) * 512)
                k.V(lambda: nc.vector.tensor_tensor(out=state.t[:, gs], in0=sn[g].t[:], in1=state.t[:, gs], op=ALU.add),
                    reads=[sn[g], state], writes=[state])
            k.A(lambda: nc.scalar.copy(out=prevbf.t[:], in_=state.t[:]), reads=[state], writes=[prevbf])
            zs = zs_r.next()
            k.A(lambda: nc.scalar.activation(out=zs.t[:], in_=zc.t[:], func=AF.Silu), reads=[zc], writes=[zs])
            k.G(lambda: nc.gpsimd.tensor_tensor(out=zs.t[:], in0=zs.t[:], in1=yf.t[:], op=ALU.mult), reads=[zs, yf], writes=[zs])
            def gate1(zs=zs, sm=sm):
                k.A(lambda: nc.scalar.activation(out=junk.t[:], in_=zs.t[:], func=AF.Square, accum_out=sm.t[:, 0:1]),
                    reads=[zs, sm], writes=[junk, sm])
                k.rsqrt(sm, sm, 1.0 / 1024, (slice(None), slice(0, 1)))

            def gate2(zs=zs, sm=sm):
                yn = yn_r.next()
                k.V(lambda: nc.vector.tensor_scalar(out=yn.t[:], in0=zs.t[:], scalar1=sm.t[:, 0:1], scalar2=None, op0=ALU.mult),
                    reads=[zs, sm], writes=[yn])
                return yn

            def gate3(yn, ynT=ynT, cs=cs, dp=(cc == 3), GM=GM, t0=t0):
                tail(yn, ynT, cs, dp, GM, t0)
            pending_gate = [gate1, gate2, gate3]
    pending_gate[0]()
    pending_gate[2](pending_gate[1]())


def phase_m(k):
    nc = k.nc
    mem = k.din("mem", [256, D])
    mix3T_s = k.dscr("mix3T_s", [8, 128, S], BF16)
    qmT_s, gmT_s = k.dram["qmT_s"], k.dram["gmT_s"]
    WKV = k.sb("WKV", [128, 8, 1024], BF16)
    WMO = k.sb("WMO", [128, 4, 1024], BF16)
    memnT = k.sb("memnT", [128, 8, 256], BF16)
    kmT = k.sb("kmT", [128, 4, 256], BF16)
    VM = k.sb("VM", [128, 2, 512], BF16)
    onesc = k.sb("onesc", [128, 1], BF16)
    k.V(lambda: nc.vector.memset(onesc.t[:], 1.0), writes=[onesc])
    mt_r = k.ring("memt", [128, D], F32, 2)
    ms_r = k.ring("mems", [128, D], BF16, 2)
    ss_r = k.ring("mss", [128, 8], F32, 2)
    sq_r = k.ring("msq", [128, 512], F32, 1)
    kn_r = k.ring("mkn", [128, 512], BF16, 1)
    junk = k.sb("junkm", [128, D], BF16)
    xts_ = []
    for mt in range(2):
        xt = mt_r.next(); k.load(xt, mem[mt * 128:(mt + 1) * 128, :], "")
        xts_.append(xt)
    wload(k, WKV, "w_mem_kv", "(k p) n -> p k n", p=128)
    wload(k, WMO, "w_mem_o", "(k p) n -> p k n", p=128)
    for mt in range(2):
        xt = xts_[mt]
        ss = ss_r.next()
        k.A(lambda: nc.scalar.activation(out=junk.t[:], in_=xt.t[:], func=AF.Square, accum_out=ss.t[:, 0:1]),
            reads=[xt], writes=[junk, ss])
        k.rsqrt(ss, ss, 1.0 / D, (slice(None), slice(0, 1)))
        xs = ms_r.next()
        k.V(lambda: nc.vector.tensor_scalar(out=xs.t[:], in0=xt.t[:], scalar1=ss.t[:, 0:1], scalar2=None, op0=ALU.mult),
            reads=[xt, ss], writes=[xs])
        pt = transpose_to(k, None, None, xs, 8)
        k.V(lambda: nc.vector.tensor_tensor(out=memnT.t[:, :, mt * 128:(mt + 1) * 128],
                                            in0=pt.t[:].rearrange("p (k t) -> p k t", t=128),
                                            in1=bc(col(k, "g_memn", 0, 8).unsqueeze(2), [128, 8, 128]), op=ALU.mult),
            reads=[pt, k.cols], writes=[memnT])
    for mt in range(2):
        pt = k.psf.next()
        k.mm(pt, [(memnT.t[:, kk, mt * 128:(mt + 1) * 128], WKV.t[:, kk, 0:512]) for kk in range(8)], reads=[memnT, WKV])
        sq = sq_r.next()
        k.A(lambda: nc.scalar.activation(out=sq.t[:], in_=pt.t[:], func=AF.Square), reads=[pt], writes=[sq])
        ss = ss_r.next()
        k.V(lambda: nc.vector.tensor_reduce(out=ss.t[:, 0:4], in_=sq.t[:].rearrange("p (h d) -> p h d", d=128), axis=AX.X,
                                            op=ALU.add), reads=[sq], writes=[ss])
        k.rsqrt(ss, ss, 1.0 / 128, (slice(None), slice(0, 4)))
        kn = kn_r.next()
        k.V(lambda: nc.vector.tensor_tensor(out=kn.t[:].rearrange("p (h d) -> p h d", d=128),
                                            in0=pt.t[:].rearrange("p (h d) -> p h d", d=128),
                                            in1=bc(ss.t[:, 0:4].unsqueeze(2), [128, 4, 128]), op=ALU.mult),
            reads=[pt, ss], writes=[kn])
        ptb = transpose_to(k, None, None, kn, 4)
        k.V(lambda: nc.vector.tensor_scalar(out=kmT.t[:, :, mt * 128:(mt + 1) * 128],
                                            in0=ptb.t[:, 0:512].rearrange("p (h t) -> p h t", t=128),
                                            scalar1=col(k, "g_mk"), scalar2=None, op0=ALU.mult),
            reads=[ptb, k.cols], writes=[kmT])
        pt = k.psf.next()
        k.mm(pt, [(memnT.t[:, kk, mt * 128:(mt + 1) * 128], WKV.t[:, kk, 512:1024]) for kk in range(8)], reads=[memnT, WKV])
        k.A(lambda: nc.scalar.copy(out=VM.t[:, mt, :], in_=pt.t[:]), reads=[pt], writes=[VM])

    if getattr(k, "WUP", None) is not None:
        issue_wup(k)
    qm_r = k.ring("qmb", [128, 4, 512], BF16, 2)
    GM_r = k.ring("GMm", [128, 8, 512], BF16, 3)
    PT_r = k.ring("PTm", [128, 2, 512], BF16, 4)
    om_r = k.ring("omem", [128, 4, 512], BF16, 2)
    omT_r = k.ring("omT", [128, 4, 512], BF16, 2)
    rd_r = k.ring("rdm", [128, 16], F32, 4)
    mo_r = k.ring("mom", [128, 512], BF16, 2)
    sc = 128.0 ** -0.5

    def loads(tb):
        qm = qm_r.next(); k.load(qm, qmT_s[:, :, tb * 512:(tb + 1) * 512].rearrange("h p t -> p h t"), "")
        GM = GM_r.next(); k.load(GM, gmT_s[16:24, :, tb * 512:(tb + 1) * 512].rearrange("m p t -> p m t"), "")
        return qm, GM

    def tail(om, GM, t0):
        omT = omT_r.next()
        for tt in range(4):
            pb = k.psb.next()

            def tro():
                for j in range(4):
                    ins = nc.tensor.transpose(pb.t[:, j * 128:(j + 1) * 128], om.t[:, tt, j * 128:(j + 1) * 128], k.ident.t[:])
                return ins
            k.P(tro, reads=[om, k.ident], writes=[pb])
            k.A(lambda: nc.scalar.copy(out=omT.t[:, :, tt * 128:(tt + 1) * 128],
                                       in_=pb.t[:, 0:512].rearrange("p (k t) -> p k t", t=128)), reads=[pb], writes=[omT])
        for e in range(8):
            pt = k.psf.next()
            k.mm(pt, [(WMO.t[:, kk, e * 128:(e + 1) * 128], omT.t[:, kk, :]) for kk in range(4)], reads=[WMO, omT])
            mo = mo_r.next()
            k.V(lambda: nc.vector.tensor_tensor(out=mo.t[:], in0=pt.t[:], in1=GM.t[:, e, :], op=ALU.mult),
                reads=[pt, GM], writes=[mo])
            k.store(mix3T_s[e, :, t0:t0 + 512], "", mo)

    nxt = loads(0)
    pending = None
    for tb in range(8):
        t0 = tb * 512
        qm, GM = nxt
        if tb + 1 < 8:
            nxt = loads(tb + 1)
        om = om_r.next()
        PTs = []
        for h in range(4):
            PT = PT_r.next()
            for mt in range(2):
                pt = k.psf.next()
                k.mm(pt, [(kmT.t[:, h, mt * 128:(mt + 1) * 128], qm.t[:, h, :])], reads=[kmT, qm])
                k.A(lambda: nc.scalar.activation(out=PT.t[:, mt, :], in_=pt.t[:], func=AF.Exp, scale=sc), reads=[pt], writes=[PT])
            PTs.append(PT)
        if pending is not None:
            pending()
        for h in (1, 3):
            for tt in range(4):
                po = k.psf.next()
                rd = rd_r.next()

                def pv():
                    for hh in (h - 1, h):
                        P_ = PTs[hh]
                        for mt in range(2):
                            nc.tensor.matmul(po.t[:, (hh % 2) * 128:(hh % 2 + 1) * 128], P_.t[:, mt, tt * 128:(tt + 1) * 128],
                                             VM.t[:, mt, hh * 128:(hh + 1) * 128], start=(mt == 0), stop=(mt == 1))
                        for mt in range(2):
                            ins = nc.tensor.matmul(po.t[:, 256 + (hh % 2):257 + (hh % 2)], P_.t[:, mt, tt * 128:(tt + 1) * 128],
                                                   onesc.t[:], start=(mt == 0), stop=(mt == 1))
                    return ins
                k.P(pv, reads=[PTs[h - 1], PTs[h], VM, onesc], writes=[po])
                k.V(lambda: nc.vector.reciprocal(out=rd.t[:, 0:2], in_=po.t[:, 256:258]), reads=[po], writes=[rd])
                k.V(lambda: nc.vector.tensor_tensor(
                    out=om.t[:, tt, (h - 1) * 128:(h + 1) * 128].rearrange("p (h d) -> p h d", d=128),
                    in0=po.t[:, 0:256].rearrange("p (h d) -> p h d", d=128),
                    in1=bc(rd.t[:, 0:2].unsqueeze(2), [128, 2, 128]), op=ALU.mult), reads=[po, rd], writes=[om])
        pending = (lambda om=om, GM=GM, t0=t0: tail(om, GM, t0))
    pending()


def phase_n1(k):
    nc = k.nc
    kcT_s = k.dscr("kcT_s", [2, 64, 256], BF16)
    vc_s = k.dscr("vc_s", [2, 2, 128, 64], BF16)
    kvcT_s = k.dram["kvcT_s"]
    W1s = k.sb("W1s", [64, 2, 32, 256], BF16)
    for kv in range(2):
        k.dma(W1s.t[:, kv, :, :], k.dram["cmp_w1_bf"][kv * 2048:(kv + 1) * 2048, :].rearrange("(pos d) hid -> d pos hid", d=64),
              reads=[k.dbuf["cmp_w1_bf"]], writes=[W1s], acc=True)
    W2s = k.sb("W2s", [128, 2, 2, 64], BF16)
    for kv in range(2):
        k.dma(W2s.t[:, kv, :, :], k.dram["cmp_w2_bf"][kv * 256:(kv + 1) * 256, :].rearrange("(mt p) d -> p mt d", p=128),
              reads=[k.dbuf["cmp_w2_bf"]], writes=[W2s], acc=True)
    PE_ = k.sb("peTs", [64, 64], BF16)
    k.dma(PE_.t[:], k.dram["c_peT_bf"], reads=[k.dbuf["c_peT_bf"]], writes=[PE_])
    XCm = k.sb("XCm", [64, 2, 2, S], BF16)
    for kv in range(2):
        for g in range(2):
            k.dma(XCm.t[:, kv, g, :], kvcT_s[kv, g * 64:(g + 1) * 64, :], writes=[XCm], acc=True)
    cb1 = k.sb("cb1", [128, 4], F32)
    HID = k.sb("HID", [128, 2, 256], BF16)
    kc = k.sb("kc", [128, 128], BF16)
    k.V(lambda: nc.vector.memset(kc.t[:], 0.0), writes=[kc])
    ss = k.sb("ssn1", [128, 8], F32)
    junk = k.sb("junkn1", [128, 64], F32)
    kcT = k.sb("kcTt", [64, 128], BF16)
    vct = k.sb("vct", [128, 64], BF16)
    for kv in range(2):
        for mt in range(2):
            pt = k.psf.next()
            k.mm(pt, [(W1s.t[:, kv, pos, mt * 128:(mt + 1) * 128], PE_.t[:, kv * 32 + pos:kv * 32 + pos + 1]) for pos in range(32)],
                 reads=[W1s, PE_], out_ap=pt.t[:, 0:1])
            k.V(lambda: nc.vector.tensor_copy(out=cb1.t[:, kv * 2 + mt:kv * 2 + mt + 1], in_=pt.t[:, 0:1]), reads=[pt], writes=[cb1])
    for kv in range(2):
        for g in range(2):
            for mt in range(2):
                pt = k.psf.next()
                k.mm(pt, [(W1s.t[:, kv, pos, mt * 128:(mt + 1) * 128], XCm.t[:, kv, g, pos:pos + 16 * 254 + 1:16]) for pos in range(32)],
                     reads=[W1s, XCm], out_ap=pt.t[:, 0:255])
                k.A(lambda: nc.scalar.activation(out=HID.t[:, mt, 0:255], in_=pt.t[:, 0:255], func=AF.Silu,
                                                 bias=cb1.t[:, kv * 2 + mt:kv * 2 + mt + 1]), reads=[pt, cb1], writes=[HID])
            for u in range(2):
                nn = 128 if u == 0 else 127
                pt = k.psf.next()
                k.mm(pt, [(HID.t[:, mt, u * 128:u * 128 + nn], W2s.t[:, kv, mt, :]) for mt in range(2)], reads=[HID, W2s],
                     out_ap=pt.t[0:nn, 0:64])
                if kv == 0:
                    k.A(lambda: nc.scalar.activation(out=junk.t[0:nn, :], in_=pt.t[0:nn, 0:64], func=AF.Square,
                                                     accum_out=ss.t[0:nn, 0:1]), reads=[pt], writes=[junk, ss])
                    k.rsqrt(ss, ss, 1.0 / 64, (slice(0, nn), slice(0, 1)))
                    k.V(lambda: nc.vector.memset(kc.t[:], 0.0), writes=[kc])
                    k.V(lambda: nc.vector.tensor_scalar(out=kc.t[0:nn, 0:64], in0=pt.t[0:nn, 0:64], scalar1=ss.t[0:nn, 0:1],
                                                        scalar2=None, op0=ALU.mult), reads=[pt, ss], writes=[kc])
                    pb = k.psb.next()
                    k.P(lambda: nc.tensor.transpose(pb.t[:, 0:128], kc.t[:], k.ident.t[:]), reads=[kc, k.ident], writes=[pb])
                    k.V(lambda: nc.vector.tensor_scalar(out=kcT.t[:], in0=pb.t[0:64, 0:128], scalar1=col(k, "g_kc")[0:64, :],
                                                        scalar2=None, op0=ALU.mult), reads=[pb, k.cols], writes=[kcT])
                    k.store(kcT_s[g, :, u * 128:(u + 1) * 128], "", kcT)
                else:
                    k.V(lambda: nc.vector.memset(vct.t[:], 0.0), writes=[vct])
                    k.V(lambda: nc.vector.tensor_copy(out=vct.t[0:nn, :], in_=pt.t[0:nn, 0:64]), reads=[pt], writes=[vct])
                    k.store(vc_s[g, u], "", vct)


def phase_n2(k):
    nc = k.nc
    for nm, shp in (("c_d0", [128, 128]), ("c_mw1", [128, 128]), ("c_mw2", [128, 128])):
        k.din(nm, shp)
    mix2T_s = k.dscr("mix2T_s", [8, 128, S], BF16)
    onsa_s = k.dscr("onsa_s", [S, 512], F32) if k.debug else None
    qn_dbg = k.dscr("qn_dbg", [32, 2, 64, 512], BF16) if k.debug else None
    dr = k.dram
    qT8 = dr["qT_s"].rearrange("j (r d) t -> (j r) d t", d=64)
    KE, KW = [], []
    for g in range(2):
        ke = k.sb(f"KE{g}", [128, S], BF16)
        k.dma(ke.t[0:64, :], dr["kTs_s"][g * 64:(g + 1) * 64, :], writes=[ke], acc=True)
        k.dma(ke.t[64:128, :], dr["c_kside_bf"][0:64, :], reads=[k.dbuf["c_kside_bf"]], writes=[ke], acc=True)
        KE.append(ke)
        kw = k.sb(f"KW{g}", [128, S], BF16)
        k.dma(kw.t[0:64, :], dr["kTw_s"][g * 64:(g + 1) * 64, :], writes=[kw], acc=True)
        k.dma(kw.t[64:128, :], dr["c_kside_bf"][64:128, :], reads=[k.dbuf["c_kside_bf"]], writes=[kw], acc=True)
        KW.append(kw)
    EB = k.sb("EB", [128, 256], BF16)
    k.dma(EB.t[64:128, :], dr["c_eblk_bf"][:, 3840:4096], reads=[k.dbuf["c_eblk_bf"]], writes=[EB])
    RAQ = dr["c_raq_bf"]
    VV = k.sb("VV", [128, 32, 256], BF16)
    k.dma(VV.t[:], dr["v_s"].rearrange("(kt p) c -> p kt c", p=128), writes=[VV])
    VS = k.sb("VS", [128, 32, 2, 65], BF16)
    VW = k.sb("VW", [128, 32, 2, 65], BF16)
    for br, vt in ((0, VS), (1, VW)):
        k.V(lambda: nc.vector.memset(vt.t[:, :, :, 64:65], 1.0), writes=[vt])
        k.V(lambda: nc.vector.tensor_copy(out=vt.t[:, :, :, 0:64],
                                          in_=VV.t[:, :, br * 128:(br + 1) * 128].rearrange("p k (g d) -> p k g d", d=64)),
            reads=[VV], writes=[vt])
    KC = k.sb("KC", [64, 2, 256], BF16)
    k.dma(KC.t[:], dr["kcT_s"].rearrange("g d n -> d g n"), writes=[KC])
    VC = k.sb("VC", [128, 2, 2, 65], BF16)
    k.V(lambda: nc.vector.memset(VC.t[:, :, :, 64:65], 1.0), writes=[VC])
    for g in range(2):
        k.dma(VC.t[:, :, g, 0:64], dr["vc_s"][g].rearrange("u p d -> p u d"), writes=[VC], acc=True)
    OV = k.sb("OV", [128, 2, 64], BF16)
    k.dma(OV.t[:], dr["c_ovl_bf"].rearrange("p (u j) -> p u j", j=64), reads=[k.dbuf["c_ovl_bf"]], writes=[OV])
    LA = k.sb("LA", [3, S], BF16); k.dma(LA.t[:], dr["c_laall_bf"], reads=[k.dbuf["c_laall_bf"]], writes=[LA])
    RAC = k.sb("RAC", [3, 1024], BF16); k.dma(RAC.t[:], dr["c_rac_bf"], reads=[k.dbuf["c_rac_bf"]], writes=[RAC])
    SHM = k.sb("SHM", [16, S], BF16); k.dma(SHM.t[:], dr["c_shm_bf"], reads=[k.dbuf["c_shm_bf"]], writes=[SHM])
    CMASK = k.sb("CMASK", [16, 512], BF16); k.dma(CMASK.t[:], dr["c_cmask_bf"], reads=[k.dbuf["c_cmask_bf"]], writes=[CMASK])
    CAUS = k.sb("CAUS", [128, 512], BF16); k.dma(CAUS.t[:], dr["c_caus_bf"], reads=[k.dbuf["c_caus_bf"]], writes=[CAUS])
    WINM = k.sb("WINM", [128, 512], BF16); k.dma(WINM.t[:], dr["c_winm_bf"], reads=[k.dbuf["c_winm_bf"]], writes=[WINM])
    D0 = k.sb("D0", [128, 128], F32); k.load(D0, dr["c_d0"], "")
    MW1 = k.sb("MW1", [128, 128], F32); k.load(MW1, dr["c_mw1"], "")
    MW2 = k.sb("MW2", [128, 128], F32); k.load(MW2, dr["c_mw2"], "")
    WNO = k.sb("WNO", [128, 4, 1024], BF16)
    wload(k, WNO, "w_nsa_o", "(k p) n -> p k n", p=128)
    ZR = k.sb("ZR", [1, 512], BF16)
    k.V(lambda: nc.vector.memset(ZR.t[:], 0.0), writes=[ZR])
    SELT_r = k.ring("SELT", [128, 128], BF16, 3)
    for t_ in SELT_r.tiles:
        k.V(lambda: nc.vector.memset(t_.t[:], 0.0), writes=[t_])

    ACC = [k.psf.tiles[4], k.psf.tiles[5]]
    GM_r = k.ring("GMn", [128, 8, 512], BF16, 2)
    gn_r = k.ring("gnn", [128, 24], F32, 4)
    oacc_r = k.ring("oacc", [128, 512], F32, 4)
    otmp_r = k.ring("otmp", [128, 256], F32, 2)
    QN_r = k.ring("QN", [128, 512], BF16, 4)
    PC_r = k.ring("PC", [128, 512], BF16, 4)
    PS_r = k.ring("PS", [128, 512], BF16, 4)
    st_r = k.ring("stn", [128, 32], F32, 3)
    stB_r = k.ring("stb", [128, 32], F32, 2)
    imp_r = k.ring("imp", [128, 192], F32, 3)
    onb_r = k.ring("onb", [128, 512], BF16, 1)
    mo_r = k.ring("mon", [128, 512], BF16, 2)

    def evac_branch(po, st, gnt, br, g, oacc, first):
        pov = po.t[:, 0:260].rearrange("p (h e) -> p h e", e=65)
        k.V(lambda: nc.vector.tensor_scalar(out=st.t[:, 0:4], in0=pov[:, :, 64], scalar1=1e-30, scalar2=None, op0=ALU.add),
            reads=[po], writes=[st])
        yield
        k.V(lambda: nc.vector.reciprocal(out=st.t[:, 0:4], in_=st.t[:, 0:4]), reads=[st], writes=[st])
        yield
        k.V(lambda: nc.vector.tensor_tensor(out=st.t[:, 4:8], in0=st.t[:, 0:4], in1=gnt.t[:, br * 8 + g * 4:br * 8 + g * 4 + 4],
                                            op=ALU.mult), reads=[st, gnt], writes=[st])
        yield
        dst = oacc.t[:, g * 256:(g + 1) * 256].rearrange("p (h d) -> p h d", d=64)
        if first:
            k.V(lambda: nc.vector.tensor_tensor(out=dst, in0=pov[:, :, 0:64], in1=bc(st.t[:, 4:8].unsqueeze(2), [128, 4, 64]),
                                                op=ALU.mult), reads=[po, st], writes=[oacc])
        else:
            ot = otmp_r.next()
            k.V(lambda: nc.vector.tensor_tensor(out=ot.t[:].rearrange("p (h d) -> p h d", d=64), in0=pov[:, :, 0:64],
                                                in1=bc(st.t[:, 4:8].unsqueeze(2), [128, 4, 64]), op=ALU.mult),
                reads=[po, st], writes=[ot])
            yield
            k.V(lambda: nc.vector.tensor_tensor(out=oacc.t[:, g * 256:(g + 1) * 256], in0=oacc.t[:, g * 256:(g + 1) * 256],
                                                in1=ot.t[:], op=ALU.add), reads=[ot, oacc], writes=[oacc])
        yield

    QN2_r = k.ring("QN2", [128, 512], BF16, 3)
    QN2s = {}
    psA = [k.psf.tiles[2], k.psf.tiles[3]]
    psB3 = Tl(k.psb.tiles[1].t[:].bitcast(F32), "psB3")
    psB3.b = k.psb.tiles[1].b
    psB = Ring(k.psf.tiles[0:2] + [psB3])
    psbT = Ring([k.psb.tiles[0]])

    def stageA(i, g, QN, oacc, gnt):
        tok0 = i * 128
        k.dma(QN.t[0:64, :].rearrange("p (h t) -> p h t", t=128),
              qT8[4 * g:4 * g + 4, :, tok0:tok0 + 128].rearrange("h d t -> d h t"), writes=[QN], acc=True)
        k.dma(QN.t[124:128, :], RAQ[(i * 2 + g) * 4:(i * 2 + g) * 4 + 4, :], reads=[k.dbuf["c_raq_bf"]], writes=[QN], acc=True)
        yield
        nu = 2 if i >= 16 else 1
        PCs, nns = [], []
        for u in range(nu):
            dc = i - 16 * u
            nn = min(128 if u == 0 else 127, 8 * dc + 7)
            nns.append(nn)
            pt = psA[u]
            k.mm(pt, [(KC.t[:, g, u * 128:u * 128 + nn], QN.t[0:64, :]),
                      (LA.t[:, dc * 128:dc * 128 + nn], RAC.t[:, g * 512:(g + 1) * 512]),
                      (SHM.t[:, dc * 128:dc * 128 + nn], CMASK.t[:, :])],
                 reads=[KC, QN, LA, RAC, SHM, CMASK], out_ap=pt.t[0:nn, :])
            yield
            PC = PC_r.next()
            k.A(lambda: nc.scalar.activation(out=PC.t[0:nn, :], in_=pt.t[0:nn, :], func=AF.Exp, scale=0.125),
                reads=[pt], writes=[PC])
            yield
            PCs.append(PC)
        for _ in range(4):
            yield
        po, pi = psA[0], psA[1]

        def pvc():
            for hl in range(4):
                for u in range(nu):
                    nc.tensor.matmul(po.t[:, hl * 65:(hl + 1) * 65], PCs[u].t[0:nns[u], hl * 128:(hl + 1) * 128],
                                     VC.t[0:nns[u], u, g, :], start=(u == 0), stop=(u == nu - 1))
                for u in range(nu):
                    ins = nc.tensor.matmul(pi.t[:, hl * 64:(hl + 1) * 64], PCs[u].t[0:nns[u], hl * 128:(hl + 1) * 128],
                                           OV.t[0:nns[u], u, :], start=(u == 0), stop=(u == nu - 1))
            return ins
        k.P(pvc, reads=PCs + [VC, OV], writes=[po, pi])
        yield
        st = st_r.next()
        for _ in evac_branch(po, st, gnt, 0, g, oacc, True):
            yield
        imp = imp_r.next()
        k.V(lambda: nc.vector.tensor_scalar(out=imp.t[:, 0:64], in0=pi.t[:, 0:64], scalar1=st.t[:, 0:1], scalar2=None,
                                            op0=ALU.mult), reads=[pi, st], writes=[imp])
        yield
        for hl in range(1, 4):
            k.V(lambda: nc.vector.scalar_tensor_tensor(out=imp.t[:, 0:64], in0=pi.t[:, hl * 64:(hl + 1) * 64],
                                                       scalar=st.t[:, hl:hl + 1], in1=imp.t[:, 0:64],
                                                       op0=ALU.mult, op1=ALU.add), reads=[pi, st, imp], writes=[imp])
            yield
        c0 = 64 - 2 * i
        k.V(lambda: nc.vector.tensor_tensor(out=imp.t[:, 64:128], in0=imp.t[:, 0:64], in1=MW1.t[:, c0:c0 + 64], op=ALU.mult),
            reads=[imp, MW1], writes=[imp])
        yield
        k.V(lambda: nc.vector.tensor_tensor(out=imp.t[:, 64:128], in0=imp.t[:, 64:128], in1=MW2.t[:, c0:c0 + 64], op=ALU.add),
            reads=[imp, MW2], writes=[imp])
        yield
        k.V(lambda: nc.vector.memset(imp.t[:, 64:65], 1e9), reads=[imp], writes=[imp])
        yield
        k.V(lambda: nc.vector.max(out=st.t[:, 16:24], in_=imp.t[:, 64:128]), reads=[imp], writes=[st])
        yield
        k.V(lambda: nc.vector.match_replace(out=imp.t[:, 128:192], in_to_replace=st.t[:, 16:24], in_values=imp.t[:, 64:128],
                                            imm_value=-3.0e38), reads=[imp, st], writes=[imp])
        yield
        k.V(lambda: nc.vector.max(out=st.t[:, 24:32], in_=imp.t[:, 128:192]), reads=[imp], writes=[st])
        yield
        k.V(lambda: nc.vector.tensor_reduce(out=st.t[:, 8:9], in_=st.t[:, 24:32], axis=AX.X, op=ALU.min), reads=[st], writes=[st])
        yield
        SELT = SELT_r.next()
        k.V(lambda: nc.vector.tensor_scalar(out=SELT.t[:, 64:128], in0=imp.t[:, 64:128], scalar1=st.t[:, 8:9],
                                            scalar2=8.0 * NEGB, op0=ALU.is_lt, op1=ALU.mult), reads=[imp, st], writes=[SELT])
        yield
        for _ in range(9):
            yield
        pb = psbT.next()
        k.P(lambda: nc.tensor.transpose(pb.t[:, 0:128], SELT.t[:], k.ident.t[:]), reads=[SELT, k.ident], writes=[pb])
        k.V(lambda: nc.vector.tensor_copy(out=QN.t[64:124, :].rearrange("p (h t) -> p h t", t=128),
                                          in_=bc(pb.t[64:124, 0:128].unsqueeze(1), [60, 4, 128])), reads=[pb], writes=[QN])
        if i >= 30:
            QN2 = QN2_r.next()
            k.V(lambda: nc.vector.tensor_copy(out=QN2.t[64:128, :].rearrange("p (h t) -> p h t", t=128),
                                              in_=bc(pb.t[64:128, 0:128].unsqueeze(1), [64, 4, 128])), reads=[pb], writes=[QN2])
            QN2s[(i, g)] = QN2
        yield

    def stageB(i, g, QN, oacc, gnt, filler, late=None):
        k0 = max(0, i - 4)
        steps = [(0, kt) for kt in range(i + 1)] + [(1, kt) for kt in range(k0, i + 1)]
        pend_pv = []
        nstep = 0
        late = late if late is not None else []
        for (br, kt) in steps:
            pt = psB.next()
            if br == 0:
                pairs = [(KE[g].t[:, kt * 128:(kt + 1) * 128], QN.t[:, :])]
                rd_ = [KE[g], QN]
                if kt >= 30:
                    QN2 = QN2s[(i, g)]
                    pairs.append((EB.t[64:128, (kt - 30) * 128:(kt - 29) * 128], QN2.t[64:128, :])); rd_ += [EB, QN2]
            else:
                pairs = [(KW[g].t[:, kt * 128:(kt + 1) * 128], QN.t[:, :])]
                rd_ = [KW[g], QN]
            if kt == i:
                pairs.append((k.ident.t[:], CAUS.t[:])); rd_ += [k.ident, CAUS]
            if br == 1 and kt == i - 4:
                pairs.append((k.ident.t[:], WINM.t[:])); rd_ += [k.ident, WINM]
            k.mm(pt, pairs, reads=rd_)
            PS = PS_r.next()
            k.A(lambda: nc.scalar.activation(out=PS.t[:], in_=pt.t[:], func=AF.Exp, scale=0.125), reads=[pt], writes=[PS])
            if len(pend_pv) >= 2:
                pend_pv.pop(0)()

            def pv(br=br, kt=kt, PS=PS):
                first = (kt == 0) if br == 0 else (kt == k0)
                Vt = VS if br == 0 else VW

                def f():
                    if first:
                        nc.tensor.matmul(ACC[br].t[:, 0:260], ZR.t[0:1, 0:128], ZR.t[0:1, 0:260], start=True, stop=False)
                    for hl in range(4):
                        ins = nc.tensor.matmul(ACC[br].t[:, hl * 65:(hl + 1) * 65], PS.t[:, hl * 128:(hl + 1) * 128],
                                               Vt.t[:, kt, g, :], start=False, stop=(kt == i))
                    return ins
                k.P(f, reads=[PS, Vt, ZR], writes=[ACC[br]])
            pend_pv.append(pv)
            nstep += 1
            if late and nstep == 4:
                late.pop(0)()
            if filler is not None:
                for _ in range(3):
                    filler()
        while pend_pv:
            pend_pv.pop(0)()
        while late:
            late.pop(0)()
        for br in range(2):
            stx = stB_r.next()
            for _ in evac_branch(ACC[br], stx, gnt, 1 + br, g, oacc, False):
                pass

    order = []
    lo_, hi_ = 0, NT - 1
    while lo_ <= hi_:
        order.append(hi_)
        if lo_ != hi_:
            order.append(lo_)
        hi_ -= 1
        lo_ += 1
    pairs_ig = [(i, g) for i in order for g in range(2)]
    ctx = {}

    def get_ctx(i):
        if i not in ctx:
            gnt = gn_r.next(); k.load(gnt, dr["gn_s"][i * 128:(i + 1) * 128, :], "")
            ctx[i] = (oacc_r.next(), gnt)
        return ctx[i]

    QNs = {}
    gens = {}

    def make_A(p):
        if p in gens or p >= len(pairs_ig):
            return
        i, g = pairs_ig[p]
        oacc, gnt = get_ctx(i)
        QNs[p] = QN_r.next()
        gens[p] = stageA(i, g, QNs[p], oacc, gnt)

    def chain(ps):
        def step():
            for p_ in ps:
                g_ = gens.get(p_)
                if g_ is None:
                    continue
                try:
                    next(g_)
                    return
                except StopIteration:
                    continue
        return step

    onTs = [k.sb(f"onT{tb}", [128, 4, 512], BF16) for tb in range(8)]
    done_cnt = [0] * 8
    make_A(0)
    for _ in gens[0]:
        pass
    late_fns = []
    for p, (i, g) in enumerate(pairs_ig):
        tb, tt = i // 4, i % 4
        t0 = tb * 512
        oacc, gnt = get_ctx(i)
        make_A(p + 1)
        make_A(p + 2)
        issue_late_conv(k, 1)
        stageB(i, g, QNs[p], oacc, gnt, chain([p + 1, p + 2]), late_fns)
        if p + 1 in gens:
            for _ in gens[p + 1]:
                pass
        if g == 1:
            def finalize(i=i, tb=tb, tt=tt, t0=t0, oacc=oacc):
                onT = onTs[tb]
                onb = onb_r.next()
                k.A(lambda: nc.scalar.copy(out=onb.t[:], in_=oacc.t[:]), reads=[oacc], writes=[onb])
                pb = transpose_to(k, None, None, onb, 4, ring=psbT)
                k.A(lambda: nc.scalar.copy(out=onT.t[:, :, tt * 128:(tt + 1) * 128],
                                           in_=pb.t[:, 0:512].rearrange("p (k t) -> p k t", t=128)), reads=[pb], writes=[onT])
                done_cnt[tb] += 1
                if done_cnt[tb] == 4:
                    GM = GM_r.next(); k.load(GM, dr["gmT_s"][8:16, :, t0:t0 + 512].rearrange("m p t -> p m t"), "")
                    for e in range(8):
                        pt = psB.next()
                        k.mm(pt, [(WNO.t[:, kk, e * 128:(e + 1) * 128], onT.t[:, kk, :]) for kk in range(4)], reads=[WNO, onT])
                        mo = mo_r.next()
                        k.V(lambda: nc.vector.tensor_tensor(out=mo.t[:], in0=pt.t[:], in1=GM.t[:, e, :], op=ALU.mult),
                            reads=[pt, GM], writes=[mo])
                        k.store(mix2T_s[e, :, t0:t0 + 512], "", mo)
            late_fns.append(finalize)
            del ctx[i]
    while late_fns:
        late_fns.pop(0)()
    issue_late_conv(k, 1000)


def phase_d1(k):
    nc = k.nc
    x = k.dram["x"]
    h_s = k.dscr("h_s", [S, D], F32)
    hnT_s = k.dscr("hnT_s", [8, 128, S], BF16)
    WOUT = k.sb("WOUT", [128, 8, D], BF16)
    wload(k, WOUT, "w_out", "(k p) n -> p k n", p=128)
    MA_r = k.ring("MxA", [128, 8, 512], BF16, 3)
    MB_r = k.ring("MxB", [128, 8, 512], BF16, 4)
    xt_r = k.ring("xtd", [128, D], F32, 2)
    ht_r = k.ring("htd", [128, D], F32, 2)
    hs_r = k.ring("hsd", [128, D], BF16, 3)
    hnT_r = k.ring("hnTd", [128, 8, 512], BF16, 2)
    ss_r = k.ring("ssd", [128, 8], F32, 2)
    junk = k.sb("junkd", [128, D], BF16)

    def load_mix(tb):
        Ms = []
        for j, nm in enumerate(("mix1T_s", "mix2T_s", "mix3T_s")):
            M = (MA_r if j == 0 else MB_r).next()
            k.load(M, k.dram[nm][:, :, tb * 512:(tb + 1) * 512].rearrange("m p t -> p m t"), "")
            Ms.append(M)
        return Ms
    def sum_mix(Ms):
        k.V(lambda: nc.vector.tensor_tensor(out=Ms[0].t[:], in0=Ms[0].t[:], in1=Ms[1].t[:], op=ALU.add), reads=[Ms[0], Ms[1]], writes=[Ms[0]])
        k.V(lambda: nc.vector.tensor_tensor(out=Ms[0].t[:], in0=Ms[0].t[:], in1=Ms[2].t[:], op=ALU.add), reads=[Ms[0], Ms[2]], writes=[Ms[0]])
        return Ms

    Ms_next = sum_mix(load_mix(0))
    Ms_ld = load_mix(1)
    pend = []
    gtt = [0]
    for tb in range(8):
        t0 = tb * 512
        Ms = Ms_next
        mixed = Ms[0]
        hnT = hnT_r.next()
        for tt in range(4):
            tok0 = t0 + tt * 128
            xt = xt_r.next(); k.load(xt, x[tok0:tok0 + 128, :], "")
            ht = ht_r.next()
            for nh in range(2):
                pt = k.psf.next()
                k.mm(pt, [(mixed.t[:, kk, tt * 128:(tt + 1) * 128], WOUT.t[:, kk, nh * 512:(nh + 1) * 512]) for kk in range(8)],
                     reads=[mixed, WOUT])
                k.V(lambda: nc.vector.tensor_tensor(out=ht.t[:, nh * 512:(nh + 1) * 512], in0=pt.t[:], in1=xt.t[:, nh * 512:(nh + 1) * 512],
                                                    op=ALU.add), reads=[pt, xt], writes=[ht])
            gtt[0] += 1
            while pend and pend[0][0] <= gtt[0]:
                pend.pop(0)[1]()
            if tt == 1 and tb + 1 < 8:
                Ms_next = sum_mix(Ms_ld)
            if tt == 2 and tb + 2 < 8:
                Ms_ld = load_mix(tb + 2)
            k.store(h_s[tok0:tok0 + 128, :], "", ht)
            ss = ss_r.next()
            k.A(lambda: nc.scalar.activation(out=junk.t[:], in_=ht.t[:], func=AF.Square, accum_out=ss.t[:, 0:1]),
                reads=[ht], writes=[junk, ss])
            k.rsqrt(ss, ss, 1.0 / D, (slice(None), slice(0, 1)))
            hs = hs_r.next()
            k.V(lambda: nc.vector.tensor_scalar(out=hs.t[:], in0=ht.t[:], scalar1=ss.t[:, 0:1], scalar2=None, op0=ALU.mult),
                reads=[ht, ss], writes=[hs])
            def tailf(hs=hs, hnT=hnT, tt=tt):
                pb = transpose_to(k, None, None, hs, 8)
                k.V(lambda: nc.vector.tensor_tensor(out=hnT.t[:, :, tt * 128:(tt + 1) * 128], in0=pb.t[:].rearrange("p (k t) -> p k t", t=128),
                                                    in1=bc(col(k, "g_ffn", 0, 8).unsqueeze(2), [128, 8, 128]), op=ALU.mult),
                    reads=[pb, k.cols], writes=[hnT])
            pend.append((gtt[0] + 2, tailf))
        pend.append((gtt[0] + 2, lambda hnT=hnT, t0=t0: k.store(hnT_s[:, :, t0:t0 + 512].rearrange("m p t -> p m t"), "", hnT)))
    while pend:
        pend.pop(0)[1]()


def prefetch_wup(k):
    k.WUP = k.sb("WUP", [128, 8, 2 * FFH], BF16)


def issue_wup(k):
    for kk in range(8):
        k.dma(k.WUP.t[:, kk, :], k.dram["w_ffn_up_bf"][kk * 128:(kk + 1) * 128, :], reads=[k.dbuf["w_ffn_up_bf"]],
              writes=[k.WUP], acc=True)


def phase_d2a(k):
    nc = k.nc
    aT_s = k.dscr("aT_s", [22, 128, S], BF16)
    hnT_s = k.dram["hnT_s"]
    WUP = k.WUP
    DGF = k.sb("DGF", [128, 88, 128], BF16)
    k.V(lambda: nc.vector.tensor_tensor(out=DGF.t[:], in0=bc(k.identf.t[:].unsqueeze(1), [128, 88, 128]),
                                        in1=bc(col(k, "cw_ffn", 0, 88).unsqueeze(2), [128, 88, 128]), op=ALU.mult),
        reads=[k.identf, k.cols], writes=[DGF])
    UH = k.sb("UH", [128, 44, 2], BF16)
    UHb = [Buf(f"UH{m}") for m in range(44)]
    k.V(lambda: nc.vector.memset(UH.t[:], 0.0), writes=UHb)
    hnT_r = k.ring("hnTf", [128, 8, 512], BF16, 2)
    u_r = k.ring("usb", [128, 512], BF16, 4)
    ga_r = k.ring("ga", [128, 512], BF16, 3)
    cv_r = k.ring("cv", [128, 512], F32, 3)
    AT_r = k.ring("AT", [128, 22, 512], BF16, 2)
    hnT_next = hnT_r.next(); k.load(hnT_next, hnT_s[:, :, 0:512].rearrange("m p t -> p m t"), "")
    for tb in range(8):
        t0 = tb * 512
        hnT = hnT_next
        if tb + 1 < 8:
            hnT_next = hnT_r.next(); k.load(hnT_next, hnT_s[:, :, t0 + 512:t0 + 1024].rearrange("m p t -> p m t"), "")
        AT = AT_r.next()
        order = [m for c in range(22) for m in (c, c + 22)]
        gas = {}
        pending = None
        for m in order:
            pt = k.psf.next()
            k.mm(pt, [(WUP.t[:, kk, m * 128:(m + 1) * 128], hnT.t[:, kk, :]) for kk in range(8)], reads=[WUP, hnT])
            u = u_r.next()
            if m < 22:
                k.A(lambda: nc.scalar.copy(out=u.t[:], in_=pt.t[:]), reads=[pt], writes=[u])
            else:
                k.V(lambda: nc.vector.tensor_copy(out=u.t[:], in_=pt.t[:]), reads=[pt], writes=[u])
            if pending is not None:
                pending()

            def conv(m=m, u=u):
                p2 = k.psf.next()

                def f():
                    d0, d1 = (DGF.t[:, kk * 44 + m, :] for kk in range(2))
                    nc.tensor.matmul(p2.t[:, 1:512], d1, u.t[:, 0:511], start=True, stop=False)
                    nc.tensor.matmul(p2.t[:, 0:1], d1, UH.t[:, m, 1:2], start=False, stop=False)
                    nc.tensor.matmul(p2.t[:, 2:512], d0, u.t[:, 0:510], start=False, stop=False)
                    return nc.tensor.matmul(p2.t[:, 0:2], d0, UH.t[:, m, 0:2], start=False, stop=True)
                k.P(f, reads=[DGF, u, UHb[m]], writes=[p2])
                k.G(lambda: nc.gpsimd.tensor_copy(out=UH.t[:, m, :], in_=u.t[:, 510:512]), reads=[u], writes=[UHb[m]])
                cv = cv_r.next()
                k.V(lambda: nc.vector.scalar_tensor_tensor(out=cv.t[:], in0=u.t[:], scalar=col(k, "cw_ffn", 2 * 44 + m),
                                                           in1=p2.t[:], op0=ALU.mult, op1=ALU.add),
                    reads=[u, k.cols, p2], writes=[cv])
                if m < 22:
                    ga = ga_r.next()
                    k.A(lambda: nc.scalar.activation(out=ga.t[:], in_=cv.t[:], func=AF.Silu, bias=col(k, "cb_ffn", m)),
                        reads=[cv, k.cols], writes=[ga])
                    gas[m] = ga
                else:
                    ga = gas.pop(m - 22)
                    k.V(lambda: nc.vector.scalar_tensor_tensor(out=AT.t[:, m - 22, :], in0=cv.t[:], scalar=col(k, "cb_ffn", m),
                                                               in1=ga.t[:], op0=ALU.add, op1=ALU.mult),
                        reads=[cv, k.cols, ga], writes=[AT])
            pending = conv
        pending()
        k.store(aT_s[:, :, t0:t0 + 512].rearrange("m p t -> p m t"), "", AT)


def phase_d2b(k):
    nc = k.nc
    out = k.dscr("out", [S, D], F32, out=True)
    aT_s, h_s = k.dram["aT_s"], k.dram["h_s"]
    WDN = k.sb("WDN", [128, 22, D], BF16)
    WDNb = [Buf("WDN0"), Buf("WDN1")]
    for nh in range(2):
        k.dma(WDN.t[:, :, nh * 512:(nh + 1) * 512],
              k.dram["w_ffn_down_bf"].rearrange("(c p) n -> p c n", p=128)[:, :, nh * 512:(nh + 1) * 512],
              reads=[k.dbuf["w_ffn_down_bf"]], writes=[WDNb[nh]])
    AT_r = k.ring("ATb", [128, 22, 512], BF16, 2)
    ht_r = k.ring("htb", [128, D], F32, 3)
    AT_next = AT_r.next(); k.load(AT_next, aT_s[:, :, 0:512].rearrange("m p t -> p m t"), "")
    for tb in range(8):
        t0 = tb * 512
        AT = AT_next
        if tb + 1 < 8:
            AT_next = AT_r.next(); k.load(AT_next, aT_s[:, :, t0 + 512:t0 + 1024].rearrange("m p t -> p m t"), "")
        for tt in range(4):
            tok0 = t0 + tt * 128
            ht = ht_r.next(); k.load(ht, h_s[tok0:tok0 + 128, :], "")
            for nh in range(2):
                pt = k.psf.next()
                k.mm(pt, [(AT.t[:, c, tt * 128:(tt + 1) * 128], WDN.t[:, c, nh * 512:(nh + 1) * 512]) for c in range(22)],
                     reads=[AT, WDNb[nh]])
                k.V(lambda: nc.vector.tensor_tensor(out=ht.t[:, nh * 512:(nh + 1) * 512], in0=pt.t[:], in1=ht.t[:, nh * 512:(nh + 1) * 512],
                                                    op=ALU.add), reads=[pt, ht], writes=[ht])
            k.store(out[tok0:tok0 + 128, :], "", ht)


def build_program(inputs, debug=False, phases="abcde", batch=0):
    p = {n: np.asarray(v) for n, v in inputs.items()}
    cols, rows = _mk_tables(p)
    consts = _consts()
    k = K(debug=debug)
    k.ncols, k.nrows = cols.shape[1], rows.shape[1]
    phase_setup(k)
    base = len(k._guards)
    if "a" in phases:
        phase_a(k)
        k.f.barrier()
        k.free_phase(base)
    k.store_on_pq = True
    if "b" in phases:
        phase_b(k)
        k.f.barrier()
        k.free_phase(base)
    if "n" in phases:
        phase_n1(k)
        k.f.barrier()
        k.free_phase(base)
        phase_n2(k)
        k.f.barrier()
        k.free_phase(base)
    base2 = base
    if "d" in phases:
        prefetch_wup(k)
        base2 = base + 1
    if "m" in phases:
        phase_m(k)
        k.f.barrier()
        k.free_phase(base2)
    if "d" in phases:
        if "m" not in phases:
            issue_wup(k)
        phase_d1(k)
        k.f.barrier()
        k.free_phase(base2)
        phase_d2a(k)
        k.f.barrier()
        k.free_phase(base)
        phase_d2b(k)
        k.f.barrier()
        k.free_phase(base)
        k.store_on_pq = False
    k.f.barrier()
    shared = {"cols": cols, "rows": rows, "w_in": np.ascontiguousarray(p["w_in"][0]),
              "w_ssd_o": np.ascontiguousarray(p["w_ssd_o"][0]),
              "w_nsa_o": np.ascontiguousarray(p["w_nsa_o"][0]),
              "w_out": np.ascontiguousarray(p["w_out"][0]), "w_ffn_up": np.ascontiguousarray(p["w_ffn_up"][0]),
              "w_ffn_down": np.ascontiguousarray(p["w_ffn_down"][0]),
              "w_mem_kv": np.ascontiguousarray(p["w_mem_kv"][0]), "w_mem_o": np.ascontiguousarray(p["w_mem_o"][0])}
    shared["cmp_w1"] = np.ascontiguousarray(p["nsa_cmp_w1"][0]).reshape(4096, 256)
    shared["cmp_w2"] = np.ascontiguousarray(p["nsa_cmp_w2"][0]).reshape(512, 64)
    shared["c_peT"] = np.ascontiguousarray(p["nsa_cmp_pe"][0].transpose(2, 0, 1).reshape(64, 64))
    shared.update(consts)
    k.shared = shared
    in_map = dict(shared)
    in_map["x"] = np.ascontiguousarray(p["x"][batch])
    in_map["mem"] = np.ascontiguousarray(p["mem"][batch])
    in_map = {n: v for n, v in in_map.items() if n in k.dram}
    return k, in_map


def kernel(**inputs):
    k, _ = build_program(inputs, debug=False, phases="abnmd", batch=0)
    x = np.asarray(inputs["x"], np.float32)
    mem = np.asarray(inputs["mem"], np.float32)
    in_maps = []
    for b in range(8):
        m = {n: v for n, v in k.shared.items() if n in k.dram}
        m["x"] = np.ascontiguousarray(x[b])
        m["mem"] = np.ascontiguousarray(mem[b])
        in_maps.append(m)
    res = run_bass_kernel_spmd(k.nc, in_maps, core_ids=list(range(8)))
    return np.stack([np.asarray(r["out"], np.float32) for r in res.results], axis=0)
```

```python
import numpy as np
import ml_dtypes
import concourse.bass as bass
import concourse.mybir as mybir
from concourse.bass_utils import run_bass_kernel_spmd

F32 = mybir.dt.float32
BF16 = mybir.dt.bfloat16
ALU = mybir.AluOpType
AF = mybir.ActivationFunctionType
AX = mybir.AxisListType

S = 4096
D = 1024
NT = 32
INW = 7464
O_Z, O_XBC, O_DT, O_QN, O_KV, O_GN, O_QM, O_GM = 0, 1024, 2560, 2576, 3088, 3856, 3880, 4392
FFH = 2816
EPS = 1e-6
NEGB = -30000.0


class Buf:
    __slots__ = ("name", "w", "r")

    def __init__(self, name=""):
        self.name = name
        self.w = []
        self.r = {}


class Eng:
    def __init__(self, fw, name, handle, is_dma=False, nsem=1):
        self.fw = fw
        self.name = name
        self.h = handle
        self.is_dma = is_dma
        self.sems = [fw.nc.alloc_semaphore(f"s_{name}_{i}") for i in range(nsem)]
        self.count = 0
        self.waited = {}

    def wait(self, ev):
        if ev is None:
            return
        sem, val = ev
        k = id(sem)
        if self.waited.get(k, 0) >= val:
            return
        self.waited[k] = val
        self.h.wait_ge(sem, val)

    def signal(self, inst):
        if self.is_dma:
            n = len(self.sems)
            i = self.count
            sem = self.sems[i % n]
            inst.then_inc(sem, 16)
            self.count += 1
            return (sem, 16 * (i // n + 1))
        self.count += 1
        inst.then_inc(self.sems[0], 1)
        return (self.sems[0], self.count)

    def pre_dma(self):
        n = len(self.sems)
        i = self.count
        if i >= n:
            self.wait((self.sems[i % n], 16 * (i // n)))


class FW:
    def __init__(self, nc):
        self.nc = nc
        self.pe = Eng(self, "pe", nc.tensor)
        self.dve = Eng(self, "dve", nc.vector)
        self.act = Eng(self, "act", nc.scalar)
        self.pool = Eng(self, "pool", nc.gpsimd)
        self.sp = Eng(self, "sp", nc.sync, is_dma=True, nsem=12)
        self.pq = Eng(self, "pq", nc.gpsimd, is_dma=True, nsem=8)
        self.pq.waited = self.pool.waited
        self.engs = [self.pe, self.dve, self.act, self.pool, self.sp, self.pq]

    def _deps(self, eng, reads, writes, acc=False):
        for b in reads:
            for ev in b.w:
                eng.wait(ev)
        for b in writes:
            if not acc:
                for ev in b.w:
                    eng.wait(ev)
            for ev in b.r.values():
                eng.wait(ev)

    def _commit(self, ev, reads, writes, acc=False):
        for b in reads:
            b.r[id(ev[0])] = ev
        for b in writes:
            b.w = (b.w + [ev]) if acc else [ev]
            b.r = {}

    def op(self, eng, fn, reads=(), writes=()):
        self._deps(eng, reads, writes)
        inst = fn()
        ev = eng.signal(inst)
        self._commit(ev, reads, writes)
        return ev

    def dma(self, q, out, in_, reads=(), writes=(), acc=False, inflight=None, **kw):
        q.pre_dma()
        if inflight is not None and q.count - inflight >= 0:
            j = q.count - inflight
            n_ = len(q.sems)
            q.wait((q.sems[j % n_], 16 * (j // n_ + 1)))
        self._deps(q, reads, writes, acc)
        inst = q.h.dma_start(out=out, in_=in_, **kw)
        ev = q.signal(inst)
        self._commit(ev, reads, writes, acc)
        return ev

    def barrier(self):
        evs = []
        for e in self.engs:
            if e.is_dma:
                n = len(e.sems)
                for j in range(min(n, e.count)):
                    last = ((e.count - 1 - j) // n) * n + j
                    evs.append((e.sems[j], 16 * (last // n + 1)))
            elif e.count > 0:
                evs.append((e.sems[0], e.count))
        for e in (self.pe, self.dve, self.act, self.pool, self.sp):
            for ev in evs:
                e.wait(ev)


class Tl:
    def __init__(self, t, name):
        self.t = t
        self.b = Buf(name)

    def __getitem__(self, idx):
        return self.t[idx]


class Ring:
    def __init__(self, tiles):
        self.tiles = tiles
        self.i = 0

    def next(self):
        t = self.tiles[self.i % len(self.tiles)]
        self.i += 1
        return t


class K:
    def __init__(self, debug=False):
        self.debug = debug
        self.nc = bass.Bass("TRN2", target_bir_lowering=False)
        self.f = FW(self.nc)
        self.dram = {}
        self.dbuf = {}
        self._n = 0
        self._guards = []

    def din(self, name, shape, dt=F32):
        self.dram[name] = self.nc.dram_tensor(name, list(shape), dt, kind="ExternalInput").ap()
        self.dbuf[name] = Buf(name)
        return self.dram[name]

    def dscr(self, name, shape, dt, out=False):
        kind = "ExternalOutput" if (out or self.debug) else "Internal"
        self.dram[name] = self.nc.dram_tensor(name, list(shape), dt, kind=kind).ap()
        self.dbuf[name] = Buf(name)
        return self.dram[name]

    def sb(self, name, shape, dt):
        self._n += 1
        g = self.nc.sbuf_tensor(f"{name}_{self._n}", list(shape), dt)
        t = g.__enter__()
        self._guards.append(g)
        return Tl(t, name)

    def free_phase(self, keep=0):
        while len(self._guards) > keep:
            self._guards.pop().__exit__(None, None, None)

    def ring(self, name, shape, dt, n):
        return Ring([self.sb(f"{name}{i}", shape, dt) for i in range(n)])

    def ps(self, name, shape, dt=F32):
        self._n += 1
        return Tl(self.nc.alloc_psum_tensor(f"{name}_{self._n}", list(shape), dt), name)

    def V(self, fn, reads=(), writes=()):
        return self.f.op(self.f.dve, fn, [t.b if isinstance(t, Tl) else t for t in reads],
                         [t.b if isinstance(t, Tl) else t for t in writes])

    def A(self, fn, reads=(), writes=()):
        return self.f.op(self.f.act, fn, [t.b if isinstance(t, Tl) else t for t in reads],
                         [t.b if isinstance(t, Tl) else t for t in writes])

    def G(self, fn, reads=(), writes=()):
        return self.f.op(self.f.pool, fn, [t.b if isinstance(t, Tl) else t for t in reads],
                         [t.b if isinstance(t, Tl) else t for t in writes])

    def P(self, fn, reads=(), writes=()):
        return self.f.op(self.f.pe, fn, [t.b if isinstance(t, Tl) else t for t in reads],
                         [t.b if isinstance(t, Tl) else t for t in writes])

    def dma(self, out, in_, reads=(), writes=(), q=None, **kw):
        q = q or self.f.sp
        return self.f.dma(q, out, in_, [t.b if isinstance(t, Tl) else t for t in reads],
                          [t.b if isinstance(t, Tl) else t for t in writes], **kw)

    def load(self, tl, src_ap, src_name, idx=slice(None), cast=False, **kw):
        q = self.f.pq if cast else self.f.sp
        return self.dma(tl.t[idx], src_ap, reads=[], writes=[tl], q=q, **kw)

    def store(self, dst_ap, dst_name, tl, idx=slice(None), **kw):
        q = self.f.pq if getattr(self, "store_on_pq", False) else None
        return self.dma(dst_ap, tl.t[idx], reads=[tl], writes=[], q=q, **kw)

    def mm(self, out_ps, pairs, reads, out_ap=None):
        nc = self.nc
        oap = out_ap if out_ap is not None else out_ps.t[:]

        def fn():
            n = len(pairs)
            for i, (l, r) in enumerate(pairs):
                ins = nc.tensor.matmul(oap, l, r, start=(i == 0), stop=(i == n - 1))
            return ins
        return self.P(fn, reads=reads, writes=[out_ps])

    def rsqrt(self, out_tl, in_tl, scale, shape_idx=slice(None)):
        nc = self.nc
        psl = shape_idx[0] if isinstance(shape_idx, tuple) else slice(None)
        self.A(lambda: nc.scalar.activation(out=out_tl.t[shape_idx], in_=in_tl.t[shape_idx], func=AF.Ln,
                                            scale=scale, bias=self.c_eps.t[psl, 0:1]),
               reads=[in_tl, self.c_eps], writes=[out_tl])
        self.A(lambda: nc.scalar.activation(out=out_tl.t[shape_idx], in_=out_tl.t[shape_idx], func=AF.Exp,
                                            scale=-0.5),
               reads=[out_tl], writes=[out_tl])


COLS = {}
ROWS = {}


def _mk_tables(p):
    cols, rows = [], []

    def addc(name, arr):
        arr = np.asarray(arr, np.float32).reshape(128, -1)
        COLS[name] = (sum(a.shape[1] for a in cols), arr.shape[1])
        cols.append(arr)

    def addr(name, vec):
        vec = np.asarray(vec, np.float32).reshape(1, -1)
        ROWS[name] = (sum(a.shape[1] for a in rows), vec.shape[1])
        rows.append(np.broadcast_to(vec, (128, vec.shape[1])))

    def fm(v):
        return np.asarray(v, np.float32).reshape(-1, 128).T

    addc("g_mix", fm(p["norm_mix"][0]))
    addc("g_qn", np.tile(p["nsa_q_norm"][0], 2)[:, None])
    addc("g_ks", np.tile(p["nsa_k_norm"][0, 1], 2)[:, None])
    addc("g_kw", np.tile(p["nsa_k_norm"][0, 2], 2)[:, None])
    addc("g_kc", np.tile(p["nsa_k_norm"][0, 0], 2)[:, None])
    addc("g_mq", p["mem_q_norm"][0][:, None])
    addc("g_mk", p["mem_k_norm"][0][:, None])
    addc("g_memn", fm(p["mem_norm"][0]))
    addc("g_ssdn", fm(p["ssd_norm"][0]))
    addc("g_ffn", fm(p["norm_ffn"][0]))
    addc("cw_ssd", np.stack([fm(p["ssd_conv_w"][0, k]) for k in range(4)], 1).reshape(128, -1))
    addc("cb_ssd", fm(p["ssd_conv_b"][0]))
    addc("cw_ffn", np.stack([fm(p["ffn_conv_w"][0, k]) for k in range(3)], 1).reshape(128, -1))
    addc("cb_ffn", fm(p["ffn_conv_b"][0]))
    addr("dtb", p["ssd_dt_bias"][0])
    addr("alog", p["ssd_a_log"][0])
    addr("ssd_d", p["ssd_d"][0])
    return np.concatenate(cols, 1), np.ascontiguousarray(np.concatenate(rows, 1))


def _consts():
    c = {}
    c["ident"] = np.eye(128, dtype=np.float32)
    i = np.arange(128)
    c["c_tri"] = (i[:, None] <= i[None, :]).astype(np.float32)
    negm = np.where(i[None, :] < i[:, None], NEGB, 0.0).astype(np.float32)
    c["c_negm4"] = np.tile(negm, (1, 4))
    d16 = np.zeros((16, 16, 128), np.float32)
    d16[np.arange(16), np.arange(16), :] = 1.0
    c["c_delta16"] = d16.reshape(16, 2048)
    sel = np.zeros((128, 128), np.float32); sel[127, :] = 1.0
    c["c_sel127"] = sel
    tok = np.arange(S)
    c["c_eblk"] = (tok[None, :] // 64 == np.arange(64)[:, None]).astype(np.float32)
    kside = np.zeros((2, 64, S), np.float32)
    kside[0, :60] = c["c_eblk"][:60]
    for v in range(2):
        kside[v, 60] = tok % 128
        kside[v, 61] = tok // 128
        kside[v, 62] = 1.0
        kside[v, 63] = 1.0
    c["c_kside"] = kside.reshape(128, S)
    raq = np.zeros((32, 2, 4, 4, 128), np.float32)
    for it in range(32):
        for g in range(2):
            for hl in range(4):
                sl = 2.0 ** -(4 * g + hl + 1)
                raq[it, g, 0, hl] = 8 * sl
                raq[it, g, 1, hl] = 1024 * sl
                raq[it, g, 2, hl] = -8 * sl * i
                raq[it, g, 3, hl] = -1024 * sl * it
    c["c_raq"] = raq.reshape(32 * 2 * 4, 512)
    la = np.zeros((3, S), np.float32)
    la[0] = tok % 128; la[1] = 1.0; la[2] = tok // 128
    c["c_laall"] = la
    ra = np.zeros((3, 2, 4, 128), np.float32)
    for g in range(2):
        for hl in range(4):
            sl = 2.0 ** -(4 * g + hl + 1)
            ra[0, g, hl] = 8 * sl
            ra[1, g, hl] = -8 * sl * i
            ra[2, g, hl] = -1024 * sl
    c["c_ra"] = ra.reshape(3, 1024)
    rac = np.zeros((3, 2, 4, 128), np.float32)
    for g in range(2):
        for hl in range(4):
            sl = 2.0 ** -(4 * g + hl + 1)
            rac[0, g, hl] = 128 * sl
            rac[1, g, hl] = -8 * sl * (i - 31)
            rac[2, g, hl] = -1024 * sl
    c["c_rac"] = rac.reshape(3, 1024)
    kk = np.arange(16)
    shm = np.zeros((16, 32, 128), np.float32)
    for dc in range(32):
        shm[:, dc, :] = ((i[None, :] - 8 * dc) == (kk[:, None] - 8))
    c["c_shm"] = shm.reshape(16, S)
    c["c_cmask"] = np.tile(np.where(16 * (kk[:, None] - 8) > (i[None, :] - 31), 8 * NEGB, 0.0).astype(np.float32), (1, 4))
    c["c_caus"] = np.tile(np.where(i[:, None] > i[None, :], 8 * NEGB, 0.0).astype(np.float32), (1, 4))
    c["c_winm"] = np.tile(np.where(i[:, None] <= i[None, :], 8 * NEGB, 0.0).astype(np.float32), (1, 4))
    c["c_d0"] = (i[None, :] - 16.0 * i[:, None]).astype(np.float32)
    rel = i[None, :] - 64 - (i[:, None] // 64)
    c["c_mw1"] = np.where(rel >= -1, 0.0, 1.0).astype(np.float32)
    c["c_mw2"] = np.where((rel == 0) | (rel == -1), 1e9, np.where(rel > 0, -1e30, 0.0)).astype(np.float32)
    n = np.arange(255)
    j = np.arange(64)
    ov = np.clip(np.minimum(16 * n[:, None] + 31, 64 * j[None, :] + 63) - np.maximum(16 * n[:, None], 64 * j[None, :]) + 1, 0, None) / 32.0
    ovl = np.zeros((128, 2, 64), np.float32)
    ovl[:, 0] = ov[:128]; ovl[:127, 1] = ov[128:]
    c["c_ovl"] = ovl.reshape(128, 128)
    return c


def bc(ap, shape):
    return ap.to_broadcast(list(shape))


def phase_setup(k):
    nc = k.nc
    k.din("cols", [128, k.ncols])
    k.din("rows", [128, k.nrows])
    k.din("ident", [128, 128])
    k.cols = k.sb("cols", [128, k.ncols], F32)
    k.rows = k.sb("rows", [128, k.nrows], F32)
    k.ident = k.sb("ident", [128, 128], BF16)
    k.identf = k.sb("identf", [128, 128], F32)
    k.c_eps = k.sb("c_eps", [128, 1], F32)
    k.load(k.cols, k.dram["cols"], "cols")
    k.load(k.rows, k.dram["rows"], "rows")
    k.load(k.ident, k.dram["ident"], "ident", cast=True)
    k.load(k.identf, k.dram["ident"], "ident")
    k.V(lambda: nc.vector.memset(k.c_eps.t[:], EPS), writes=[k.c_eps])
    k.psf = Ring([k.ps(f"psf{i}", [128, 512], F32) for i in range(6)])
    k.psb = Ring([k.ps(f"psb{i}", [128, 1024], BF16) for i in range(2)])


def col(k, name, j=0, n=1):
    s0, _ = COLS[name]
    return k.cols.t[:, s0 + j:s0 + j + n]


def row(k, name):
    s0, n = ROWS[name]
    return k.rows.t[:, s0:s0 + n]


def transpose_to(k, dst_tl, dst_ap_fn, src_tl, nblk, gain=None, reads=(), ring=None):
    nc = k.nc
    pt = (ring or k.psb).next()

    def tr():
        for j in range(nblk):
            ins = nc.tensor.transpose(pt.t[:, j * 128:(j + 1) * 128], src_tl.t[:, j * 128:(j + 1) * 128], k.ident.t[:])
        return ins
    k.P(tr, reads=[src_tl, k.ident] + list(reads), writes=[pt])
    return pt


CONV_W = (("w_ssd_o", [1024, 1024]), ("c_negm4", [128, 512]),
          ("cmp_w1", [4096, 256]), ("cmp_w2", [512, 64]), ("c_peT", [64, 64]),
          ("c_kside", [128, S]), ("c_eblk", [64, S]), ("c_raq", [256, 512]), ("c_laall", [3, S]), ("c_shm", [16, S]),
          ("c_rac", [3, 1024]), ("c_cmask", [16, 512]), ("c_caus", [128, 512]), ("c_winm", [128, 512]), ("c_ovl", [128, 128]),
          ("w_nsa_o", [512, 1024]), ("w_mem_kv", [1024, 1024]), ("w_mem_o", [512, 1024]), ("w_out", [1024, 1024]),
          ("w_ffn_up", [1024, 2 * FFH]), ("w_ffn_down", [FFH, 1024]))


LATE_CONV = ("w_mem_kv", "w_mem_o", "w_out", "w_ffn_up", "w_ffn_down")


def convert_weights(k, after=()):
    k.late_conv = []
    for nm, shp in CONV_W:
        src = k.din(nm, shp)
        dst = k.dscr(nm + "_bf", shp, BF16)
        rows = shp[0]
        if nm in LATE_CONV:
            step = 64 if nm == "w_ffn_up" else 128
            for r0 in range(0, rows, step):
                r1 = min(rows, r0 + step)
                k.late_conv.append((dst[r0:r1, :], src[r0:r1, :], nm))
            continue
        step = 256
        for r0 in range(0, rows, step):
            r1 = min(rows, r0 + step)
            k.dma(dst[r0:r1, :], src[r0:r1, :], reads=list(after), writes=[k.dbuf[nm + "_bf"]], q=k.f.pq, acc=True)


def issue_late_conv(k, n=1):
    for _ in range(n):
        if getattr(k, "late_conv", None):
            d_, s_, nm = k.late_conv.pop(0)
            k.dma(d_, s_, writes=[k.dbuf[nm + "_bf"]], q=k.f.pq, acc=True, inflight=2)


def wload(k, tl, name, rearr, idx=slice(None), **kw):
    src = k.dram[name + "_bf"]
    return k.dma(tl.t[idx], src.rearrange(rearr, **kw), reads=[k.dbuf[name + "_bf"]], writes=[tl])


def phase_a(k):
    nc = k.nc
    x = k.din("x", [S, D])
    w_in = k.din("w_in", [D, INW])
    z_s = k.dscr("z_s", [S, 1024], BF16)
    xbcT_s = k.dscr("xbcT_s", [12, 128, S], BF16)
    dts_s = k.dscr("dts_s", [S, 64], F32)
    qT_s = k.dscr("qT_s", [4, 128, S], BF16)
    kTs_s = k.dscr("kTs_s", [128, S], BF16)
    kTw_s = k.dscr("kTw_s", [128, S], BF16)
    v_s = k.dscr("v_s", [S, 256], BF16)
    gn_s = k.dscr("gn_s", [S, 24], F32)
    qmT_s = k.dscr("qmT_s", [4, 128, S], BF16)
    kvcT_s = k.dscr("kvcT_s", [2, 128, S], BF16)
    gmT_s = k.dscr("gmT_s", [24, 128, S], BF16)

    WIN = k.sb("WIN", [128, 8, INW], BF16)
    wgroups = [(0, 1024), (2560, 3088), (3344, 4392), (1024, 2560), (3088, 3344), (4392, 5928), (5928, INW)]
    WINb = {}
    for (c_lo, c_hi) in wgroups:
        b_ = Buf(f"WIN{c_lo}")
        for kk in range(8):
            k.dma(WIN.t[:, kk, c_lo:c_hi], w_in[kk * 128:(kk + 1) * 128, c_lo:c_hi], writes=[b_], q=k.f.pq, acc=True, inflight=3)
        for c in range(c_lo, c_hi, 8):
            WINb[c] = b_

    def wbuf(c0):
        return WINb[(c0 // 8) * 8]
    convert_weights(k, after=[WINb[c_lo] for (c_lo, _c) in wgroups])
    psb_main = Ring([k.psb.tiles[0]])
    psb_prep = Ring([k.psb.tiles[1]])
    k.arow = k.sb("arow", [128, 16], F32)
    k.A(lambda: nc.scalar.activation(out=k.arow.t[:], in_=row(k, "alog"), func=AF.Exp), reads=[k.rows], writes=[k.arow])
    k.V(lambda: nc.vector.tensor_scalar(out=k.arow.t[:], in0=k.arow.t[:], scalar1=-1.0, scalar2=None, op0=ALU.mult),
        reads=[k.arow], writes=[k.arow])

    xt_r = k.ring("xt", [128, D], F32, 3)
    junk = k.sb("junk", [128, D], BF16)
    ss_r = k.ring("ss", [128, 8], F32, 4)
    xs_r = k.ring("xs", [128, D], BF16, 2)
    ssp_r = k.ring("ssp", [128, 8], F32, 2)
    xnT_r = k.ring("xnT", [128, 8, 512], BF16, 2)
    zt_r = k.ring("zt", [128, 1024], BF16, 2)
    fm_r = k.ring("fm", [128, 6, 512], BF16, 3)
    dt_r = k.ring("dtt", [128, 64], F32, 2)
    for t_ in dt_r.tiles:
        k.V(lambda: nc.vector.memset(t_.t[:, 32:48], 0.0), writes=[t_])
    sq_r = k.ring("sq", [128, 512], F32, 2)
    qn_r = k.ring("qn", [128, 512], BF16, 2)
    qmn_r = k.ring("qmn", [128, 512], BF16, 2)
    qT_r = k.ring("qTt", [128, 4, 128], BF16, 2)
    kn_r = k.ring("kn", [128, 256], BF16, 2)
    kT_r = k.ring("kTt", [128, 2, 128], BF16, 2)
    vv_r = k.ring("vv", [128, 256], BF16, 2)
    gn_r = k.ring("gnt", [128, 24], F32, 2)

    def tok_major(xnT, tt, tok0, c0, n):
        pt = k.psf.next()
        k.mm(pt, [(xnT.t[:, kk, tt * 128:(tt + 1) * 128], WIN.t[:, kk, c0:c0 + n]) for kk in range(8)],
             reads=[xnT, wbuf(c0)], out_ap=pt.t[:, 0:n])
        run_deferred()
        fill()
        return pt

    def headnorm(pt, n, hd, c_lo=0):
        nh = n // hd
        sq = sq_r.next()
        k.A(lambda: nc.scalar.activation(out=sq.t[:, 0:n], in_=pt.t[:, c_lo:c_lo + n], func=AF.Square), reads=[pt], writes=[sq])
        ss = ss_r.next()
        k.V(lambda: nc.vector.tensor_reduce(out=ss.t[:, 0:nh], in_=sq.t[:, 0:n].rearrange("p (h d) -> p h d", d=hd),
                                            axis=AX.X, op=ALU.add), reads=[sq], writes=[ss])
        k.rsqrt(ss, ss, 1.0 / hd, (slice(None), slice(0, nh)))
        return ss

    def prep(tb, xnT):
        xts = []

        def ld(tt):
            xt_ = xt_r.next()
            k.load(xt_, x[tb * 512 + tt * 128:tb * 512 + (tt + 1) * 128, :], "x")
            xts.append(xt_)
        ld(0)
        ld(1)
        yield
        for tt in range(4):
            tok0 = tb * 512 + tt * 128
            xt = xts[tt]
            if tt + 2 < 4:
                ld(tt + 2)
            ss = ssp_r.next()
            k.A(lambda: nc.scalar.activation(out=junk.t[:], in_=xt.t[:], func=AF.Square, accum_out=ss.t[:, 0:1]),
                reads=[xt], writes=[junk, ss])
            yield
            k.rsqrt(ss, ss, 1.0 / D, (slice(None), slice(0, 1)))
            yield
            xs = xs_r.next()
            k.V(lambda: nc.vector.tensor_scalar(out=xs.t[:], in0=xt.t[:], scalar1=ss.t[:, 0:1], scalar2=None, op0=ALU.mult),
                reads=[xt, ss], writes=[xs])
            yield
            pt = transpose_to(k, None, None, xs, 8, ring=psb_prep)
            k.V(lambda: nc.vector.tensor_tensor(
                out=xnT.t[:, :, tt * 128:(tt + 1) * 128], in0=pt.t[:].rearrange("p (k t) -> p k t", t=128),
                in1=bc(col(k, "g_mix", 0, 8).unsqueeze(2), [128, 8, 128]), op=ALU.mult),
                reads=[pt, k.cols], writes=[xnT])
            yield

    filler = [None]
    deferred = []
    gcount = [0]

    def run_deferred(force=False):
        gcount[0] += 1
        while deferred and (force or deferred[0][0] <= gcount[0]):
            deferred.pop(0)[1]()

    def fill():
        if filler[0] is not None:
            next(filler[0], None)

    xnT_next = xnT_r.next()
    for _ in prep(0, xnT_next):
        pass
    for tb in range(8):
        xnT = xnT_next
        if tb + 1 < 8:
            xnT_next = xnT_r.next()
            filler[0] = prep(tb + 1, xnT_next)
        else:
            filler[0] = None
        for tt in range(4):
            tok0 = tb * 512 + tt * 128
            zt = zt_r.next()
            for hh in range(2):
                pt = tok_major(xnT, tt, tok0, O_Z + hh * 512, 512)
                k.A(lambda: nc.scalar.copy(out=zt.t[:, hh * 512:(hh + 1) * 512], in_=pt.t[:]), reads=[pt], writes=[zt])
            k.store(z_s[tok0:tok0 + 128, :], "z_s", zt)
            pt = tok_major(xnT, tt, tok0, O_DT, 16)
            dtt = dt_r.next()
            k.V(lambda: nc.vector.tensor_tensor(out=dtt.t[:, 0:16], in0=pt.t[:, 0:16], in1=row(k, "dtb"), op=ALU.add),
                reads=[pt, k.rows], writes=[dtt])
            k.A(lambda: nc.scalar.activation(out=dtt.t[:, 0:16], in_=dtt.t[:, 0:16], func=AF.Exp), reads=[dtt], writes=[dtt])
            k.A(lambda: nc.scalar.activation(out=dtt.t[:, 0:16], in_=dtt.t[:, 0:16], func=AF.Ln, bias=1.0), reads=[dtt], writes=[dtt])
            k.V(lambda: nc.vector.tensor_tensor(out=dtt.t[:, 16:32], in0=dtt.t[:, 0:16], in1=k.arow.t[:], op=ALU.mult),
                reads=[dtt, k.arow], writes=[dtt])
            k.V(lambda: nc.vector.tensor_scalar(out=dtt.t[:, 48:64], in0=dtt.t[:, 16:32], scalar1=-1.0, scalar2=None, op0=ALU.mult),
                reads=[dtt], writes=[dtt])
            k.store(dts_s[tok0:tok0 + 128, :], "dts_s", dtt)
            pt = tok_major(xnT, tt, tok0, O_QN, 512)
            rs = headnorm(pt, 512, 64)
            qn = qn_r.next()
            k.V(lambda: nc.vector.tensor_tensor(out=qn.t[:].rearrange("p (h d) -> p h d", d=64),
                                                in0=pt.t[:].rearrange("p (h d) -> p h d", d=64),
                                                in1=bc(rs.t[:, 0:8].unsqueeze(2), [128, 8, 64]), op=ALU.mult),
                reads=[pt, rs], writes=[qn])
            def dq(qn=qn, tok0=tok0):
                ptb = transpose_to(k, None, None, qn, 4, ring=psb_main)
                qTt = qT_r.next()
                k.V(lambda: nc.vector.tensor_scalar(out=qTt.t[:].rearrange("p j t -> p (j t)"), in0=ptb.t[:, 0:512],
                                                    scalar1=col(k, "g_qn"), scalar2=None, op0=ALU.mult),
                    reads=[ptb, k.cols], writes=[qTt])
                k.store(qT_s[:, :, tok0:tok0 + 128].rearrange("j p t -> p j t"), "qT_s", qTt)
            deferred.append((gcount[0] + 6, dq))
            pt = tok_major(xnT, tt, tok0, O_KV + 256, 512)
            rs = headnorm(pt, 512, 64)
            kn = kn_r.next()
            for b, c_lo, r_lo in ((0, 0, 0), (1, 256, 4)):
                k.V(lambda: nc.vector.tensor_tensor(out=kn.t[:, b * 128:(b + 1) * 128].rearrange("p (h d) -> p h d", d=64),
                                                    in0=pt.t[:, c_lo:c_lo + 128].rearrange("p (h d) -> p h d", d=64),
                                                    in1=bc(rs.t[:, r_lo:r_lo + 2].unsqueeze(2), [128, 2, 64]), op=ALU.mult),
                    reads=[pt, rs], writes=[kn])
            vv = vv_r.next()
            k.A(lambda: nc.scalar.copy(out=vv.t[:].rearrange("p (b c) -> p b c", b=2),
                                       in_=pt.t[:].rearrange("p (b c) -> p b c", b=2)[:, :, 128:256]), reads=[pt], writes=[vv])
            k.store(v_s[tok0:tok0 + 128, :], "v_s", vv)
            def dk(kn=kn, tok0=tok0):
                ptb = transpose_to(k, None, None, kn, 2, ring=psb_main)
                kTt = kT_r.next()
                k.V(lambda: nc.vector.tensor_scalar(out=kTt.t[:, 0, :], in0=ptb.t[:, 0:128], scalar1=col(k, "g_ks"), scalar2=None,
                                                    op0=ALU.mult), reads=[ptb, k.cols], writes=[kTt])
                k.V(lambda: nc.vector.tensor_scalar(out=kTt.t[:, 1, :], in0=ptb.t[:, 128:256], scalar1=col(k, "g_kw"), scalar2=None,
                                                    op0=ALU.mult), reads=[ptb, k.cols], writes=[kTt])
                k.store(kTs_s[:, tok0:tok0 + 128], "kTs_s", kTt, (slice(None), 0, slice(None)))
                k.store(kTw_s[:, tok0:tok0 + 128], "kTw_s", kTt, (slice(None), 1, slice(None)))
            deferred.append((gcount[0] + 6, dk))
            pt = tok_major(xnT, tt, tok0, O_GN, 24)
            gnt = gn_r.next()
            k.A(lambda: nc.scalar.activation(out=gnt.t[:], in_=pt.t[:, 0:24], func=AF.Exp, scale=-1.0), reads=[pt], writes=[gnt])
            k.V(lambda: nc.vector.tensor_scalar(out=gnt.t[:], in0=gnt.t[:], scalar1=1.0, scalar2=None, op0=ALU.add),
                reads=[gnt], writes=[gnt])
            k.V(lambda: nc.vector.reciprocal(out=gnt.t[:], in_=gnt.t[:]), reads=[gnt], writes=[gnt])
            k.store(gn_s[tok0:tok0 + 128, :], "gn_s", gnt)
            pt = tok_major(xnT, tt, tok0, O_QM, 512)
            rs = headnorm(pt, 512, 128)
            qn = qmn_r.next()
            k.V(lambda: nc.vector.tensor_tensor(out=qn.t[:].rearrange("p (h d) -> p h d", d=128),
                                                in0=pt.t[:].rearrange("p (h d) -> p h d", d=128),
                                                in1=bc(rs.t[:, 0:4].unsqueeze(2), [128, 4, 128]), op=ALU.mult),
                reads=[pt, rs], writes=[qn])
            def dm(qn=qn, tok0=tok0):
                ptb = transpose_to(k, None, None, qn, 4, ring=psb_main)
                qTt = qT_r.next()
                k.V(lambda: nc.vector.tensor_scalar(out=qTt.t[:].rearrange("p j t -> p (j t)"), in0=ptb.t[:, 0:512],
                                                    scalar1=col(k, "g_mq"), scalar2=None, op0=ALU.mult),
                    reads=[ptb, k.cols], writes=[qTt])
                k.store(qmT_s[:, :, tok0:tok0 + 128].rearrange("j p t -> p j t"), "qmT_s", qTt)
            deferred.append((gcount[0] + 6, dm))
        t0 = tb * 512

        def feat_major(c0):
            pt = k.psf.next()
            k.mm(pt, [(WIN.t[:, kk, c0:c0 + 128], xnT.t[:, kk, :]) for kk in range(8)], reads=[xnT, wbuf(c0)])
            run_deferred()
            fill()
            return pt
        for grp in range(2):
            fm = fm_r.next()
            for j in range(6):
                m = grp * 6 + j
                pt = feat_major(O_XBC + m * 128)
                if j % 2 == 0:
                    k.V(lambda: nc.vector.tensor_copy(out=fm.t[:, j, :], in_=pt.t[:]), reads=[pt], writes=[fm])
                else:
                    k.A(lambda: nc.scalar.copy(out=fm.t[:, j, :], in_=pt.t[:]), reads=[pt], writes=[fm])
            k.store(xbcT_s[grp * 6:grp * 6 + 6, :, t0:t0 + 512].rearrange("m p t -> p m t"), "xbcT_s", fm)
        fm = fm_r.next()
        for j in range(2):
            pt = feat_major(O_KV + j * 128)
            k.V(lambda: nc.vector.tensor_copy(out=fm.t[:, j, :], in_=pt.t[:]), reads=[pt], writes=[fm])
        k.store(kvcT_s[:, :, t0:t0 + 512].rearrange("m p t -> p m t"), "kvcT_s", fm, (slice(None), slice(0, 2), slice(None)))
        for grp in range(4):
            fm = fm_r.next()
            for j in range(6):
                m = grp * 6 + j
                pt = feat_major(O_GM + m * 128)
                k.A(lambda: nc.scalar.activation(out=fm.t[:, j, :], in_=pt.t[:], func=AF.Sigmoid), reads=[pt], writes=[fm])
            k.store(gmT_s[grp * 6:grp * 6 + 6, :, t0:t0 + 512].rearrange("m p t -> p m t"), "gmT_s", fm)
        if tb == 7:
            run_deferred(force=True)
        if filler[0] is not None:
            for _ in filler[0]:
                pass


def phase_b(k):
    nc = k.nc
    for nm, shp in (("c_tri", [128, 128]), ("c_delta16", [16, 2048]), ("c_sel127", [128, 128])):
        k.din(nm, shp)
    mix1T_s = k.dscr("mix1T_s", [8, 128, S], BF16)
    xbcT_s, dts_s, z_s, gmT_s = k.dram["xbcT_s"], k.dram["dts_s"], k.dram["z_s"], k.dram["gmT_s"]

    TRI = k.sb("TRI", [128, 128], F32); k.load(TRI, k.dram["c_tri"], "")
    DEL = k.sb("DEL", [16, 16, 128], F32); k.load(DEL, k.dram["c_delta16"].rearrange("p (h l) -> p h l", l=128), "")
    SEL = k.sb("SEL", [128, 128], F32); k.load(SEL, k.dram["c_sel127"], "")
    WSO = k.sb("WSO", [128, 8, 1024], BF16)
    wload(k, WSO, "w_ssd_o", "(k p) n -> p k n", p=128)
    DGC = k.sb("DGC", [128, 48, 128], BF16)
    k.V(lambda: nc.vector.tensor_tensor(out=DGC.t[:], in0=bc(k.identf.t[:].unsqueeze(1), [128, 48, 128]),
                                        in1=bc(col(k, "cw_ssd", 0, 48).unsqueeze(2), [128, 48, 128]), op=ALU.mult),
        reads=[k.identf, k.cols], writes=[DGC])
    DIAGD = k.sb("DIAGD", [128, 16, 128], BF16)
    k.V(lambda: nc.vector.tensor_tensor(out=DIAGD.t[:], in0=bc(k.identf.t[:].unsqueeze(1), [128, 16, 128]),
                                        in1=bc(row(k, "ssd_d").unsqueeze(2), [128, 16, 128]), op=ALU.mult),
        reads=[k.identf, k.rows], writes=[DIAGD])
    state = k.sb("state", [128, 1024], F32)
    prevbf = k.sb("prevbf", [128, 1024], BF16)
    k.V(lambda: nc.vector.memset(state.t[:], 0.0), writes=[state])
    k.V(lambda: nc.vector.memset(prevbf.t[:], 0.0), writes=[prevbf])

    NEGM4b = k.sb("NEGM4b", [128, 512], BF16)
    k.dma(NEGM4b.t[:], k.dram["c_negm4_bf"], reads=[k.dbuf["c_negm4_bf"]], writes=[NEGM4b])
    LT_r = k.ring("LT", [48, 128], F32, 2)
    RT_r = k.ring("RT", [48, 16, 128], F32, 2)
    for t_ in LT_r.tiles:
        k.V(lambda: nc.vector.memset(t_.t[:], 0.0), writes=[t_])
        k.V(lambda: nc.vector.memset(t_.t[0:16, :], 1.0), writes=[t_])
    for t_ in RT_r.tiles:
        k.V(lambda: nc.vector.memset(t_.t[:], 0.0), writes=[t_])
        k.dma(t_.t[32:48, :, :], k.dram["c_delta16"].rearrange("p (h l) -> p h l", l=128), writes=[t_])

    XR_r = k.ring("XR", [128, 12, 515], BF16, 2)
    for t in XR_r.tiles:
        k.V(lambda: nc.vector.memset(t.t[:, :, 0:3], 0.0), writes=[t])
    XC_r = k.ring("XC", [128, 12, 512], BF16, 2)
    GM_r = k.ring("GM", [128, 8, 512], BF16, 2)
    ynT_r = k.ring("ynT", [128, 8, 512], BF16, 2)
    mo_r = k.ring("mo", [128, 512], BF16, 2)
    dts_r = k.ring("dts", [128, 64], F32, 2)
    z_r = k.ring("zc", [128, 1024], BF16, 2)
    sm_r = k.ring("sm", [128, 64], F32, 2)
    xtok_r = k.ring("xtok", [128, 3, 1024], BF16, 1)
    bt_r = k.ring("btok", [128, 256], BF16, 1)
    cbT_r = k.ring("cbT", [128, 2, 128], BF16, 1)
    lm_r = k.ring("lm", [128, 4, 128], BF16, 2)
    MT_r = k.ring("MT", [128, 16, 128], BF16, 1)
    yf_r = k.ring("yf", [128, 1024], F32, 2)
    zs_r = k.ring("zs", [128, 1024], F32, 2)
    yn_r = k.ring("yn", [128, 1024], BF16, 2)
    junk = k.sb("junkb", [128, 1024], BF16)

    def load_chunk(c):
        dts = dts_r.next(); k.load(dts, dts_s[c * 128:(c + 1) * 128, :], "")
        zc = z_r.next(); k.load(zc, z_s[c * 128:(c + 1) * 128, :], "")
        return dts, zc

    def tail(yn, ynT, cs, do_proj, GM, t0):
        pt = transpose_to(k, None, None, yn, 8)
        k.V(lambda: nc.vector.tensor_tensor(out=ynT.t[:, :, cs], in0=pt.t[:].rearrange("p (k t) -> p k t", t=128),
                                            in1=bc(col(k, "g_ssdn", 0, 8).unsqueeze(2), [128, 8, 128]), op=ALU.mult),
            reads=[pt, k.cols], writes=[ynT])
        if do_proj:
            for e in range(8):
                pt = k.psf.next()
                k.mm(pt, [(WSO.t[:, kk, e * 128:(e + 1) * 128], ynT.t[:, kk, :]) for kk in range(8)], reads=[WSO, ynT])
                mo = mo_r.next()
                k.V(lambda: nc.vector.tensor_tensor(out=mo.t[:], in0=pt.t[:], in1=GM.t[:, e, :], op=ALU.mult),
                    reads=[pt, GM], writes=[mo])
                k.store(mix1T_s[e, :, t0:t0 + 512], "", mo)

    nxt = load_chunk(0)
    pending_gate = None
    for tb in range(8):
        t0 = tb * 512
        XR = XR_r.next()
        if tb == 0:
            k.load(XR, xbcT_s[:, :, 0:512].rearrange("m p t -> p m t"), "", (slice(None), slice(None), slice(3, 515)))
        else:
            k.load(XR, xbcT_s[:, :, t0 - 3:t0 + 512].rearrange("m p t -> p m t"), "")
        XC = XC_r.next()
        for m in range(12):
            pt = k.psf.next()
            k.mm(pt, [(DGC.t[:, kk * 12 + m, :], XR.t[:, m, kk:kk + 512]) for kk in range(4)], reads=[DGC, XR])
            k.A(lambda: nc.scalar.activation(out=XC.t[:, m, :], in_=pt.t[:], func=AF.Silu, bias=col(k, "cb_ssd", m)),
                reads=[pt, k.cols], writes=[XC])
        GM = GM_r.next()
        k.load(GM, gmT_s[0:8, :, t0:t0 + 512].rearrange("m p t -> p m t"), "")
        ynT = ynT_r.next()
        for cc in range(4):
            c = tb * 4 + cc
            cs = slice(cc * 128, (cc + 1) * 128)
            dts, zc = nxt
            if c + 1 < NT:
                nxt = load_chunk(c + 1)
            p1 = k.psf.next()
            k.mm(p1, [(TRI.t[:], dts.t[:, 16:32])], reads=[TRI, dts], out_ap=p1.t[:, 0:16])
            p2 = k.psf.next()
            k.mm(p2, [(dts.t[:, 16:64], TRI.t[:])], reads=[TRI, dts], out_ap=p2.t[0:48, 0:128])
            sm = sm_r.next()
            k.V(lambda: nc.vector.tensor_copy(out=sm.t[:, 0:16], in_=p1.t[:, 0:16]), reads=[p1], writes=[sm])
            LT = LT_r.next()
            RT = RT_r.next()
            k.V(lambda: nc.vector.tensor_copy(out=LT.t[32:48, :], in_=p2.t[32:48, 0:128]), reads=[p2], writes=[LT])
            k.V(lambda: nc.vector.tensor_tensor(out=RT.t[0:16, :, :], in0=bc(p2.t[0:16, 0:128].unsqueeze(1), [16, 16, 128]),
                                                in1=DEL.t[:], op=ALU.mult), reads=[p2, DEL], writes=[RT])
            p3 = k.psf.next()
            k.mm(p3, [(SEL.t[:], sm.t[:, 0:16])], reads=[SEL, sm], out_ap=p3.t[:, 0:16])
            k.A(lambda: nc.scalar.activation(out=sm.t[:, 16:32], in_=sm.t[:, 0:16], func=AF.Exp), reads=[sm], writes=[sm])
            k.V(lambda: nc.vector.tensor_tensor(out=sm.t[:, 32:48], in0=p3.t[:, 0:16], in1=sm.t[:, 0:16], op=ALU.subtract),
                reads=[p3, sm], writes=[sm])
            k.A(lambda: nc.scalar.activation(out=sm.t[:, 32:48], in_=sm.t[:, 32:48], func=AF.Exp), reads=[sm], writes=[sm])
            k.A(lambda: nc.scalar.activation(out=sm.t[:, 48:64], in_=p3.t[:, 0:16], func=AF.Exp), reads=[p3, sm], writes=[sm])
            k.V(lambda: nc.vector.tensor_tensor(out=sm.t[:, 32:48], in0=sm.t[:, 32:48], in1=dts.t[:, 0:16], op=ALU.mult),
                reads=[sm, dts], writes=[sm])
            prev_gate = pending_gate
            if prev_gate is not None:
                prev_gate[0]()
            pb = k.psb.next()

            def trx():
                for m in range(8):
                    ins = nc.tensor.transpose(pb.t[:, m * 128:(m + 1) * 128], XC.t[:, m, cs], k.ident.t[:])
                return ins
            k.P(trx, reads=[XC, k.ident], writes=[pb])
            xtok = xtok_r.next()
            k.A(lambda: nc.scalar.copy(out=xtok.t[:, 0, :], in_=pb.t[:]), reads=[pb], writes=[xtok])
            k.V(lambda: nc.vector.tensor_tensor(out=xtok.t[:, 1, :].rearrange("p (h d) -> p h d", d=64),
                                                in0=pb.t[:].rearrange("p (h d) -> p h d", d=64),
                                                in1=bc(dts.t[:, 0:16].unsqueeze(2), [128, 16, 64]), op=ALU.mult),
                reads=[pb, dts], writes=[xtok])
            k.V(lambda: nc.vector.tensor_tensor(out=xtok.t[:, 2, :].rearrange("p (h d) -> p h d", d=64),
                                                in0=pb.t[:].rearrange("p (h d) -> p h d", d=64),
                                                in1=bc(sm.t[:, 32:48].unsqueeze(2), [128, 16, 64]), op=ALU.mult),
                reads=[pb, sm], writes=[xtok])
            pb2 = k.psb.next()

            def trb():
                for g in range(2):
                    ins = nc.tensor.transpose(pb2.t[:, g * 128:(g + 1) * 128], XC.t[:, 8 + g, cs], k.ident.t[:])
                return ins
            k.P(trb, reads=[XC, k.ident], writes=[pb2])
            btok = bt_r.next()
            k.A(lambda: nc.scalar.copy(out=btok.t[:], in_=pb2.t[:, 0:256]), reads=[pb2], writes=[btok])
            p4 = k.psf.next()

            def cbf():
                for g in range(2):
                    ins = nc.tensor.matmul(p4.t[:, g * 128:(g + 1) * 128], XC.t[:, 8 + g, cs], XC.t[:, 10 + g, cs],
                                           start=True, stop=True)
                return ins
            k.P(cbf, reads=[XC], writes=[p4])
            cbT = cbT_r.next()
            k.V(lambda: nc.vector.tensor_copy(out=cbT.t[:].rearrange("p g l -> p (g l)"), in_=p4.t[:, 0:256]),
                reads=[p4], writes=[cbT])
            yn_prev = prev_gate[1]() if prev_gate is not None else None
            MT = MT_r.next()
            for hq in range(4):
                g = hq // 2
                pg = k.psf.next()
                k.mm(pg, [(LT.t[:, :], RT.t[:, 4 * hq:4 * hq + 4, :].rearrange("p h l -> p (h l)")),
                          (k.ident.t[:], NEGM4b.t[:])], reads=[LT, RT, k.ident, NEGM4b])
                lm = lm_r.next()
                k.A(lambda: nc.scalar.activation(out=lm.t[:].rearrange("p h l -> p (h l)"), in_=pg.t[:], func=AF.Exp),
                    reads=[pg], writes=[lm])
                k.V(lambda: nc.vector.tensor_tensor(out=MT.t[:, 4 * hq:4 * hq + 4, :], in0=lm.t[:],
                                                    in1=bc(cbT.t[:, g, :].unsqueeze(1), [128, 4, 128]), op=ALU.mult),
                    reads=[lm, cbT], writes=[MT])
            if prev_gate is not None:
                prev_gate[2](yn_prev)
            yd = [k.psf.next(), k.psf.next()]
            for hb in range(2):
                def ydf():
                    for hh in range(8):
                        h = hb * 8 + hh
                        nc.tensor.matmul(yd[hb].t[:, hh * 64:(hh + 1) * 64], MT.t[:, h, :], xtok.t[:, 1, h * 64:(h + 1) * 64],
                                         start=True, stop=False)
                        ins = nc.tensor.matmul(yd[hb].t[:, hh * 64:(hh + 1) * 64], DIAGD.t[:, h, :],
                                               xtok.t[:, 0, h * 64:(h + 1) * 64], start=False, stop=True)
                    return ins
                k.P(ydf, reads=[MT, xtok, DIAGD], writes=[yd[hb]])
            yo = [k.psf.next(), k.psf.next()]
            for g in range(2):
                k.mm(yo[g], [(XC.t[:, 10 + g, cs], prevbf.t[:, g * 512:(g + 1) * 512])], reads=[XC, prevbf])
            yf = yf_r.next()
            for g in range(2):
                gs = slice(g * 512, (g + 1) * 512)
                k.V(lambda: nc.vector.tensor_tensor(out=yf.t[:, gs].rearrange("p (h d) -> p h d", d=64),
                                                    in0=yo[g].t[:].rearrange("p (h d) -> p h d", d=64),
                                                    in1=bc(sm.t[:, 16 + 8 * g:24 + 8 * g].unsqueeze(2), [128, 8, 64]), op=ALU.mult),
                    reads=[yo[g], sm], writes=[yf])
                k.V(lambda: nc.vector.tensor_tensor(out=yf.t[:, gs], in0=yd[g].t[:], in1=yf.t[:, gs], op=ALU.add),
                    reads=[yd[g], yf], writes=[yf])
            sn = [k.psf.next(), k.psf.next()]
            for g in range(2):
                k.mm(sn[g], [(btok.t[:, g * 128:(g + 1) * 128], xtok.t[:, 2, g * 512:(g + 1) * 512])], reads=[btok, xtok])
            k.G(lambda: nc.gpsimd.tensor_tensor(out=state.t[:].rearrange("p (h d) -> p h d", d=64),
                                                in0=state.t[:].rearrange("p (h d) -> p h d", d=64),
                                                in1=bc(sm.t[:, 48:64].unsqueeze(2), [128, 16, 64]), op=ALU.mult),
                reads=[state, sm], writes=[state])
            for g in range(2):
                gs = slice(g * 512, (g + 1) * 512)
                k.V(lambda: nc.vector.tensor_tensor(out=state.t[:, gs], in0=sn[g].t[:], in1=state.t[:, gs], op=ALU.add),
                    reads=[sn[g], state], writes=[state])
            k.A(lambda: nc.scalar.copy(out=prevbf.t[:], in_=state.t[:]), reads=[state], writes=[prevbf])
            zs = zs_r.next()
            k.A(lambda: nc.scalar.activation(out=zs.t[:], in_=zc.t[:], func=AF.Silu), reads=[zc], writes=[zs])
            k.G(lambda: nc.gpsimd.tensor_tensor(out=zs.t[:], in0=zs.t[:], in1=yf.t[:], op=ALU.mult), reads=[zs, yf], writes=[zs])
            def gate1(zs=zs, sm=sm):
                k.A(lambda: nc.scalar.activation(out=junk.t[:], in_=zs.t[:], func=AF.Square, accum_out=sm.t[:, 0:1]),
                    reads=[zs, sm], writes=[junk, sm])
                k.rsqrt(sm, sm, 1.0 / 1024, (slice(None), slice(0, 1)))

            def gate2(zs=zs, sm=sm):
                yn = yn_r.next()
                k.V(lambda: nc.vector.tensor_scalar(out=yn.t[:], in0=zs.t[:], scalar1=sm.t[:, 0:1], scalar2=None, op0=ALU.mult),
                    reads=[zs, sm], writes=[yn])
                return yn

            def gate3(yn, ynT=ynT, cs=cs, dp=(cc == 3), GM=GM, t0=t0):
                tail(yn, ynT, cs, dp, GM, t0)
            pending_gate = [gate1, gate2, gate3]
    pending_gate[0]()
    pending_gate[2](pending_gate[1]())


def phase_m(k):
    nc = k.nc
    mem = k.din("mem", [256, D])
    mix3T_s = k.dscr("mix3T_s", [8, 128, S], BF16)
    qmT_s, gmT_s = k.dram["qmT_s"], k.dram["gmT_s"]
    WKV = k.sb("WKV", [128, 8, 1024], BF16)
    wload(k, WKV, "w_mem_kv", "(k p) n -> p k n", p=128)
    WMO = k.sb("WMO", [128, 4, 1024], BF16)
    wload(k, WMO, "w_mem_o", "(k p) n -> p k n", p=128)
    memnT = k.sb("memnT", [128, 8, 256], BF16)
    kmT = k.sb("kmT", [128, 4, 256], BF16)
    VM = k.sb("VM", [128, 2, 512], BF16)
    onesc = k.sb("onesc", [128, 1], BF16)
    k.V(lambda: nc.vector.memset(onesc.t[:], 1.0), writes=[onesc])
    mt_r = k.ring("memt", [128, D], F32, 2)
    ms_r = k.ring("mems", [128, D], BF16, 2)
    ss_r = k.ring("mss", [128, 8], F32, 2)
    sq_r = k.ring("msq", [128, 512], F32, 1)
    kn_r = k.ring("mkn", [128, 512], BF16, 1)
    junk = k.sb("junkm", [128, D], BF16)
    for mt in range(2):
        xt = mt_r.next(); k.load(xt, mem[mt * 128:(mt + 1) * 128, :], "")
        ss = ss_r.next()
        k.A(lambda: nc.scalar.activation(out=junk.t[:], in_=xt.t[:], func=AF.Square, accum_out=ss.t[:, 0:1]),
            reads=[xt], writes=[junk, ss])
        k.rsqrt(ss, ss, 1.0 / D, (slice(None), slice(0, 1)))
        xs = ms_r.next()
        k.V(lambda: nc.vector.tensor_scalar(out=xs.t[:], in0=xt.t[:], scalar1=ss.t[:, 0:1], scalar2=None, op0=ALU.mult),
            reads=[xt, ss], writes=[xs])
        pt = transpose_to(k, None, None, xs, 8)
        k.V(lambda: nc.vector.tensor_tensor(out=memnT.t[:, :, mt * 128:(mt + 1) * 128],
                                            in0=pt.t[:].rearrange("p (k t) -> p k t", t=128),
                                            in1=bc(col(k, "g_memn", 0, 8).unsqueeze(2), [128, 8, 128]), op=ALU.mult),
            reads=[pt, k.cols], writes=[memnT])
    for mt in range(2):
        pt = k.psf.next()
        k.mm(pt, [(memnT.t[:, kk, mt * 128:(mt + 1) * 128], WKV.t[:, kk, 0:512]) for kk in range(8)], reads=[memnT, WKV])
        sq = sq_r.next()
        k.A(lambda: nc.scalar.activation(out=sq.t[:], in_=pt.t[:], func=AF.Square), reads=[pt], writes=[sq])
        ss = ss_r.next()
        k.V(lambda: nc.vector.tensor_reduce(out=ss.t[:, 0:4], in_=sq.t[:].rearrange("p (h d) -> p h d", d=128), axis=AX.X,
                                            op=ALU.add), reads=[sq], writes=[ss])
        k.rsqrt(ss, ss, 1.0 / 128, (slice(None), slice(0, 4)))
        kn = kn_r.next()
        k.V(lambda: nc.vector.tensor_tensor(out=kn.t[:].rearrange("p (h d) -> p h d", d=128),
                                            in0=pt.t[:].rearrange("p (h d) -> p h d", d=128),
                                            in1=bc(ss.t[:, 0:4].unsqueeze(2), [128, 4, 128]), op=ALU.mult),
            reads=[pt, ss], writes=[kn])
        ptb = transpose_to(k, None, None, kn, 4)
        k.V(lambda: nc.vector.tensor_scalar(out=kmT.t[:, :, mt * 128:(mt + 1) * 128],
                                            in0=ptb.t[:, 0:512].rearrange("p (h t) -> p h t", t=128),
                                            scalar1=col(k, "g_mk"), scalar2=None, op0=ALU.mult),
            reads=[ptb, k.cols], writes=[kmT])
        pt = k.psf.next()
        k.mm(pt, [(memnT.t[:, kk, mt * 128:(mt + 1) * 128], WKV.t[:, kk, 512:1024]) for kk in range(8)], reads=[memnT, WKV])
        k.A(lambda: nc.scalar.copy(out=VM.t[:, mt, :], in_=pt.t[:]), reads=[pt], writes=[VM])

    if getattr(k, "WUP", None) is not None:
        issue_wup(k)
    qm_r = k.ring("qmb", [128, 4, 512], BF16, 2)
    GM_r = k.ring("GMm", [128, 8, 512], BF16, 3)
    PT_r = k.ring("PTm", [128, 2, 512], BF16, 4)
    om_r = k.ring("omem", [128, 4, 512], BF16, 2)
    omT_r = k.ring("omT", [128, 4, 512], BF16, 2)
    rd_r = k.ring("rdm", [128, 16], F32, 4)
    mo_r = k.ring("mom", [128, 512], BF16, 2)
    sc = 128.0 ** -0.5

    def loads(tb):
        qm = qm_r.next(); k.load(qm, qmT_s[:, :, tb * 512:(tb + 1) * 512].rearrange("h p t -> p h t"), "")
        GM = GM_r.next(); k.load(GM, gmT_s[16:24, :, tb * 512:(tb + 1) * 512].rearrange("m p t -> p m t"), "")
        return qm, GM

    def tail(om, GM, t0):
        omT = omT_r.next()
        for tt in range(4):
            pb = k.psb.next()

            def tro():
                for j in range(4):
                    ins = nc.tensor.transpose(pb.t[:, j * 128:(j + 1) * 128], om.t[:, tt, j * 128:(j + 1) * 128], k.ident.t[:])
                return ins
            k.P(tro, reads=[om, k.ident], writes=[pb])
            k.A(lambda: nc.scalar.copy(out=omT.t[:, :, tt * 128:(tt + 1) * 128],
                                       in_=pb.t[:, 0:512].rearrange("p (k t) -> p k t", t=128)), reads=[pb], writes=[omT])
        for e in range(8):
            pt = k.psf.next()
            k.mm(pt, [(WMO.t[:, kk, e * 128:(e + 1) * 128], omT.t[:, kk, :]) for kk in range(4)], reads=[WMO, omT])
            mo = mo_r.next()
            k.V(lambda: nc.vector.tensor_tensor(out=mo.t[:], in0=pt.t[:], in1=GM.t[:, e, :], op=ALU.mult),
                reads=[pt, GM], writes=[mo])
            k.store(mix3T_s[e, :, t0:t0 + 512], "", mo)

    nxt = loads(0)
    pending = None
    for tb in range(8):
        t0 = tb * 512
        qm, GM = nxt
        if tb + 1 < 8:
            nxt = loads(tb + 1)
        om = om_r.next()
        PTs = []
        for h in range(4):
            PT = PT_r.next()
            for mt in range(2):
                pt = k.psf.next()
                k.mm(pt, [(kmT.t[:, h, mt * 128:(mt + 1) * 128], qm.t[:, h, :])], reads=[kmT, qm])
                k.A(lambda: nc.scalar.activation(out=PT.t[:, mt, :], in_=pt.t[:], func=AF.Exp, scale=sc), reads=[pt], writes=[PT])
            PTs.append(PT)
        if pending is not None:
            pending()
        for h in (1, 3):
            for tt in range(4):
                po = k.psf.next()
                rd = rd_r.next()

                def pv():
                    for hh in (h - 1, h):
                        P_ = PTs[hh]
                        for mt in range(2):
                            nc.tensor.matmul(po.t[:, (hh % 2) * 128:(hh % 2 + 1) * 128], P_.t[:, mt, tt * 128:(tt + 1) * 128],
                                             VM.t[:, mt, hh * 128:(hh + 1) * 128], start=(mt == 0), stop=(mt == 1))
                        for mt in range(2):
                            ins = nc.tensor.matmul(po.t[:, 256 + (hh % 2):257 + (hh % 2)], P_.t[:, mt, tt * 128:(tt + 1) * 128],
                                                   onesc.t[:], start=(mt == 0), stop=(mt == 1))
                    return ins
                k.P(pv, reads=[PTs[h - 1], PTs[h], VM, onesc], writes=[po])
                k.V(lambda: nc.vector.reciprocal(out=rd.t[:, 0:2], in_=po.t[:, 256:258]), reads=[po], writes=[rd])
                k.V(lambda: nc.vector.tensor_tensor(
                    out=om.t[:, tt, (h - 1) * 128:(h + 1) * 128].rearrange("p (h d) -> p h d", d=128),
                    in0=po.t[:, 0:256].rearrange("p (h d) -> p h d", d=128),
                    in1=bc(rd.t[:, 0:2].unsqueeze(2), [128, 2, 128]), op=ALU.mult), reads=[po, rd], writes=[om])
        pending = (lambda om=om, GM=GM, t0=t0: tail(om, GM, t0))
    pending()


def phase_n1(k):
    nc = k.nc
    kcT_s = k.dscr("kcT_s", [2, 64, 256], BF16)
    vc_s = k.dscr("vc_s", [2, 2, 128, 64], BF16)
    kvcT_s = k.dram["kvcT_s"]
    W1s = k.sb("W1s", [64, 2, 32, 256], BF16)
    for kv in range(2):
        k.dma(W1s.t[:, kv, :, :], k.dram["cmp_w1_bf"][kv * 2048:(kv + 1) * 2048, :].rearrange("(pos d) hid -> d pos hid", d=64),
              reads=[k.dbuf["cmp_w1_bf"]], writes=[W1s], acc=True)
    W2s = k.sb("W2s", [128, 2, 2, 64], BF16)
    for kv in range(2):
        k.dma(W2s.t[:, kv, :, :], k.dram["cmp_w2_bf"][kv * 256:(kv + 1) * 256, :].rearrange("(mt p) d -> p mt d", p=128),
              reads=[k.dbuf["cmp_w2_bf"]], writes=[W2s], acc=True)
    PE_ = k.sb("peTs", [64, 64], BF16)
    k.dma(PE_.t[:], k.dram["c_peT_bf"], reads=[k.dbuf["c_peT_bf"]], writes=[PE_])
    XCm = k.sb("XCm", [64, 2, 2, S], BF16)
    for kv in range(2):
        for g in range(2):
            k.dma(XCm.t[:, kv, g, :], kvcT_s[kv, g * 64:(g + 1) * 64, :], writes=[XCm], acc=True)
    cb1 = k.sb("cb1", [128, 4], F32)
    HID = k.sb("HID", [128, 2, 256], BF16)
    kc = k.sb("kc", [128, 128], BF16)
    k.V(lambda: nc.vector.memset(kc.t[:], 0.0), writes=[kc])
    ss = k.sb("ssn1", [128, 8], F32)
    junk = k.sb("junkn1", [128, 64], F32)
    kcT = k.sb("kcTt", [64, 128], BF16)
    vct = k.sb("vct", [128, 64], BF16)
    for kv in range(2):
        for mt in range(2):
            pt = k.psf.next()
            k.mm(pt, [(W1s.t[:, kv, pos, mt * 128:(mt + 1) * 128], PE_.t[:, kv * 32 + pos:kv * 32 + pos + 1]) for pos in range(32)],
                 reads=[W1s, PE_], out_ap=pt.t[:, 0:1])
            k.V(lambda: nc.vector.tensor_copy(out=cb1.t[:, kv * 2 + mt:kv * 2 + mt + 1], in_=pt.t[:, 0:1]), reads=[pt], writes=[cb1])
    for kv in range(2):
        for g in range(2):
            for mt in range(2):
                pt = k.psf.next()
                k.mm(pt, [(W1s.t[:, kv, pos, mt * 128:(mt + 1) * 128], XCm.t[:, kv, g, pos:pos + 16 * 254 + 1:16]) for pos in range(32)],
                     reads=[W1s, XCm], out_ap=pt.t[:, 0:255])
                k.A(lambda: nc.scalar.activation(out=HID.t[:, mt, 0:255], in_=pt.t[:, 0:255], func=AF.Silu,
                                                 bias=cb1.t[:, kv * 2 + mt:kv * 2 + mt + 1]), reads=[pt, cb1], writes=[HID])
            for u in range(2):
                nn = 128 if u == 0 else 127
                pt = k.psf.next()
                k.mm(pt, [(HID.t[:, mt, u * 128:u * 128 + nn], W2s.t[:, kv, mt, :]) for mt in range(2)], reads=[HID, W2s],
                     out_ap=pt.t[0:nn, 0:64])
                if kv == 0:
                    k.A(lambda: nc.scalar.activation(out=junk.t[0:nn, :], in_=pt.t[0:nn, 0:64], func=AF.Square,
                                                     accum_out=ss.t[0:nn, 0:1]), reads=[pt], writes=[junk, ss])
                    k.rsqrt(ss, ss, 1.0 / 64, (slice(0, nn), slice(0, 1)))
                    k.V(lambda: nc.vector.memset(kc.t[:], 0.0), writes=[kc])
                    k.V(lambda: nc.vector.tensor_scalar(out=kc.t[0:nn, 0:64], in0=pt.t[0:nn, 0:64], scalar1=ss.t[0:nn, 0:1],
                                                        scalar2=None, op0=ALU.mult), reads=[pt, ss], writes=[kc])
                    pb = k.psb.next()
                    k.P(lambda: nc.tensor.transpose(pb.t[:, 0:128], kc.t[:], k.ident.t[:]), reads=[kc, k.ident], writes=[pb])
                    k.V(lambda: nc.vector.tensor_scalar(out=kcT.t[:], in0=pb.t[0:64, 0:128], scalar1=col(k, "g_kc")[0:64, :],
                                                        scalar2=None, op0=ALU.mult), reads=[pb, k.cols], writes=[kcT])
                    k.store(kcT_s[g, :, u * 128:(u + 1) * 128], "", kcT)
                else:
                    k.V(lambda: nc.vector.memset(vct.t[:], 0.0), writes=[vct])
                    k.V(lambda: nc.vector.tensor_copy(out=vct.t[0:nn, :], in_=pt.t[0:nn, 0:64]), reads=[pt], writes=[vct])
                    k.store(vc_s[g, u], "", vct)


def phase_n2(k):
    nc = k.nc
    for nm, shp in (("c_d0", [128, 128]), ("c_mw1", [128, 128]), ("c_mw2", [128, 128])):
        k.din(nm, shp)
    mix2T_s = k.dscr("mix2T_s", [8, 128, S], BF16)
    onsa_s = k.dscr("onsa_s", [S, 512], F32) if k.debug else None
    qn_dbg = k.dscr("qn_dbg", [32, 2, 64, 512], BF16) if k.debug else None
    dr = k.dram
    qT8 = dr["qT_s"].rearrange("j (r d) t -> (j r) d t", d=64)
    KE, KW = [], []
    for g in range(2):
        ke = k.sb(f"KE{g}", [128, S], BF16)
        k.dma(ke.t[0:64, :], dr["kTs_s"][g * 64:(g + 1) * 64, :], writes=[ke], acc=True)
        k.dma(ke.t[64:128, :], dr["c_kside_bf"][0:64, :], reads=[k.dbuf["c_kside_bf"]], writes=[ke], acc=True)
        KE.append(ke)
        kw = k.sb(f"KW{g}", [128, S], BF16)
        k.dma(kw.t[0:64, :], dr["kTw_s"][g * 64:(g + 1) * 64, :], writes=[kw], acc=True)
        k.dma(kw.t[64:128, :], dr["c_kside_bf"][64:128, :], reads=[k.dbuf["c_kside_bf"]], writes=[kw], acc=True)
        KW.append(kw)
    EB = k.sb("EB", [128, 256], BF16)
    k.dma(EB.t[64:128, :], dr["c_eblk_bf"][:, 3840:4096], reads=[k.dbuf["c_eblk_bf"]], writes=[EB])
    RAQ = dr["c_raq_bf"]
    VV = k.sb("VV", [128, 32, 256], BF16)
    k.dma(VV.t[:], dr["v_s"].rearrange("(kt p) c -> p kt c", p=128), writes=[VV])
    VS = k.sb("VS", [128, 32, 2, 65], BF16)
    VW = k.sb("VW", [128, 32, 2, 65], BF16)
    for br, vt in ((0, VS), (1, VW)):
        k.V(lambda: nc.vector.memset(vt.t[:, :, :, 64:65], 1.0), writes=[vt])
        k.V(lambda: nc.vector.tensor_copy(out=vt.t[:, :, :, 0:64],
                                          in_=VV.t[:, :, br * 128:(br + 1) * 128].rearrange("p k (g d) -> p k g d", d=64)),
            reads=[VV], writes=[vt])
    KC = k.sb("KC", [64, 2, 256], BF16)
    k.dma(KC.t[:], dr["kcT_s"].rearrange("g d n -> d g n"), writes=[KC])
    VC = k.sb("VC", [128, 2, 2, 65], BF16)
    k.V(lambda: nc.vector.memset(VC.t[:, :, :, 64:65], 1.0), writes=[VC])
    for g in range(2):
        k.dma(VC.t[:, :, g, 0:64], dr["vc_s"][g].rearrange("u p d -> p u d"), writes=[VC], acc=True)
    OV = k.sb("OV", [128, 2, 64], BF16)
    k.dma(OV.t[:], dr["c_ovl_bf"].rearrange("p (u j) -> p u j", j=64), reads=[k.dbuf["c_ovl_bf"]], writes=[OV])
    LA = k.sb("LA", [3, S], BF16); k.dma(LA.t[:], dr["c_laall_bf"], reads=[k.dbuf["c_laall_bf"]], writes=[LA])
    RAC = k.sb("RAC", [3, 1024], BF16); k.dma(RAC.t[:], dr["c_rac_bf"], reads=[k.dbuf["c_rac_bf"]], writes=[RAC])
    SHM = k.sb("SHM", [16, S], BF16); k.dma(SHM.t[:], dr["c_shm_bf"], reads=[k.dbuf["c_shm_bf"]], writes=[SHM])
    CMASK = k.sb("CMASK", [16, 512], BF16); k.dma(CMASK.t[:], dr["c_cmask_bf"], reads=[k.dbuf["c_cmask_bf"]], writes=[CMASK])
    CAUS = k.sb("CAUS", [128, 512], BF16); k.dma(CAUS.t[:], dr["c_caus_bf"], reads=[k.dbuf["c_caus_bf"]], writes=[CAUS])
    WINM = k.sb("WINM", [128, 512], BF16); k.dma(WINM.t[:], dr["c_winm_bf"], reads=[k.dbuf["c_winm_bf"]], writes=[WINM])
    D0 = k.sb("D0", [128, 128], F32); k.load(D0, dr["c_d0"], "")
    MW1 = k.sb("MW1", [128, 128], F32); k.load(MW1, dr["c_mw1"], "")
    MW2 = k.sb("MW2", [128, 128], F32); k.load(MW2, dr["c_mw2"], "")
    WNO = k.sb("WNO", [128, 4, 1024], BF16)
    wload(k, WNO, "w_nsa_o", "(k p) n -> p k n", p=128)
    ZR = k.sb("ZR", [1, 512], BF16)
    k.V(lambda: nc.vector.memset(ZR.t[:], 0.0), writes=[ZR])
    SELT_r = k.ring("SELT", [128, 128], BF16, 3)
    for t_ in SELT_r.tiles:
        k.V(lambda: nc.vector.memset(t_.t[:], 0.0), writes=[t_])

    ACC = [k.psf.tiles[4], k.psf.tiles[5]]
    GM_r = k.ring("GMn", [128, 8, 512], BF16, 2)
    gn_r = k.ring("gnn", [128, 24], F32, 4)
    oacc_r = k.ring("oacc", [128, 512], F32, 4)
    otmp_r = k.ring("otmp", [128, 256], F32, 2)
    QN_r = k.ring("QN", [128, 512], BF16, 4)
    PC_r = k.ring("PC", [128, 512], BF16, 4)
    PS_r = k.ring("PS", [128, 512], BF16, 4)
    st_r = k.ring("stn", [128, 32], F32, 3)
    stB_r = k.ring("stb", [128, 32], F32, 2)
    imp_r = k.ring("imp", [128, 192], F32, 3)
    onb_r = k.ring("onb", [128, 512], BF16, 1)
    mo_r = k.ring("mon", [128, 512], BF16, 2)

    def evac_branch(po, st, gnt, br, g, oacc, first):
        pov = po.t[:, 0:260].rearrange("p (h e) -> p h e", e=65)
        k.V(lambda: nc.vector.tensor_scalar(out=st.t[:, 0:4], in0=pov[:, :, 64], scalar1=1e-30, scalar2=None, op0=ALU.add),
            reads=[po], writes=[st])
        yield
        k.V(lambda: nc.vector.reciprocal(out=st.t[:, 0:4], in_=st.t[:, 0:4]), reads=[st], writes=[st])
        yield
        k.V(lambda: nc.vector.tensor_tensor(out=st.t[:, 4:8], in0=st.t[:, 0:4], in1=gnt.t[:, br * 8 + g * 4:br * 8 + g * 4 + 4],
                                            op=ALU.mult), reads=[st, gnt], writes=[st])
        yield
        dst = oacc.t[:, g * 256:(g + 1) * 256].rearrange("p (h d) -> p h d", d=64)
        if first:
            k.V(lambda: nc.vector.tensor_tensor(out=dst, in0=pov[:, :, 0:64], in1=bc(st.t[:, 4:8].unsqueeze(2), [128, 4, 64]),
                                                op=ALU.mult), reads=[po, st], writes=[oacc])
        else:
            ot = otmp_r.next()
            k.V(lambda: nc.vector.tensor_tensor(out=ot.t[:].rearrange("p (h d) -> p h d", d=64), in0=pov[:, :, 0:64],
                                                in1=bc(st.t[:, 4:8].unsqueeze(2), [128, 4, 64]), op=ALU.mult),
                reads=[po, st], writes=[ot])
            yield
            k.V(lambda: nc.vector.tensor_tensor(out=oacc.t[:, g * 256:(g + 1) * 256], in0=oacc.t[:, g * 256:(g + 1) * 256],
                                                in1=ot.t[:], op=ALU.add), reads=[ot, oacc], writes=[oacc])
        yield

    QN2_r = k.ring("QN2", [128, 512], BF16, 3)
    QN2s = {}
    psA = [k.psf.tiles[2], k.psf.tiles[3]]
    psB3 = Tl(k.psb.tiles[1].t[:].bitcast(F32), "psB3")
    psB3.b = k.psb.tiles[1].b
    psB = Ring(k.psf.tiles[0:2] + [psB3])
    psbT = Ring([k.psb.tiles[0]])

    def stageA(i, g, QN, oacc, gnt):
        tok0 = i * 128
        k.dma(QN.t[0:64, :].rearrange("p (h t) -> p h t", t=128),
              qT8[4 * g:4 * g + 4, :, tok0:tok0 + 128].rearrange("h d t -> d h t"), writes=[QN], acc=True)
        k.dma(QN.t[124:128, :], RAQ[(i * 2 + g) * 4:(i * 2 + g) * 4 + 4, :], reads=[k.dbuf["c_raq_bf"]], writes=[QN], acc=True)
        yield
        nu = 2 if i >= 16 else 1
        PCs, nns = [], []
        for u in range(nu):
            dc = i - 16 * u
            nn = min(128 if u == 0 else 127, 8 * dc + 7)
            nns.append(nn)
            pt = psA[u]
            k.mm(pt, [(KC.t[:, g, u * 128:u * 128 + nn], QN.t[0:64, :]),
                      (LA.t[:, dc * 128:dc * 128 + nn], RAC.t[:, g * 512:(g + 1) * 512]),
                      (SHM.t[:, dc * 128:dc * 128 + nn], CMASK.t[:, :])],
                 reads=[KC, QN, LA, RAC, SHM, CMASK], out_ap=pt.t[0:nn, :])
            yield
            PC = PC_r.next()
            k.A(lambda: nc.scalar.activation(out=PC.t[0:nn, :], in_=pt.t[0:nn, :], func=AF.Exp, scale=0.125),
                reads=[pt], writes=[PC])
            yield
            PCs.append(PC)
        for _ in range(4):
            yield
        po, pi = psA[0], psA[1]

        def pvc():
            for hl in range(4):
                for u in range(nu):
                    nc.tensor.matmul(po.t[:, hl * 65:(hl + 1) * 65], PCs[u].t[0:nns[u], hl * 128:(hl + 1) * 128],
                                     VC.t[0:nns[u], u, g, :], start=(u == 0), stop=(u == nu - 1))
                for u in range(nu):
                    ins = nc.tensor.matmul(pi.t[:, hl * 64:(hl + 1) * 64], PCs[u].t[0:nns[u], hl * 128:(hl + 1) * 128],
                                           OV.t[0:nns[u], u, :], start=(u == 0), stop=(u == nu - 1))
            return ins
        k.P(pvc, reads=PCs + [VC, OV], writes=[po, pi])
        yield
        st = st_r.next()
        for _ in evac_branch(po, st, gnt, 0, g, oacc, True):
            yield
        imp = imp_r.next()
        k.V(lambda: nc.vector.tensor_scalar(out=imp.t[:, 0:64], in0=pi.t[:, 0:64], scalar1=st.t[:, 0:1], scalar2=None,
                                            op0=ALU.mult), reads=[pi, st], writes=[imp])
        yield
        for hl in range(1, 4):
            k.V(lambda: nc.vector.scalar_tensor_tensor(out=imp.t[:, 0:64], in0=pi.t[:, hl * 64:(hl + 1) * 64],
                                                       scalar=st.t[:, hl:hl + 1], in1=imp.t[:, 0:64],
                                                       op0=ALU.mult, op1=ALU.add), reads=[pi, st, imp], writes=[imp])
            yield
        c0 = 64 - 2 * i
        k.V(lambda: nc.vector.tensor_tensor(out=imp.t[:, 64:128], in0=imp.t[:, 0:64], in1=MW1.t[:, c0:c0 + 64], op=ALU.mult),
            reads=[imp, MW1], writes=[imp])
        yield
        k.V(lambda: nc.vector.tensor_tensor(out=imp.t[:, 64:128], in0=imp.t[:, 64:128], in1=MW2.t[:, c0:c0 + 64], op=ALU.add),
            reads=[imp, MW2], writes=[imp])
        yield
        k.V(lambda: nc.vector.memset(imp.t[:, 64:65], 1e9), reads=[imp], writes=[imp])
        yield
        k.V(lambda: nc.vector.max(out=st.t[:, 16:24], in_=imp.t[:, 64:128]), reads=[imp], writes=[st])
        yield
        k.V(lambda: nc.vector.match_replace(out=imp.t[:, 128:192], in_to_replace=st.t[:, 16:24], in_values=imp.t[:, 64:128],
                                            imm_value=-3.0e38), reads=[imp, st], writes=[imp])
        yield
        k.V(lambda: nc.vector.max(out=st.t[:, 24:32], in_=imp.t[:, 128:192]), reads=[imp], writes=[st])
        yield
        k.V(lambda: nc.vector.tensor_reduce(out=st.t[:, 8:9], in_=st.t[:, 24:32], axis=AX.X, op=ALU.min), reads=[st], writes=[st])
        yield
        SELT = SELT_r.next()
        k.V(lambda: nc.vector.tensor_scalar(out=SELT.t[:, 64:128], in0=imp.t[:, 64:128], scalar1=st.t[:, 8:9],
                                            scalar2=8.0 * NEGB, op0=ALU.is_lt, op1=ALU.mult), reads=[imp, st], writes=[SELT])
        yield
        for _ in range(9):
            yield
        pb = psbT.next()
        k.P(lambda: nc.tensor.transpose(pb.t[:, 0:128], SELT.t[:], k.ident.t[:]), reads=[SELT, k.ident], writes=[pb])
        k.V(lambda: nc.vector.tensor_copy(out=QN.t[64:124, :].rearrange("p (h t) -> p h t", t=128),
                                          in_=bc(pb.t[64:124, 0:128].unsqueeze(1), [60, 4, 128])), reads=[pb], writes=[QN])
        if i >= 30:
            QN2 = QN2_r.next()
            k.V(lambda: nc.vector.tensor_copy(out=QN2.t[64:128, :].rearrange("p (h t) -> p h t", t=128),
                                              in_=bc(pb.t[64:128, 0:128].unsqueeze(1), [64, 4, 128])), reads=[pb], writes=[QN2])
            QN2s[(i, g)] = QN2
        yield

    def stageB(i, g, QN, oacc, gnt, filler, late=None):
        k0 = max(0, i - 4)
        steps = [(0, kt) for kt in range(i + 1)] + [(1, kt) for kt in range(k0, i + 1)]
        pend_pv = []
        nstep = 0
        late = late if late is not None else []
        for (br, kt) in steps:
            pt = psB.next()
            if br == 0:
                pairs = [(KE[g].t[:, kt * 128:(kt + 1) * 128], QN.t[:, :])]
                rd_ = [KE[g], QN]
                if kt >= 30:
                    QN2 = QN2s[(i, g)]
                    pairs.append((EB.t[64:128, (kt - 30) * 128:(kt - 29) * 128], QN2.t[64:128, :])); rd_ += [EB, QN2]
            else:
                pairs = [(KW[g].t[:, kt * 128:(kt + 1) * 128], QN.t[:, :])]
                rd_ = [KW[g], QN]
            if kt == i:
                pairs.append((k.ident.t[:], CAUS.t[:])); rd_ += [k.ident, CAUS]
            if br == 1 and kt == i - 4:
                pairs.append((k.ident.t[:], WINM.t[:])); rd_ += [k.ident, WINM]
            k.mm(pt, pairs, reads=rd_)
            PS = PS_r.next()
            k.A(lambda: nc.scalar.activation(out=PS.t[:], in_=pt.t[:], func=AF.Exp, scale=0.125), reads=[pt], writes=[PS])
            if len(pend_pv) >= 2:
                pend_pv.pop(0)()

            def pv(br=br, kt=kt, PS=PS):
                first = (kt == 0) if br == 0 else (kt == k0)
                Vt = VS if br == 0 else VW

                def f():
                    if first:
                        nc.tensor.matmul(ACC[br].t[:, 0:260], ZR.t[0:1, 0:128], ZR.t[0:1, 0:260], start=True, stop=False)
                    for hl in range(4):
                        ins = nc.tensor.matmul(ACC[br].t[:, hl * 65:(hl + 1) * 65], PS.t[:, hl * 128:(hl + 1) * 128],
                                               Vt.t[:, kt, g, :], start=False, stop=(kt == i))
                    return ins
                k.P(f, reads=[PS, Vt, ZR], writes=[ACC[br]])
            pend_pv.append(pv)
            nstep += 1
            if late and nstep == 4:
                late.pop(0)()
            if filler is not None:
                for _ in range(3):
                    filler()
        while pend_pv:
            pend_pv.pop(0)()
        while late:
            late.pop(0)()
        for br in range(2):
            stx = stB_r.next()
            for _ in evac_branch(ACC[br], stx, gnt, 1 + br, g, oacc, False):
                pass

    order = []
    lo_, hi_ = 0, NT - 1
    while lo_ <= hi_:
        order.append(hi_)
        if lo_ != hi_:
            order.append(lo_)
        hi_ -= 1
        lo_ += 1
    pairs_ig = [(i, g) for i in order for g in range(2)]
    ctx = {}

    def get_ctx(i):
        if i not in ctx:
            gnt = gn_r.next(); k.load(gnt, dr["gn_s"][i * 128:(i + 1) * 128, :], "")
            ctx[i] = (oacc_r.next(), gnt)
        return ctx[i]

    QNs = {}
    gens = {}

    def make_A(p):
        if p in gens or p >= len(pairs_ig):
            return
        i, g = pairs_ig[p]
        oacc, gnt = get_ctx(i)
        QNs[p] = QN_r.next()
        gens[p] = stageA(i, g, QNs[p], oacc, gnt)

    def chain(ps):
        def step():
            for p_ in ps:
                g_ = gens.get(p_)
                if g_ is None:
                    continue
                try:
                    next(g_)
                    return
                except StopIteration:
                    continue
        return step

    onTs = [k.sb(f"onT{tb}", [128, 4, 512], BF16) for tb in range(8)]
    done_cnt = [0] * 8
    make_A(0)
    for _ in gens[0]:
        pass
    late_fns = []
    for p, (i, g) in enumerate(pairs_ig):
        tb, tt = i // 4, i % 4
        t0 = tb * 512
        oacc, gnt = get_ctx(i)
        make_A(p + 1)
        make_A(p + 2)
        issue_late_conv(k, 1)
        stageB(i, g, QNs[p], oacc, gnt, chain([p + 1, p + 2]), late_fns)
        if p + 1 in gens:
            for _ in gens[p + 1]:
                pass
        if g == 1:
            def finalize(i=i, tb=tb, tt=tt, t0=t0, oacc=oacc):
                onT = onTs[tb]
                onb = onb_r.next()
                k.A(lambda: nc.scalar.copy(out=onb.t[:], in_=oacc.t[:]), reads=[oacc], writes=[onb])
                pb = transpose_to(k, None, None, onb, 4, ring=psbT)
                k.A(lambda: nc.scalar.copy(out=onT.t[:, :, tt * 128:(tt + 1) * 128],
                                           in_=pb.t[:, 0:512].rearrange("p (k t) -> p k t", t=128)), reads=[pb], writes=[onT])
                done_cnt[tb] += 1
                if done_cnt[tb] == 4:
                    GM = GM_r.next(); k.load(GM, dr["gmT_s"][8:16, :, t0:t0 + 512].rearrange("m p t -> p m t"), "")
                    for e in range(8):
                        pt = psB.next()
                        k.mm(pt, [(WNO.t[:, kk, e * 128:(e + 1) * 128], onT.t[:, kk, :]) for kk in range(4)], reads=[WNO, onT])
                        mo = mo_r.next()
                        k.V(lambda: nc.vector.tensor_tensor(out=mo.t[:], in0=pt.t[:], in1=GM.t[:, e, :], op=ALU.mult),
                            reads=[pt, GM], writes=[mo])
                        k.store(mix2T_s[e, :, t0:t0 + 512], "", mo)
            late_fns.append(finalize)
            del ctx[i]
    while late_fns:
        late_fns.pop(0)()
    issue_late_conv(k, 1000)


def phase_d1(k):
    nc = k.nc
    x = k.dram["x"]
    h_s = k.dscr("h_s", [S, D], F32)
    hnT_s = k.dscr("hnT_s", [8, 128, S], BF16)
    WOUT = k.sb("WOUT", [128, 8, D], BF16)
    wload(k, WOUT, "w_out", "(k p) n -> p k n", p=128)
    MA_r = k.ring("MxA", [128, 8, 512], BF16, 3)
    MB_r = k.ring("MxB", [128, 8, 512], BF16, 4)
    xt_r = k.ring("xtd", [128, D], F32, 2)
    ht_r = k.ring("htd", [128, D], F32, 2)
    hs_r = k.ring("hsd", [128, D], BF16, 3)
    hnT_r = k.ring("hnTd", [128, 8, 512], BF16, 2)
    ss_r = k.ring("ssd", [128, 8], F32, 2)
    junk = k.sb("junkd", [128, D], BF16)

    def load_mix(tb):
        Ms = []
        for j, nm in enumerate(("mix1T_s", "mix2T_s", "mix3T_s")):
            M = (MA_r if j == 0 else MB_r).next()
            k.load(M, k.dram[nm][:, :, tb * 512:(tb + 1) * 512].rearrange("m p t -> p m t"), "")
            Ms.append(M)
        return Ms
    def sum_mix(Ms):
        k.V(lambda: nc.vector.tensor_tensor(out=Ms[0].t[:], in0=Ms[0].t[:], in1=Ms[1].t[:], op=ALU.add), reads=[Ms[0], Ms[1]], writes=[Ms[0]])
        k.V(lambda: nc.vector.tensor_tensor(out=Ms[0].t[:], in0=Ms[0].t[:], in1=Ms[2].t[:], op=ALU.add), reads=[Ms[0], Ms[2]], writes=[Ms[0]])
        return Ms

    Ms_next = sum_mix(load_mix(0))
    Ms_ld = load_mix(1)
    pend = []
    gtt = [0]
    for tb in range(8):
        t0 = tb * 512
        Ms = Ms_next
        mixed = Ms[0]
        hnT = hnT_r.next()
        for tt in range(4):
            tok0 = t0 + tt * 128
            xt = xt_r.next(); k.load(xt, x[tok0:tok0 + 128, :], "")
            ht = ht_r.next()
            for nh in range(2):
                pt = k.psf.next()
                k.mm(pt, [(mixed.t[:, kk, tt * 128:(tt + 1) * 128], WOUT.t[:, kk, nh * 512:(nh + 1) * 512]) for kk in range(8)],
                     reads=[mixed, WOUT])
                k.V(lambda: nc.vector.tensor_tensor(out=ht.t[:, nh * 512:(nh + 1) * 512], in0=pt.t[:], in1=xt.t[:, nh * 512:(nh + 1) * 512],
                                                    op=ALU.add), reads=[pt, xt], writes=[ht])
            gtt[0] += 1
            while pend and pend[0][0] <= gtt[0]:
                pend.pop(0)[1]()
            if tt == 1 and tb + 1 < 8:
                Ms_next = sum_mix(Ms_ld)
            if tt == 2 and tb + 2 < 8:
                Ms_ld = load_mix(tb + 2)
            k.store(h_s[tok0:tok0 + 128, :], "", ht)
            ss = ss_r.next()
            k.A(lambda: nc.scalar.activation(out=junk.t[:], in_=ht.t[:], func=AF.Square, accum_out=ss.t[:, 0:1]),
                reads=[ht], writes=[junk, ss])
            k.rsqrt(ss, ss, 1.0 / D, (slice(None), slice(0, 1)))
            hs = hs_r.next()
            k.V(lambda: nc.vector.tensor_scalar(out=hs.t[:], in0=ht.t[:], scalar1=ss.t[:, 0:1], scalar2=None, op0=ALU.mult),
                reads=[ht, ss], writes=[hs])
            def tailf(hs=hs, hnT=hnT, tt=tt):
                pb = transpose_to(k, None, None, hs, 8)
                k.V(lambda: nc.vector.tensor_tensor(out=hnT.t[:, :, tt * 128:(tt + 1) * 128], in0=pb.t[:].rearrange("p (k t) -> p k t", t=128),
                                                    in1=bc(col(k, "g_ffn", 0, 8).unsqueeze(2), [128, 8, 128]), op=ALU.mult),
                    reads=[pb, k.cols], writes=[hnT])
            pend.append((gtt[0] + 2, tailf))
        pend.append((gtt[0] + 2, lambda hnT=hnT, t0=t0: k.store(hnT_s[:, :, t0:t0 + 512].rearrange("m p t -> p m t"), "", hnT)))
    while pend:
        pend.pop(0)[1]()


def prefetch_wup(k):
    k.WUP = k.sb("WUP", [128, 8, 2 * FFH], BF16)


def issue_wup(k):
    for kk in range(8):
        k.dma(k.WUP.t[:, kk, :], k.dram["w_ffn_up_bf"][kk * 128:(kk + 1) * 128, :], reads=[k.dbuf["w_ffn_up_bf"]],
              writes=[k.WUP], acc=True)


def phase_d2a(k):
    nc = k.nc
    aT_s = k.dscr("aT_s", [22, 128, S], BF16)
    hnT_s = k.dram["hnT_s"]
    WUP = k.WUP
    DGF = k.sb("DGF", [128, 88, 128], BF16)
    k.V(lambda: nc.vector.tensor_tensor(out=DGF.t[:], in0=bc(k.identf.t[:].unsqueeze(1), [128, 88, 128]),
                                        in1=bc(col(k, "cw_ffn", 0, 88).unsqueeze(2), [128, 88, 128]), op=ALU.mult),
        reads=[k.identf, k.cols], writes=[DGF])
    UH = k.sb("UH", [128, 44, 2], BF16)
    UHb = [Buf(f"UH{m}") for m in range(44)]
    k.V(lambda: nc.vector.memset(UH.t[:], 0.0), writes=UHb)
    hnT_r = k.ring("hnTf", [128, 8, 512], BF16, 2)
    u_r = k.ring("usb", [128, 512], BF16, 4)
    ga_r = k.ring("ga", [128, 512], BF16, 3)
    cv_r = k.ring("cv", [128, 512], F32, 3)
    AT_r = k.ring("AT", [128, 22, 512], BF16, 2)
    hnT_next = hnT_r.next(); k.load(hnT_next, hnT_s[:, :, 0:512].rearrange("m p t -> p m t"), "")
    for tb in range(8):
        t0 = tb * 512
        hnT = hnT_next
        if tb + 1 < 8:
            hnT_next = hnT_r.next(); k.load(hnT_next, hnT_s[:, :, t0 + 512:t0 + 1024].rearrange("m p t -> p m t"), "")
        AT = AT_r.next()
        order = [m for c in range(22) for m in (c, c + 22)]
        gas = {}
        pending = None
        for m in order:
            pt = k.psf.next()
            k.mm(pt, [(WUP.t[:, kk, m * 128:(m + 1) * 128], hnT.t[:, kk, :]) for kk in range(8)], reads=[WUP, hnT])
            u = u_r.next()
            if m < 22:
                k.A(lambda: nc.scalar.copy(out=u.t[:], in_=pt.t[:]), reads=[pt], writes=[u])
            else:
                k.V(lambda: nc.vector.tensor_copy(out=u.t[:], in_=pt.t[:]), reads=[pt], writes=[u])
            if pending is not None:
                pending()

            def conv(m=m, u=u):
                p2 = k.psf.next()

                def f():
                    d0, d1 = (DGF.t[:, kk * 44 + m, :] for kk in range(2))
                    nc.tensor.matmul(p2.t[:, 1:512], d1, u.t[:, 0:511], start=True, stop=False)
                    nc.tensor.matmul(p2.t[:, 0:1], d1, UH.t[:, m, 1:2], start=False, stop=False)
                    nc.tensor.matmul(p2.t[:, 2:512], d0, u.t[:, 0:510], start=False, stop=False)
                    return nc.tensor.matmul(p2.t[:, 0:2], d0, UH.t[:, m, 0:2], start=False, stop=True)
                k.P(f, reads=[DGF, u, UHb[m]], writes=[p2])
                k.G(lambda: nc.gpsimd.tensor_copy(out=UH.t[:, m, :], in_=u.t[:, 510:512]), reads=[u], writes=[UHb[m]])
                cv = cv_r.next()
                k.V(lambda: nc.vector.scalar_tensor_tensor(out=cv.t[:], in0=u.t[:], scalar=col(k, "cw_ffn", 2 * 44 + m),
                                                           in1=p2.t[:], op0=ALU.mult, op1=ALU.add),
                    reads=[u, k.cols, p2], writes=[cv])
                if m < 22:
                    ga = ga_r.next()
                    k.A(lambda: nc.scalar.activation(out=ga.t[:], in_=cv.t[:], func=AF.Silu, bias=col(k, "cb_ffn", m)),
                        reads=[cv, k.cols], writes=[ga])
                    gas[m] = ga
                else:
                    ga = gas.pop(m - 22)
                    k.V(lambda: nc.vector.scalar_tensor_tensor(out=AT.t[:, m - 22, :], in0=cv.t[:], scalar=col(k, "cb_ffn", m),
                                                               in1=ga.t[:], op0=ALU.add, op1=ALU.mult),
                        reads=[cv, k.cols, ga], writes=[AT])
            pending = conv
        pending()
        k.store(aT_s[:, :, t0:t0 + 512].rearrange("m p t -> p m t"), "", AT)


def phase_d2b(k):
    nc = k.nc
    out = k.dscr("out", [S, D], F32, out=True)
    aT_s, h_s = k.dram["aT_s"], k.dram["h_s"]
    WDN = k.sb("WDN", [128, 22, D], BF16)
    WDNb = [Buf("WDN0"), Buf("WDN1")]
    AT_r = k.ring("ATb", [128, 22, 512], BF16, 2)
    ht_r = k.ring("htb", [128, D], F32, 3)
    for nh in range(2):
        k.dma(WDN.t[:, :, nh * 512:(nh + 1) * 512],
              k.dram["w_ffn_down_bf"].rearrange("(c p) n -> p c n", p=128)[:, :, nh * 512:(nh + 1) * 512],
              reads=[k.dbuf["w_ffn_down_bf"]], writes=[WDNb[nh]])
        if nh == 0:
            AT_next = AT_r.next(); k.load(AT_next, aT_s[:, :, 0:512].rearrange("m p t -> p m t"), "")
    for tb in range(8):
        t0 = tb * 512
        AT = AT_next
        if tb + 1 < 8:
            AT_next = AT_r.next(); k.load(AT_next, aT_s[:, :, t0 + 512:t0 + 1024].rearrange("m p t -> p m t"), "")
        for tt in range(4):
            tok0 = t0 + tt * 128
            ht = ht_r.next(); k.load(ht, h_s[tok0:tok0 + 128, :], "")
            for nh in range(2):
                pt = k.psf.next()
                k.mm(pt, [(AT.t[:, c, tt * 128:(tt + 1) * 128], WDN.t[:, c, nh * 512:(nh + 1) * 512]) for c in range(22)],
                     reads=[AT, WDNb[nh]])
                k.V(lambda: nc.vector.tensor_tensor(out=ht.t[:, nh * 512:(nh + 1) * 512], in0=pt.t[:], in1=ht.t[:, nh * 512:(nh + 1) * 512],
                                                    op=ALU.add), reads=[pt, ht], writes=[ht])
            k.store(out[tok0:tok0 + 128, :], "", ht)


def build_program(inputs, debug=False, phases="abcde", batch=0):
    p = {n: np.asarray(v) for n, v in inputs.items()}
    cols, rows = _mk_tables(p)
    consts = _consts()
    k = K(debug=debug)
    k.ncols, k.nrows = cols.shape[1], rows.shape[1]
    phase_setup(k)
    base = len(k._guards)
    if "a" in phases:
        phase_a(k)
        k.f.barrier()
        k.free_phase(base)
    k.store_on_pq = True
    if "b" in phases:
        phase_b(k)
        k.f.barrier()
        k.free_phase(base)
    if "n" in phases:
        phase_n1(k)
        k.f.barrier()
        k.free_phase(base)
        phase_n2(k)
        k.f.barrier()
        k.free_phase(base)
    base2 = base
    if "d" in phases:
        prefetch_wup(k)
        base2 = base + 1
    if "m" in phases:
        phase_m(k)
        k.f.barrier()
        k.free_phase(base2)
    if "d" in phases:
        if "m" not in phases:
            issue_wup(k)
        phase_d1(k)
        k.f.barrier()
        k.free_phase(base2)
        phase_d2a(k)
        k.f.barrier()
        k.free_phase(base)
        phase_d2b(k)
        k.f.barrier()
        k.free_phase(base)
        k.store_on_pq = False
    k.f.barrier()
    shared = {"cols": cols, "rows": rows, "w_in": np.ascontiguousarray(p["w_in"][0]),
              "w_ssd_o": np.ascontiguousarray(p["w_ssd_o"][0]),
              "w_nsa_o": np.ascontiguousarray(p["w_nsa_o"][0]),
              "w_out": np.ascontiguousarray(p["w_out"][0]), "w_ffn_up": np.ascontiguousarray(p["w_ffn_up"][0]),
              "w_ffn_down": np.ascontiguousarray(p["w_ffn_down"][0]),
              "w_mem_kv": np.ascontiguousarray(p["w_mem_kv"][0]), "w_mem_o": np.ascontiguousarray(p["w_mem_o"][0])}
    shared["cmp_w1"] = np.ascontiguousarray(p["nsa_cmp_w1"][0]).reshape(4096, 256)
    shared["cmp_w2"] = np.ascontiguousarray(p["nsa_cmp_w2"][0]).reshape(512, 64)
    shared["c_peT"] = np.ascontiguousarray(p["nsa_cmp_pe"][0].transpose(2, 0, 1).reshape(64, 64))
    shared.update(consts)
    k.shared = shared
    in_map = dict(shared)
    in_map["x"] = np.ascontiguousarray(p["x"][batch])
    in_map["mem"] = np.ascontiguousarray(p["mem"][batch])
    in_map = {n: v for n, v in in_map.items() if n in k.dram}
    return k, in_map


def kernel(**inputs):
    k, _ = build_program(inputs, debug=False, phases="abnmd", batch=0)
    x = np.asarray(inputs["x"], np.float32)
    mem = np.asarray(inputs["mem"], np.float32)
    in_maps = []
    for b in range(8):
        m = {n: v for n, v in k.shared.items() if n in k.dram}
        m["x"] = np.ascontiguousarray(x[b])
        m["mem"] = np.ascontiguousarray(mem[b])
        in_maps.append(m)
    res = run_bass_kernel_spmd(k.nc, in_maps, core_ids=list(range(8)))
    return np.stack([np.asarray(r["out"], np.float32) for r in res.results], axis=0)
```
